# Optimizing a Trainium2 kernel written in Bass

```python
import math
import jax, jax.numpy as jnp
from jax import lax
import numpy as np

D_MODEL = 1024
BATCH = 4
SEQ = 4096
DEPTH = 4
DEC_BATCH = 32
DEC_SEQ = 1
PAST_LEN = 8192
PAGE_SIZE = 128

N_HEADS = 16
HEAD_DIM = 64
N_KV_HEADS = 4
GROUP = N_HEADS // N_KV_HEADS
ATT_WIDTH = N_HEADS * HEAD_DIM
KV_WIDTH = N_KV_HEADS * HEAD_DIM
CMP_BLK = 32
CMP_STRIDE = 16
CMP_R = CMP_BLK // CMP_STRIDE
CMP_HID = 2 * HEAD_DIM
SEL_BLK = 64
N_SEL = 16
WINDOW = 512
Q_BLK = 128
FORCE_SCORE = 1.0e4
NSA_IN = ATT_WIDTH + 6 * KV_WIDTH + 3 * N_HEADS + ATT_WIDTH
D_INNER = 2 * D_MODEL
M_HEADS = 4
M_HEAD_DIM = D_INNER // M_HEADS
CONV_W = 4
QKV_BLK = 4
M_CHUNK = 64
N_NSA_LAYERS = (DEPTH + 1) // 2
N_MLSTM_LAYERS = DEPTH // 2
RMS_EPS = 1e-6

kernel_name = 'nsa_mlstm_hybrid_step'


def rmsnorm(x, w):
    xf = x.astype(jnp.float32)
    y = xf * lax.rsqrt(jnp.mean(xf * xf, axis=-1, keepdims=True) + RMS_EPS)
    return (y * w.astype(jnp.float32)).astype(x.dtype)


def alibi_slopes():
    return jnp.asarray(np.exp2(-8.0 * np.arange(1, N_HEADS + 1) / N_HEADS), dtype=jnp.float32)


def masked_softmax(s, mask):
    s = jnp.where(mask, s, -jnp.inf)
    m = jnp.max(s, axis=-1, keepdims=True)
    p = jnp.exp(s - jnp.where(jnp.isfinite(m), m, 0.0))
    return p / jnp.maximum(jnp.sum(p, axis=-1, keepdims=True), 1e-30)


def cmp_to_sel(n_cmp, n_blk):
    start = np.arange(n_cmp) * CMP_STRIDE
    blk = np.arange(n_blk) * SEL_BLK
    ov = (start[:, None] < blk[None, :] + SEL_BLK) & (start[:, None] + CMP_BLK > blk[None, :])
    return jnp.asarray(ov, dtype=jnp.float32)


def compress_branch(rows, pe, w1, b1, w2):
    B, Tp = rows.shape[:2]
    n_seg = Tp // CMP_STRIDE
    n_cmp = n_seg - CMP_R + 1
    seg = rows.reshape(B, n_seg, CMP_STRIDE, 2, N_KV_HEADS, HEAD_DIM).astype(jnp.float32)
    w1f = w1.astype(jnp.float32)
    w1r = w1f.reshape(2, CMP_R, CMP_STRIDE, HEAD_DIM, CMP_HID)
    part = jnp.einsum('bnrchd,cjrde->bnjche', seg, w1r)
    hid = (jnp.einsum('cld,clde->ce', pe.astype(jnp.float32), w1f) + b1.astype(jnp.float32))
    hid = hid[None, None, :, None, :]
    for j in range(CMP_R):
        hid = hid + part[:, j:j + n_cmp, j]
    summ = jnp.einsum('bnche,ced->bnchd', jax.nn.silu(hid), w2.astype(jnp.float32))
    kc = summ[:, :, 0].transpose(0, 2, 1, 3)
    vc = summ[:, :, 1].transpose(0, 2, 1, 3)
    c_end = jnp.arange(n_cmp) * CMP_STRIDE + (CMP_BLK - 1)
    return kc, vc, c_end


def sel_blocks(rows):
    B, T = rows.shape[:2]
    return rows.reshape(B, T // SEL_BLK, SEL_BLK, N_KV_HEADS, HEAD_DIM).transpose(0, 3, 1, 2, 4)


def gather_blocks(blocks, ids):
    return jax.vmap(jax.vmap(lambda b_, i_: b_[i_]))(blocks, ids)


def nsa_attend(q, gates, t_pos, kc, vc, c_end, ks, vs, kw, vw, w_pos):
    f32 = jnp.float32
    B, Q = q.shape[0], q.shape[3]
    slopes = alibi_slopes().reshape(1, N_KV_HEADS, GROUP, 1, 1)
    qf = q.astype(f32) * (HEAD_DIM ** -0.5)
    t = t_pos[:, None]
    d_c = (t - c_end[None, :]).astype(f32)
    s_c = jnp.einsum('bkgqd,bknd->bkgqn', qf, kc.astype(f32)) - slopes * d_c
    p_c = masked_softmax(s_c, d_c >= 0)
    o_c = jnp.einsum('bkgqn,bknd->bkgqd', p_c, vc.astype(f32))
    n_cmp, n_blk = kc.shape[2], ks.shape[2]
    imp = jnp.einsum('bkgqn,nj->bkqj', p_c, cmp_to_sel(n_cmp, n_blk))
    blk = jnp.arange(n_blk)[None, :]
    tb = t // SEL_BLK
    forced = (blk == 0) | (blk == tb) | (blk == tb - 1)
    score = jnp.where(forced, FORCE_SCORE, jnp.where(blk * SEL_BLK <= t, imp, -1.0))
    top_s, idx = lax.top_k(score, min(N_SEL, n_blk))
    ks_g = gather_blocks(ks, idx).reshape(B, N_KV_HEADS, Q, -1, HEAD_DIM)
    vs_g = gather_blocks(vs, idx).reshape(B, N_KV_HEADS, Q, -1, HEAD_DIM)
    s_pos = (idx[..., None] * SEL_BLK + jnp.arange(SEL_BLK)).reshape(B, N_KV_HEADS, Q, -1)
    d_s = (t - s_pos).astype(f32)[:, :, None]
    sel_ok = jnp.repeat(top_s > -0.5, SEL_BLK, axis=-1)[:, :, None]
    s_s = jnp.einsum('bkgqd,bkqnd->bkgqn', qf, ks_g.astype(f32)) - slopes * d_s
    p_s = masked_softmax(s_s, sel_ok & (d_s >= 0))
    o_s = jnp.einsum('bkgqn,bkqnd->bkgqd', p_s, vs_g.astype(f32))
    d_w = (t - w_pos[None, :]).astype(f32)
    s_w = jnp.einsum('bkgqd,bkld->bkgql', qf, kw.astype(f32)) - slopes * d_w
    p_w = masked_softmax(s_w, (d_w >= 0) & (d_w <= WINDOW) & (w_pos[None, :] >= 0))
    o_w = jnp.einsum('bkgql,bkld->bkgqd', p_w, vw.astype(f32))
    return gates[..., 0:1] * o_c + gates[..., 1:2] * o_s + gates[..., 2:3] * o_w


def nsa_project(h, w_in):
    B, T, _ = h.shape
    p = h @ w_in
    o1 = ATT_WIDTH
    o2 = o1 + 4 * KV_WIDTH
    o3 = o2 + 2 * KV_WIDTH
    o4 = o3 + 3 * N_HEADS
    q = p[..., :o1].reshape(B, T, N_KV_HEADS, GROUP, HEAD_DIM).transpose(0, 2, 3, 1, 4)
    kv = p[..., o1:o2].reshape(B, T, 4, N_KV_HEADS, HEAD_DIM)
    win = p[..., o2:o3].reshape(B, T, 2, N_KV_HEADS, HEAD_DIM)
    gates = jax.nn.sigmoid(p[..., o3:o4].astype(jnp.float32))
    gates = gates.reshape(B, T, N_KV_HEADS, GROUP, 3).transpose(0, 2, 3, 1, 4)
    z = p[..., o4:]
    return q, kv, win, gates, z


def nsa_output(o, z, w_out):
    B, T = o.shape[0], o.shape[3]
    o = o.transpose(0, 3, 1, 2, 4).reshape(B, T, ATT_WIDTH).astype(z.dtype)
    return (o * jax.nn.silu(z)) @ w_out


def nsa_prompt(h, w_in, w_out, pe, w1, b1, w2):
    B, T, _ = h.shape
    q, kv, win, gates, z = nsa_project(h, w_in)
    kc, vc, c_end = compress_branch(kv[:, :, 0:2], pe, w1, b1, w2)
    ks, vs = sel_blocks(kv[:, :, 2]), sel_blocks(kv[:, :, 3])
    pad = ((0, 0), (0, 0), (WINDOW, 0), (0, 0))
    kw = jnp.pad(win[:, :, 0].transpose(0, 2, 1, 3), pad)
    vw = jnp.pad(win[:, :, 1].transpose(0, 2, 1, 3), pad)

    def q_block(qb):
        s0 = qb * Q_BLK
        t_pos = s0 + jnp.arange(Q_BLK)
        w_pos = s0 - WINDOW + jnp.arange(WINDOW + Q_BLK)
        return nsa_attend(
            lax.dynamic_slice_in_dim(q, s0, Q_BLK, axis=3),
            lax.dynamic_slice_in_dim(gates, s0, Q_BLK, axis=3),
            t_pos, kc, vc, c_end, ks, vs,
            lax.dynamic_slice_in_dim(kw, s0, WINDOW + Q_BLK, axis=2),
            lax.dynamic_slice_in_dim(vw, s0, WINDOW + Q_BLK, axis=2), w_pos)

    o = lax.map(q_block, jnp.arange(T // Q_BLK))
    o = o.transpose(1, 2, 3, 0, 4, 5).reshape(B, N_KV_HEADS, GROUP, T, HEAD_DIM)
    wr = min(WINDOW, T)
    return nsa_output(o, z, w_out), kv, win[:, T - wr:]


def nsa_sample(h, cache_kv_l, cache_win_l, page_table, w_in, w_out, pe, w1, b1, w2):
    B, Tn, _ = h.shape
    past_len = page_table.shape[1] * PAGE_SIZE
    wr = cache_win_l.shape[1]
    q, kv, win, gates, z = nsa_project(h, w_in)
    past = cache_kv_l[page_table].reshape(B, past_len, 4, N_KV_HEADS, HEAD_DIM).astype(kv.dtype)
    T = past_len + Tn
    Tp = -(-T // SEL_BLK) * SEL_BLK
    rows = jnp.pad(jnp.concatenate([past, kv], axis=1), ((0, 0), (0, Tp - T), (0, 0), (0, 0), (0, 0)))
    kc, vc, c_end = compress_branch(rows[:, :, 0:2], pe, w1, b1, w2)
    ks, vs = sel_blocks(rows[:, :, 2]), sel_blocks(rows[:, :, 3])
    win_all = jnp.concatenate([cache_win_l.astype(win.dtype), win], axis=1)
    kw = win_all[:, :, 0].transpose(0, 2, 1, 3)
    vw = win_all[:, :, 1].transpose(0, 2, 1, 3)
    t_pos = past_len + jnp.arange(Tn)
    w_pos = past_len - wr + jnp.arange(wr + Tn)
    o = nsa_attend(q, gates, t_pos, kc, vc, c_end, ks, vs, kw, vw, w_pos)
    return nsa_output(o, z, w_out), kv, win_all[:, Tn:]


def mlstm_cell(q, k, v, ig, fl, C0, n0, m0, chunk):
    B, NH, T, DH = q.shape
    nc = T // chunk

    def to_chunks(a):
        return jnp.moveaxis(a.reshape(B, NH, nc, chunk, *a.shape[3:]), 2, 0)

    causal = jnp.tril(jnp.ones((chunk, chunk), dtype=bool))

    def step(carry, inp):
        C, n, m = carry
        qc, kc, vc, ic, fc = inp
        b = jnp.cumsum(fc, axis=-1)
        D = jnp.where(causal, b[..., :, None] - b[..., None, :] + ic[..., None, :], -jnp.inf)
        inter = b + m[..., None]
        mt = jnp.maximum(jnp.max(D, axis=-1), inter)
        S = jnp.einsum('bhtd,bhsd->bhts', qc, kc) * jnp.exp(D - mt[..., None])
        decay = jnp.exp(inter - mt)
        num = jnp.einsum('bhts,bhse->bhte', S, vc) + decay[..., None] * jnp.einsum('bhtd,bhde->bhte', qc, C)
        den = jnp.sum(S, axis=-1) + decay * jnp.einsum('bhtd,bhd->bht', qc, n)
        hc = num / jnp.maximum(jnp.abs(den), jnp.exp(-mt))[..., None]
        m_new = mt[..., -1]
        w = jnp.exp(b[..., -1:] - b + ic - m_new[..., None])
        carry_decay = jnp.exp(b[..., -1] + m - m_new)
        C = carry_decay[..., None, None] * C + jnp.einsum('bhsd,bhs,bhse->bhde', kc, w, vc)
        n = carry_decay[..., None] * n + jnp.einsum('bhsd,bhs->bhd', kc, w)
        return (C, n, m_new), hc

    carry0 = (C0.astype(jnp.float32), n0.astype(jnp.float32), m0.astype(jnp.float32))
    (C, n, m), hs = lax.scan(step, carry0, (to_chunks(q), to_chunks(k), to_chunks(v), to_chunks(ig), to_chunks(fl)))
    h = jnp.moveaxis(hs, 0, 2).reshape(B, NH, T, DH)
    return h, C, n, m


def mlstm_mixer(h, conv_buf, C0, n0, m0, chunk, w_in, conv_w, conv_b, w_qkv, w_gate, b_gate, norm_w, skip, w_out):
    f32 = jnp.float32
    B, T, _ = h.shape
    proj = h @ w_in
    xm, z = proj[..., :D_INNER], proj[..., D_INNER:]
    xpad = jnp.concatenate([conv_buf.astype(xm.dtype), xm], axis=1)
    conv = conv_b
    for j in range(CONV_W):
        conv = conv + xpad[:, j:j + T] * conv_w[j]
    c = jax.nn.silu(conv)
    nb = D_INNER // QKV_BLK
    cb = c.reshape(B, T, nb, QKV_BLK)
    xb = xm.reshape(B, T, nb, QKV_BLK)
    q = jnp.einsum('btnj,nji->btni', cb, w_qkv[0]).reshape(B, T, D_INNER)
    k = jnp.einsum('btnj,nji->btni', cb, w_qkv[1]).reshape(B, T, D_INNER)
    v = jnp.einsum('btnj,nji->btni', xb, w_qkv[2]).reshape(B, T, D_INNER)
    gpre = (jnp.concatenate([q, k, v], axis=-1) @ w_gate + b_gate).astype(f32)
    ig = gpre[..., :M_HEADS].transpose(0, 2, 1)
    fl = jax.nn.log_sigmoid(gpre[..., M_HEADS:]).transpose(0, 2, 1)

    def heads(a):
        return a.reshape(B, T, M_HEADS, M_HEAD_DIM).transpose(0, 2, 1, 3).astype(f32)

    hc, C, n, m = mlstm_cell(heads(q), heads(k) * (M_HEAD_DIM ** -0.5), heads(v), ig, fl, C0, n0, m0, chunk)
    hn = hc * lax.rsqrt(jnp.mean(hc * hc, axis=-1, keepdims=True) + RMS_EPS)
    hn = hn.transpose(0, 2, 1, 3).reshape(B, T, D_INNER).astype(h.dtype)
    out = (hn * norm_w + skip * c) * jax.nn.silu(z)
    return out @ w_out, C, n, m, xpad[:, T:]


def setup_inputs(seed: int = 0) -> dict:
    key = jax.random.key(seed)
    ks = jax.random.split(key, 32)
    f32 = jnp.float32

    def nrm(k, shape, s):
        return jax.random.normal(k, shape, f32) * s

    n_pages = PAST_LEN // PAGE_SIZE
    n_used = DEC_BATCH * n_pages
    n_pool = n_used + max(n_used // 4, 1)
    wr = min(WINDOW, PAST_LEN)
    NA, NM = N_NSA_LAYERS, N_MLSTM_LAYERS
    page_table = jax.random.permutation(ks[0], n_pool)[:n_used].reshape(DEC_BATCH, n_pages).astype(jnp.int32)
    b_gate = jnp.concatenate([nrm(ks[25], (NM, M_HEADS), 0.1),
                              jnp.linspace(3.0, 6.0, M_HEADS, dtype=f32)[None, :] + nrm(ks[26], (NM, M_HEADS), 0.1)], axis=-1)
    return {
        'x_prompt': nrm(ks[1], (BATCH, SEQ, D_MODEL), 1.0),
        'x_sample': nrm(ks[2], (DEC_BATCH, DEC_SEQ, D_MODEL), 1.0),
        'cache_kv': nrm(ks[3], (NA, n_pool, PAGE_SIZE, 4, N_KV_HEADS, HEAD_DIM), 1.0),
        'cache_win': nrm(ks[4], (NA, DEC_BATCH, wr, 2, N_KV_HEADS, HEAD_DIM), 1.0),
        'state_C': nrm(ks[5], (NM, DEC_BATCH, M_HEADS, M_HEAD_DIM, M_HEAD_DIM), 0.1),
        'state_n': nrm(ks[6], (NM, DEC_BATCH, M_HEADS, M_HEAD_DIM), 0.1),
        'state_m': jax.random.uniform(ks[7], (NM, DEC_BATCH, M_HEADS), f32, 0.0, 3.0),
        'state_conv': nrm(ks[8], (NM, DEC_BATCH, CONV_W - 1, D_INNER), 1.0),
        'page_table': page_table,
        'norm_w': 1.0 + nrm(ks[9], (DEPTH, D_MODEL), 0.02),
        'final_norm_w': 1.0 + nrm(ks[10], (D_MODEL,), 0.02),
        'nsa_w_in': nrm(ks[11], (NA, D_MODEL, NSA_IN), D_MODEL ** -0.5),
        'nsa_w_out': nrm(ks[12], (NA, ATT_WIDTH, D_MODEL), ATT_WIDTH ** -0.5),
        'nsa_cmp_pe': nrm(ks[13], (NA, 2, CMP_BLK, HEAD_DIM), 0.1),
        'nsa_cmp_w1': nrm(ks[14], (NA, 2, CMP_BLK, HEAD_DIM, CMP_HID), (CMP_BLK * HEAD_DIM) ** -0.5),
        'nsa_cmp_b1': nrm(ks[15], (NA, 2, CMP_HID), 0.02),
        'nsa_cmp_w2': nrm(ks[16], (NA, 2, CMP_HID, HEAD_DIM), CMP_HID ** -0.5),
        'm_w_in': nrm(ks[17], (NM, D_MODEL, 2 * D_INNER), D_MODEL ** -0.5),
        'm_conv_w': nrm(ks[18], (NM, CONV_W, D_INNER), CONV_W ** -0.5),
        'm_conv_b': nrm(ks[19], (NM, D_INNER), 0.02),
        'm_w_qkv': nrm(ks[20], (NM, 3, D_INNER // QKV_BLK, QKV_BLK, QKV_BLK), QKV_BLK ** -0.5),
        'm_w_gate': nrm(ks[21], (NM, 3 * D_INNER, 2 * M_HEADS), (3 * D_INNER) ** -0.5),
        'm_b_gate': b_gate,
        'm_norm_w': 1.0 + nrm(ks[22], (NM, D_INNER), 0.02),
        'm_skip': 1.0 + nrm(ks[23], (NM, D_INNER), 0.1),
        'm_w_out': nrm(ks[24], (NM, D_INNER, D_MODEL), D_INNER ** -0.5),
    }


def reference(x_prompt, x_sample, cache_kv, cache_win, state_C, state_n, state_m, state_conv, page_table,
              norm_w, final_norm_w, nsa_w_in, nsa_w_out, nsa_cmp_pe, nsa_cmp_w1, nsa_cmp_b1, nsa_cmp_w2,
              m_w_in, m_conv_w, m_conv_b, m_w_qkv, m_w_gate, m_b_gate, m_norm_w, m_skip, m_w_out):
    f32 = jnp.float32
    yp, ys = x_prompt, x_sample
    kv_p, kv_s, win_p, win_s = [], [], [], []
    C_p, C_s, n_p, n_s, m_p, m_s, cv_p, cv_s = [], [], [], [], [], [], [], []
    for i in range(DEPTH):
        hp = rmsnorm(yp, norm_w[i])
        hs = rmsnorm(ys, norm_w[i])
        l = i // 2
        if i % 2 == 0:
            prm = (nsa_w_in[l], nsa_w_out[l], nsa_cmp_pe[l], nsa_cmp_w1[l], nsa_cmp_b1[l], nsa_cmp_w2[l])
            dp, kvp, wp = nsa_prompt(hp, *prm)
            ds, kvs, wsm = nsa_sample(hs, cache_kv[l], cache_win[l], page_table, *prm)
            kv_p.append(kvp); kv_s.append(kvs); win_p.append(wp); win_s.append(wsm)
        else:
            prm = (m_w_in[l], m_conv_w[l], m_conv_b[l], m_w_qkv[l], m_w_gate[l], m_b_gate[l],
                   m_norm_w[l], m_skip[l], m_w_out[l])
            Bp, Tp = hp.shape[:2]
            dp, Cp, np_, mp, cp = mlstm_mixer(
                hp, jnp.zeros((Bp, CONV_W - 1, D_INNER), hp.dtype),
                jnp.zeros((Bp, M_HEADS, M_HEAD_DIM, M_HEAD_DIM), f32), jnp.zeros((Bp, M_HEADS, M_HEAD_DIM), f32),
                jnp.full((Bp, M_HEADS), -jnp.inf, f32), min(M_CHUNK, Tp), *prm)
            ds, Cs, ns_, ms, cs = mlstm_mixer(hs, state_conv[l], state_C[l], state_n[l], state_m[l], hs.shape[1], *prm)
            C_p.append(Cp); C_s.append(Cs); n_p.append(np_); n_s.append(ns_)
            m_p.append(mp); m_s.append(ms); cv_p.append(cp); cv_s.append(cs)
        yp = yp + dp
        ys = ys + ds
    y_prompt = rmsnorm(yp, final_norm_w)
    y_sample = rmsnorm(ys, final_norm_w)
    return (y_prompt, y_sample, jnp.stack(kv_p), jnp.stack(kv_s), jnp.stack(win_p), jnp.stack(win_s),
            jnp.stack(C_p), jnp.stack(C_s), jnp.stack(n_p), jnp.stack(n_s), jnp.stack(m_p), jnp.stack(m_s),
            jnp.stack(cv_p), jnp.stack(cv_s))
```

```python
import numpy as np
import ml_dtypes
from contextlib import ExitStack
import concourse.bass as bass
import concourse.mybir as mybir
from concourse.bass_utils import run_bass_kernel_spmd

F32 = mybir.dt.float32
BF16 = mybir.dt.bfloat16
I32 = mybir.dt.int32
AF = mybir.ActivationFunctionType
ALU = mybir.AluOpType
AX = mybir.AxisListType
NPBF = ml_dtypes.bfloat16

ENGS = ("tensor", "vector", "scalar", "gpsimd", "sync")
BIG = 30000.0
MARGIN = 30.0

D_MODEL = 1024
N_HEADS = 16
HD = 64
NKV = 4
GRP = 4
NSA_IN = 3632
D_INNER = 2048
M_HEADS = 4
MHD = 512
RMS_EPS = 1e-6


class Sched:
    def __init__(self, nc, stack, n_dma_sems=6, same_engine_sync=True):
        self.nc = nc
        self.same = same_engine_sync
        self.semobj = {}
        self.ecount = {e: 0 for e in ENGS}
        for e in ENGS:
            self.semobj["s_" + e] = stack.enter_context(nc.semaphore("s_" + e))
        self.dq = {}
        self.dcount = {}
        for q in ("sync", "gpsimd", "scalar"):
            self.dq[q] = []
            for i in range(n_dma_sems):
                sn = "d_%s%d" % (q, i)
                self.semobj[sn] = stack.enter_context(nc.semaphore(sn))
                self.dq[q].append(sn)
                self.dcount[sn] = 0
        self.dnext = {q: 0 for q in self.dq}
        self.waited = {e: {} for e in ENGS}
        self.lastw = {}
        self.readers = {}
        self.pending = {e: [] for e in ENGS}
        self.nops = 0

    def _deps(self, reads, writes, eng=None):
        deps = []
        for r in reads:
            t = self.lastw.get(r)
            if t is not None:
                deps.append(t)
            if r.startswith("ps"):
                for t2 in self.readers.get(r, ()):
                    if t2[2] != eng:
                        deps.append(t2)
        for w in writes:
            t = self.lastw.get(w)
            if t is not None:
                deps.append(t)
            deps.extend(self.readers.get(w, ()))
        return deps

    def _filter(self, eng, deps):
        best = {}
        for (sn, val, src) in deps:
            if src == eng and not sn.startswith("d_"):
                if eng == "tensor" or not self.same:
                    continue
            if self.waited[eng].get(sn, 0) >= val:
                continue
            if best.get(sn, 0) < val:
                best[sn] = val
        for sn, val in best.items():
            self.waited[eng][sn] = val
        return list(best.items())

    def _commit(self, tok, reads, writes):
        for r in reads:
            self.readers.setdefault(r, []).append(tok)
        for w in writes:
            self.lastw[w] = tok
            self.readers[w] = []

    def op(self, eng, m, a, k, reads, writes):
        waits = self._filter(eng, self._deps(reads, writes, eng))
        self.ecount[eng] += 1
        tok = ("s_" + eng, self.ecount[eng], eng)
        self.pending[eng].append((waits, m, a, k, "s_" + eng, 1))
        self._commit(tok, reads, writes)
        self.nops += 1

    def v(self, m, *a, r=(), w=(), **k):
        self.op("vector", m, a, k, r, w)

    def a(self, m, *a, r=(), w=(), **k):
        self.op("scalar", m, a, k, r, w)

    def p(self, m, *a, r=(), w=(), **k):
        self.op("tensor", m, a, k, r, w)

    def g(self, m, *a, r=(), w=(), **k):
        self.op("gpsimd", m, a, k, r, w)

    def d(self, q, out, in_, r=(), w=(), m="dma_start", **k):
        import os as _os
        if _os.environ.get("NOGPQ") == "1" and q == "gpsimd":
            q = "sync"
        i = self.dnext[q]
        self.dnext[q] += 1
        sn = self.dq[q][i % len(self.dq[q])]
        deps = self._deps(r, w)
        prev = self.dcount[sn]
        if prev > 0:
            deps.append((sn, 16 * prev, "dma"))
        waits = self._filter(q, deps)
        self.dcount[sn] = prev + 1
        tok = (sn, 16 * (prev + 1), "dma")
        kk = dict(k)
        if m == "dma_start":
            kk["out"] = out
            kk["in_"] = in_
            args = ()
        else:
            args = (out, in_)
        self.pending[q].append((waits, m, args, kk, sn, 16))
        self._commit(tok, r, w)
        self.nops += 1

    def barrier(self):
        for e in ENGS:
            waits = []
            for e2 in ENGS:
                sn = "s_" + e2
                if e2 != e and self.ecount[e2] > self.waited[e].get(sn, 0):
                    waits.append((sn, self.ecount[e2]))
                    self.waited[e][sn] = self.ecount[e2]
            for sn, c in self.dcount.items():
                if 16 * c > self.waited[e].get(sn, 0):
                    waits.append((sn, 16 * c))
                    self.waited[e][sn] = 16 * c
            if waits:
                self.pending[e].append((waits, None, None, None, None, 0))

    def dm(self, q, m, args, kwargs, r=(), w=()):
        i = self.dnext[q]
        self.dnext[q] += 1
        sn = self.dq[q][i % len(self.dq[q])]
        deps = self._deps(r, w)
        prev = self.dcount[sn]
        if prev > 0:
            deps.append((sn, 16 * prev, "dma"))
        waits = self._filter(q, deps)
        self.dcount[sn] = prev + 1
        tok = (sn, 16 * (prev + 1), "dma")
        self.pending[q].append((waits, m, tuple(args), dict(kwargs), sn, 16))
        self._commit(tok, r, w)
        self.nops += 1

    def flush(self, final=False, barrier=True):
        if barrier and not final:
            self.barrier()
        if final:
            for sn, c in self.dcount.items():
                if c > 0 and self.waited["sync"].get(sn, 0) < 16 * c:
                    self.pending["sync"].append(([(sn, 16 * c)], None, None, None, None, 0))
            for e in ENGS:
                if e != "sync" and self.ecount[e] > 0:
                    self.pending["sync"].append(([("s_" + e, self.ecount[e])], None, None, None, None, 0))
        pend = self.pending
        self.pending = {e: [] for e in ENGS}
        semobj = self.semobj
        import os as _os
        if _os.environ.get("DUMP") == "1":
            for e in ENGS:
                print("==== engine", e)
                for waits, m, a, k, sn, inc in pend[e]:
                    print("   ", waits, m, sn, inc)

        def run(engine, lst):
            for waits, m, a, k, sn, inc in lst:
                import os as _os
                if _os.environ.get("WORDER") == "1":
                    waits = sorted(waits, key=lambda t: (t[0] == "s_tensor", t[0]))
                for wn, val in waits:
                    engine.wait_ge(semobj[wn], val)
                if m is not None:
                    ins = getattr(engine, m)(*a, **k)
                    ins.then_inc(semobj[sn], inc)

        with self.nc.Block() as block:
            if pend["tensor"]:
                @block.tensor
                def _(e):
                    run(e, pend["tensor"])
            if pend["vector"]:
                @block.vector
                def _(e):
                    run(e, pend["vector"])
            if pend["scalar"]:
                @block.scalar
                def _(e):
                    run(e, pend["scalar"])
            if pend["gpsimd"]:
                @block.gpsimd
                def _(e):
                    run(e, pend["gpsimd"])
            if pend["sync"]:
                @block.sync
                def _(e):
                    run(e, pend["sync"])


def bcast(ap, axis, n):
    sh = list(ap.shape)
    a = ap.unsqueeze(axis)
    sh.insert(axis, n)
    return a.broadcast_to(sh)


def bf16_split3(x):
    x = np.asarray(x, np.float32)
    a = x.astype(NPBF).astype(np.float32)
    b = (x - a).astype(NPBF).astype(np.float32)
    c = (x - a - b).astype(NPBF).astype(np.float32)
    return a, b, c


def make_consts(T, TS, P=0):
    c = {}
    if P:
        NE = P // 64
        keyp = np.arange(P)
        c["EallS"] = (keyp[None, :] // 64 == np.arange(NE)[:, None]).astype(np.float32).astype(NPBF)
        NCs = P // 16
        NJ = max(1, NCs // 128)
        NCp = NJ * 128
        NBs = P // 64 + 1
        NBp = ((NBs + 3) // 4) * 4
        nn = np.arange(NCp)
        jj = np.arange(NBp)
        ovs = ((16 * nn[:, None] < 64 * jj[None, :] + 64) & (16 * nn[:, None] + 32 > 64 * jj[None, :]) & (jj[None, :] < NBs))
        c["MovS"] = np.ascontiguousarray(ovs.astype(np.float32).reshape(NJ, 128, NBp).transpose(1, 0, 2)).astype(NPBF)
        own = np.full((4, 4, 4), -BIG, np.float32)
        for s_ in range(4):
            own[s_, s_, :] = 0.0
        c["ownm"] = own.astype(NPBF)
        c["maskcs"] = np.ascontiguousarray(np.broadcast_to(np.where(16 * nn + 31 <= P, 0.0, -BIG).astype(np.float32)[None], (4, NCp)))
        c["pcol2"] = np.stack([2 * np.arange(128), 2 * np.arange(128) + 1], 1).astype(np.float32)
        posl = np.arange(P)
        A4 = (64 * (posl // 64)).astype(np.float32)
        B4 = (posl % 64).astype(np.float32)
        p6 = np.stack([A4, A4, A4, B4, B4, B4])
        c["pos4"] = np.ascontiguousarray(np.broadcast_to(p6[:, None, :], (6, 4, P))).astype(NPBF)
        Ap = float(64 * (P // 64))
        Bp = float(P % 64)
        c["posP"] = np.ascontiguousarray(np.broadcast_to(np.array([Ap, Ap, Ap, Bp, Bp, Bp], np.float32)[:, None], (6, 4))).astype(NPBF)
        slp = np.exp2(-8.0 * np.arange(1, N_HEADS + 1) / N_HEADS).astype(np.float32)
        a1, a2, a3 = bf16_split3(slp)
        c["slope4"] = np.ascontiguousarray(np.stack([a1, a2, a3, a1, a2, a3]).reshape(6, 4, 4).transpose(1, 0, 2)).astype(NPBF)
    c["ident"] = np.eye(128, dtype=np.float32)
    c["identb"] = np.eye(128, dtype=np.float32).astype(NPBF)
    npos = max(T, TS)
    pos = np.arange(npos)
    A = (64 * (pos // 64)).astype(np.float32)
    B = (pos % 64).astype(np.float32)
    c["pos"] = np.stack([A, A, A, B, B, B]).astype(NPBF)
    ce = 16 * np.arange(512) + 31
    Ac = (64 * (ce // 64)).astype(np.float32)
    Bc = (ce % 64).astype(np.float32)
    c["cend"] = np.stack([Ac, Ac, Ac, Bc, Bc, Bc]).astype(NPBF)
    slopes = np.exp2(-8.0 * np.arange(1, N_HEADS + 1) / N_HEADS).astype(np.float32)
    s1, s2, s3 = bf16_split3(slopes)
    sl = np.stack([s1, s2, s3, s1, s2, s3])
    c["slope"] = np.ascontiguousarray(
        np.broadcast_to(sl.reshape(6, 4, 4, 1), (6, 4, 4, 128)).transpose(1, 0, 2, 3)).astype(NPBF)
    kk = np.arange(128)[:, None]
    qq = np.arange(128)[None, :]
    causal = np.where(kk > qq, -BIG, 0.0).astype(np.float32)
    far = np.where(kk < qq, -BIG, 0.0).astype(np.float32)
    c["causal"] = np.ascontiguousarray(np.broadcast_to(causal[:, None, :], (128, 4, 128))).astype(NPBF)
    c["far"] = np.ascontiguousarray(np.broadcast_to(far[:, None, :], (128, 4, 128))).astype(NPBF)
    m = np.arange(512)[None, :] - 256
    ql = np.arange(128)[:, None]
    c["maskc"] = np.where(16 * m + 31 <= ql, 0.0, -BIG).astype(np.float32)
    jr = np.arange(128)[None, :] - 64
    tbl = ql // 64
    allow = (jr <= tbl).astype(np.float32)
    c["allow"] = allow
    c["allowm1"] = allow - 1.0
    c["force"] = np.where((jr == tbl) | (jr == tbl - 1), 1.0e4, -1.0e30).astype(np.float32)
    pp = np.arange(128)
    c["maskbd"] = (pp[:, None] // 4 == pp[None, :] // 4).astype(np.float32)
    selh = np.zeros((4, 4, 128), np.float32)
    for h in range(4):
        selh[h, h, :] = 1.0
    c["selh"] = selh
    c["trib"] = np.where(kk > qq, -BIG, 0.0).astype(np.float32)
    c["eye4"] = np.ascontiguousarray(np.broadcast_to(np.eye(4, dtype=np.float32)[:, :, None], (4, 4, 4)))
    NB = T // 64
    key = np.arange(T)
    c["Eall"] = (key[None, :] // 64 == np.arange(NB)[:, None]).astype(np.float32).astype(NPBF)
    n = np.arange(256)
    j = np.arange(NB)
    ov = ((16 * n[:, None] < 64 * j[None, :] + 64) & (16 * n[:, None] + 32 > 64 * j[None, :])).astype(np.float32)
    c["Mov"] = np.ascontiguousarray(ov.reshape(2, 128, NB).transpose(1, 0, 2)).astype(NPBF)
    return c


class Builder:
    def __init__(self, T, TS=128, P=0):
        self.T = T
        self.TS = TS
        self.P = P
        self.NT = T // 128
        self.NB = T // 64
        self.nc = bass.Bass("TRN2", target_bir_lowering=False)
        self.ins = {}
        self.outs = {}

    def din(self, name, shape, dt=F32):
        self.ins[name] = self.nc.dram_tensor(name, list(shape), dt, kind="ExternalInput").ap()
        return self.ins[name]

    def dout(self, name, shape, dt=F32):
        self.outs[name] = self.nc.dram_tensor(name, list(shape), dt, kind="ExternalOutput").ap()
        return self.outs[name]

    def dscr(self, name, shape, dt=F32):
        return self.nc.dram_tensor(name, list(shape), dt).ap()

    def setup(self, st):
        nc = self.nc
        self.st = st
        import os as _os
        self.S = Sched(nc, st, same_engine_sync=(_os.environ.get("SAMESYNC", "1") == "1"))
        S = self.S
        T, NB = self.T, self.NB
        cin = {}
        cshapes = {"ident": ([128, 128], F32), "identb": ([128, 128], BF16), "pos": ([6, max(T, self.TS)], BF16),
                   "cend": ([6, 512], BF16), "slope": ([4, 6, 4, 128], BF16), "causal": ([128, 4, 128], BF16),
                   "far": ([128, 4, 128], BF16), "maskc": ([128, 512], F32), "allow": ([128, 128], F32),
                   "allowm1": ([128, 128], F32), "force": ([128, 128], F32), "Eall": ([NB, T], BF16),
                   "Mov": ([128, 2, NB], BF16), "maskbd": ([128, 128], F32), "selh": ([4, 4, 128], F32),
                   "trib": ([128, 128], F32), "eye4": ([4, 4, 4], F32)}
        if self.P:
            P_ = self.P
            NCs_ = P_ // 16
            NJ_ = max(1, NCs_ // 128)
            NBp_ = ((P_ // 64 + 1 + 3) // 4) * 4
            cshapes.update({"EallS": ([P_ // 64, P_], BF16), "MovS": ([128, NJ_, NBp_], BF16), "ownm": ([4, 4, 4], BF16),
                            "maskcs": ([4, NJ_ * 128], F32), "pcol2": ([128, 2], F32), "posP": ([6, 4], BF16), "pos4": ([6, 4, P_], BF16),
                            "slope4": ([4, 6, 4], BF16)})
        for k, (shp, dt) in cshapes.items():
            cin[k] = self.din("c_" + k, shp, dt)
        self.cin = cin
        self.ps = [st.enter_context(nc.psum_tensor("ps%d" % i, [128, 512], F32)) for i in range(8)]

        def sb(name, shape, dt):
            return st.enter_context(nc.sbuf_tensor(name, shape, dt))
        self.sb = sb
        self.ident = sb("ident", [128, 128], F32)
        self.identb = sb("identb", [128, 128], BF16)
        self.causal = sb("causal", [128, 4, 128], BF16)
        self.far = sb("far", [128, 4, 128], BF16)
        self.maskc = sb("maskc", [128, 512], F32)
        self.allow = sb("allow", [128, 128], F32)
        self.allowm1 = sb("allowm1", [128, 128], F32)
        self.force = sb("force", [128, 128], F32)
        self.selc = sb("selc", [102, 128], BF16)
        self.onesb = sb("onesb", [128, 128], BF16)
        self.maskbd = sb("maskbd", [128, 128], F32)
        self.selh = sb("selh", [4, 4, 128], F32)
        self.trib = sb("trib", [128, 128], F32)
        for nm in ("ident", "identb", "causal", "far", "maskc", "allow", "allowm1", "force", "maskbd", "selh", "trib"):
            t = getattr(self, nm)
            S.d("sync", t[:], cin[nm], w=[nm])
        self.negm = sb("negm", [128, 1], F32)
        S.v("memset", self.negm[:], -MARGIN, w=["negm"])
        S.v("memset", self.selc[:], 1.0, w=["selc"])
        S.v("memset", self.selc[64:96, :], 0.0, w=["selc"])
        S.v("memset", self.onesb[:], 1.0, w=["onesb"])

    def psb(self, i):
        return self.ps[i][:].bitcast(BF16)

    def front(self, xt, nw_bc, hT, junk, ss, pfx, nrows=128):
        S = self.S
        h = self.h_bf
        import os as _os
        if _os.environ.get("NOACC") == "1":
            S.v("tensor_tensor", junk[0:nrows, :], xt[0:nrows, :], xt[0:nrows, :], ALU.mult, r=[pfx + "x"], w=["junk"])
            S.v("tensor_reduce", ss[0:nrows, 0:1], junk[0:nrows, :], AX.X, ALU.add, r=["junk"], w=["ss"])
        else:
            S.a("activation", junk[0:nrows, :], xt[0:nrows, :], AF.Square, accum_out=ss[0:nrows, 0:1],
                r=[pfx + "x"], w=["junk", "ss"])
        if _os.environ.get("NOSQRT") == "1":
            S.a("activation", ss[0:nrows, 1:2], ss[0:nrows, 0:1], AF.Ln, scale=1.0 / D_MODEL, bias=self.epsc[0:nrows, 0:1],
                r=["ss", "epsc"], w=["ss1"])
            S.a("activation", ss[0:nrows, 2:3], ss[0:nrows, 1:2], AF.Exp, scale=-0.5, r=["ss1"], w=["ss2"])
        else:
            S.a("activation", ss[0:nrows, 1:2], ss[0:nrows, 0:1], AF.Sqrt, scale=1.0 / D_MODEL, bias=self.epsc[0:nrows, 0:1],
                r=["ss", "epsc"], w=["ss1"])
            S.v("reciprocal", ss[0:nrows, 2:3], ss[0:nrows, 1:2], r=["ss1"], w=["ss2"])
        S.v("scalar_tensor_tensor", h[0:nrows, :], xt[0:nrows, :], ss[0:nrows, 2:3], nw_bc[0:nrows, :], ALU.mult, ALU.mult,
            r=[pfx + "x", "ss2", "nw_bc"], w=["h_bf"])
        pb = self.psb(0)
        for c in range(8):
            S.p("transpose", pb[:, c * 128:c * 128 + nrows], h[0:nrows, c * 128:(c + 1) * 128], self.identb[0:nrows, 0:nrows],
                r=["h_bf", "identb"], w=["ps0"])
        S.v("tensor_copy", hT[:, :, 0:nrows], pb[:, 0:1024].rearrange("p (c n) -> p c n", c=8)[:, :, 0:nrows],
            r=["ps0"], w=[pfx + "hT"])

    def nsa_layer(self, l, xsrc, xdst, w):
        nc, S, T, NT, NB = self.nc, self.S, self.T, self.NT, self.NB
        ps = self.ps
        with ExitStack() as L:
            def sb(name, shape, dt):
                return L.enter_context(nc.sbuf_tensor("%s_l%d" % (name, l), shape, dt))
            nw_bc = sb("nw_bc", [128, 1024], F32)
            self.h_bf = sb("h_bf", [128, 1024], BF16)
            self.epsc = sb("epsc", [128, 1], F32)
            junk = sb("junk", [128, 1024], BF16)
            ss = sb("ss", [128, 4], F32)
            sm = self.nsa_samp_alloc(l, sb, w) if "samp" in w else None
            Lp = ExitStack()

            def sbp(name, shape, dt):
                return Lp.enter_context(nc.sbuf_tensor("%s_l%d" % (name, l), shape, dt))
            KselT = [sbp("KselT%d" % k, [102, T], BF16) for k in range(4)]
            KwinR = sbp("KwinR", [102, 4, 6 * 128], BF16)
            Vsel = sbp("Vsel", [128, NT, 4, 65], BF16)
            VwinR = sbp("VwinR", [128, 6, 4, 65], BF16)
            KcT = [sbp("KcT%d" % k, [102, 256], BF16) for k in range(4)]
            vc = sbp("vc", [128, 2, 4, 64], BF16)
            gates = sbp("gates", [128, NT, 48], F32)
            S.v("memset", self.epsc[:], RMS_EPS, w=["epsc"])
            S.d("sync", nw_bc[:], w["norm_w"].partition_broadcast(128), w=["nw_bc"])
            for k in range(4):
                for (Kt, nm) in ((KselT[k], "KselT%d" % k),):
                    wr = [nm + "_%d" % t for t in range(NT)]
                    S.v("memset", Kt[64:96, :], 0.0, w=wr)
                    S.v("memset", Kt[64:65, :], 1.0, w=wr)
                    S.d("sync", Kt[96:102, :], self.cin["pos"][:, 0:T], w=wr)
                S.v("memset", KcT[k][64:96, :], 0.0, w=["KcT%d" % k])
                S.d("sync", KcT[k][96:102, :], self.cin["cend"][:, 0:256], w=["KcT%d" % k])
            S.v("memset", Vsel[:, :, :, 64:65], 1.0, w=["Vsel_%d" % t for t in range(NT)])
            S.v("memset", VwinR[:, :, :, 64:65], 1.0, w=["VwinR_%d" % t for t in range(6)])
            S.v("memset", KwinR[64:96, :, :], 0.0, w=["KwinR_%d" % t for t in range(6)])
            S.v("memset", KwinR[64:65, :, :], 1.0, w=["KwinR_%d" % t for t in range(6)])
            kwTs = self.dscr("kwTs_l%d" % l, [NT, 64, 4, 128], BF16)
            vws = self.dscr("vws_l%d" % l, [NT, 128, 256], BF16)
            if getattr(self, "stop", None) == "init":
                S.flush()
                return
            hTs = self.dscr("hTs_l%d" % l, [NT, 128, 1024], BF16)
            zs = self.dscr("zs_l%d" % l, [NT, 128, 1024], BF16)

            with ExitStack() as A:
                def sba(name, shape, dt):
                    return A.enter_context(nc.sbuf_tensor("%s_A%d" % (name, l), shape, dt))
                KVcT = [[sba("KVcT%d%d" % (c, pr), [128, T + 32], BF16) for pr in range(2)] for c in range(2)]
                A1 = ExitStack()
                sba_outer = sba

                def sba(name, shape, dt):
                    return A1.enter_context(nc.sbuf_tensor("%s_A%d" % (name, l), shape, dt))
                NCOL = NSA_IN - 1024
                wkv = sba("wkv", [128, 8, NCOL], BF16)
                HC = NCOL // 4
                wst = [sba("wst%d" % i, [128, HC], F32) for i in range(2)]
                for c in range(8):
                    for q_ in range(4):
                        hf = q_ % 2
                        S.d("sync", wst[hf][:], w["w_in"][c * 128:(c + 1) * 128, 1024 + q_ * HC:1024 + (q_ + 1) * HC], w=["wst%d" % hf])
                        if hf == 0:
                            S.v("tensor_copy", wkv[:, c, q_ * HC:(q_ + 1) * HC], wst[hf][:], r=["wst%d" % hf], w=["wkv"])
                        else:
                            S.a("copy", wkv[:, c, q_ * HC:(q_ + 1) * HC], wst[hf][:], r=["wst%d" % hf], w=["wkv"])
                if getattr(self, "stop", None) == "A1w":
                    S.flush()
                    return
                for c in range(2):
                    for pr in range(2):
                        S.v("memset", KVcT[c][pr][:, T:T + 32], 0.0, w=["KVcT%d%d" % (c, pr)])
                xt = [sba("xtA%d" % i, [128, 1024], F32) for i in range(2)]
                hT = [sba("hTA%d" % i, [128, 8, 128], BF16) for i in range(2)]
                kvt = [sba("kvt0", [128, 1024], F32)] * 2
                wnt = [sba("wnt0", [128, 512], F32)] * 2
                zt = [sba("zt0", [128, 1024], BF16)] * 2
                kwst = [sba("kwst%d" % i, [64, 4, 128], BF16) for i in range(2)]
                vwst = [sba("vwst%d" % i, [128, 256], BF16) for i in range(2)]
                OKV, OWIN, OG, OZ = 0, 1024, 1536, 1584
                for tt in range(NT):
                    b = tt % 2
                    pf = "A%d" % b
                    S.d("sync", xt[b][:], xsrc[tt * 128:(tt + 1) * 128, :], r=["xres_%d" % tt], w=[pf + "x"])
                    self.front(xt[b], nw_bc, hT[b], junk, ss, pf)
                    if getattr(self, "stop", None) == "A1f":
                        S.flush()
                        return
                    S.d("gpsimd", hTs[tt], hT[b][:].rearrange("p c n -> p (c n)"), r=[pf + "hT"], w=["hTs%d" % tt])
                    if getattr(self, "stop", None) == "A1h":
                        S.flush()
                        return
                    groups = [("kcmp", OKV + 0), ("vcmp", OKV + 256), ("ksel", OKV + 512), ("kwin", OWIN + 0)]
                    for gi, (gn, off) in enumerate(groups):
                        if getattr(self, "stop", None) == "A1g%d" % gi:
                            S.flush()
                            return
                        bank = 1 + (gi % 4)
                        pk = "ps%d" % bank
                        if gn in ("kcmp", "vcmp"):
                            for pr in range(2):
                                for c in range(8):
                                    S.p("matmul", ps[bank][:, pr * 128:(pr + 1) * 128],
                                        wkv[:, c, off + pr * 128: off + (pr + 1) * 128], hT[b][:, c, :],
                                        start=(c == 0), stop=(c == 7), r=["wkv", pf + "hT"], w=[pk])
                            ci = 0 if gn == "kcmp" else 1
                            import os as _os
                            if _os.environ.get("NOEVAC") == "1":
                                S.flush()
                                return
                            for pr in range(2):
                                if _os.environ.get("NOEVAC") == "2" and pr == 1:
                                    S.flush()
                                    return
                                if pr == 1 and _os.environ.get("ACTV") == "ident":
                                    S.a("activation", KVcT[ci][pr][:, tt * 128:(tt + 1) * 128], ps[bank][:, pr * 128:(pr + 1) * 128], AF.Identity,
                                        r=[pk], w=["KVcT%d%d" % (ci, pr)])
                                    continue
                                if _os.environ.get("ACTV") == "serial" and pr == 1:
                                    S.a("copy", KVcT[ci][pr][:, tt * 128:(tt + 1) * 128], ps[bank][:, pr * 128:(pr + 1) * 128],
                                        r=[pk, "KVcT%d%d" % (ci, 0)], w=["KVcT%d%d" % (ci, pr)])
                                    continue
                                if _os.environ.get("ACTV") == "swap":
                                    if pr == 0:
                                        S.a("copy", KVcT[ci][pr][:, tt * 128:(tt + 1) * 128], ps[bank][:, pr * 128:(pr + 1) * 128],
                                            r=[pk], w=["KVcT%d%d" % (ci, pr)])
                                    else:
                                        S.v("tensor_copy", KVcT[ci][pr][:, tt * 128:(tt + 1) * 128], ps[bank][:, pr * 128:(pr + 1) * 128],
                                            r=[pk], w=["KVcT%d%d" % (ci, pr)])
                                    continue
                                if pr == 1 and _os.environ.get("ACTV") == "dve":
                                    S.v("tensor_copy", KVcT[ci][pr][:, tt * 128:(tt + 1) * 128], ps[bank][:, pr * 128:(pr + 1) * 128],
                                        r=[pk], w=["KVcT%d%d" % (ci, pr)])
                                    continue
                                if pr == 1 and _os.environ.get("ACTV") == "nokey":
                                    S.a("copy", KVcT[ci][pr][:, tt * 128:(tt + 1) * 128], ps[bank][:, pr * 128:(pr + 1) * 128],
                                        r=[pk], w=["KVcTx%d%d" % (ci, pr)])
                                    continue
                                eng = S.v if pr == 0 else S.a
                                mm = "tensor_copy" if pr == 0 else "copy"
                                eng(mm, KVcT[ci][pr][:, tt * 128:(tt + 1) * 128], ps[bank][:, pr * 128:(pr + 1) * 128],
                                    r=[pk], w=["KVcT%d%d" % (ci, pr)])
                        else:
                            for hh in range(4):
                                for c in range(8):
                                    S.p("matmul", ps[bank][0:64, hh * 128:(hh + 1) * 128],
                                        wkv[:, c, off + hh * 64: off + (hh + 1) * 64], hT[b][:, c, :],
                                        start=(c == 0), stop=(c == 7), r=["wkv", pf + "hT"], w=[pk])
                            if gn == "ksel":
                                for hh in range(4):
                                    eng = S.v if hh % 2 == 0 else S.a
                                    mm = "tensor_copy" if hh % 2 == 0 else "copy"
                                    eng(mm, KselT[hh][0:64, tt * 128:(tt + 1) * 128], ps[bank][0:64, hh * 128:(hh + 1) * 128],
                                        r=[pk], w=["KselT%d_%d" % (hh, tt)])
                            else:
                                S.a("copy", kwst[b][:].rearrange("p h n -> p (h n)"), ps[bank][0:64, :], r=[pk], w=[pf + "kwst"])
                                S.d("gpsimd", kwTs[tt], kwst[b][:], r=[pf + "kwst"], w=["kwTs%d" % tt])
                    if getattr(self, "stop", None) == "A1g":
                        S.flush()
                        return
                    tm = [(OKV, 512, 5), (OKV + 512, 512, 6), (OWIN, 512, 7), (OG, 48, 5), (OZ, 512, 6), (OZ + 512, 512, 7)]
                    for ti, (off, ncol, bank) in enumerate(tm):
                        pk = "ps%d" % bank
                        for c in range(8):
                            S.p("matmul", ps[bank][:, 0:ncol], hT[b][:, c, :], wkv[:, c, off:off + ncol],
                                start=(c == 0), stop=(c == 7), r=["wkv", pf + "hT"], w=[pk])
                        if ti == 0:
                            S.v("tensor_copy", kvt[b][:, 0:512], ps[bank][:, 0:512], r=[pk], w=["Akvt"])
                        elif ti == 1:
                            S.a("copy", kvt[b][:, 512:1024], ps[bank][:, 0:512], r=[pk], w=["Akvt"])
                            S.v("tensor_copy", Vsel[:, tt, :, 0:64],
                                ps[bank][:, 256:512].rearrange("p (h d) -> p h d", h=4), r=[pk], w=["Vsel_%d" % tt])
                            S.d("gpsimd", w["kv_out"][tt * 128:(tt + 1) * 128, :], kvt[b][:], r=["Akvt"], w=["kvout%d_%d" % (l, tt)])
                        elif ti == 2:
                            S.a("copy", wnt[b][:], ps[bank][:, 0:512], r=[pk], w=["Awnt"])
                            S.v("tensor_copy", vwst[b][:], ps[bank][:, 256:512], r=[pk], w=[pf + "vwst"])
                            S.d("gpsimd", vws[tt], vwst[b][:], r=[pf + "vwst"], w=["vws%d" % tt])
                            wr = min(512, T)
                            if tt * 128 >= T - wr:
                                r0 = tt * 128 - (T - wr)
                                S.d("gpsimd", w["win_out"][r0:r0 + 128, :], wnt[b][:], r=["Awnt"], w=["winout%d_%d" % (l, tt)])
                        elif ti == 3:
                            S.a("activation", gates[:, tt, :], ps[bank][:, 0:48], AF.Sigmoid, r=[pk], w=["gates_%d" % tt])
                        elif ti == 4:
                            S.a("activation", zt[b][:, 0:512], ps[bank][:, 0:512], AF.Silu, r=[pk], w=["Azt"])
                        else:
                            S.a("activation", zt[b][:, 512:1024], ps[bank][:, 0:512], AF.Silu, r=[pk], w=["Azt"])
                            S.d("gpsimd", zs[tt], zt[b][:], r=["Azt"], w=["zs%d" % tt])
                if getattr(self, "stop", None) == "A1":
                    S.flush()
                    return
                if sm is not None:
                    self.nsa_samp_A(l, sm, w, wkv, nw_bc, junk, ss, OKV, OWIN, OG, OZ)
                S.flush()
                A1.close()
                sba = sba_outer
                w1s = sba("w1s", [128, 32, 128], F32)
                w1b = [sba("w1b%d" % c, [128, 32, 128], BF16) for c in range(2)]
                w2s = sba("w2s", [128, 2, 64], F32)
                w2b = sba("w2b", [128, 2, 64], BF16)
                peTs = sba("peTs", [64, 2, 32], F32)
                peT = sba("peT", [64, 2, 32], BF16)
                hb = sba("hb", [128, 2], F32)
                b1c = sba("b1c", [128, 2], F32)
                ShT = sba("ShT", [128, 256], BF16)
                NCc = T // 16
                S.v("memset", ShT[:], 0.0, w=["ShT"])
                for c in range(2):
                    for half in range(2):
                        S.d("sync", w1s[half * 64:(half + 1) * 64, :, :], w["w1"][c].rearrange("r d e -> d r e"), w=["w1s"])
                    S.v("tensor_copy", w1b[c][:], w1s[:], r=["w1s"], w=["w1b%d" % c])
                S.d("sync", w2s[:], w["w2"].rearrange("c e d -> e c d"), w=["w2s"])
                S.v("tensor_copy", w2b[:], w2s[:], r=["w2s"], w=["w2b"])
                S.d("sync", peTs[:], w["pe"].rearrange("c r d -> d c r"), w=["peTs"])
                S.d("sync", b1c[:], w["b1"].rearrange("c e -> e c"), w=["b1c"])
                S.v("tensor_copy", peT[:], peTs[:], r=["peTs"], w=["peT"])
                for c in range(2):
                    for r_ in range(32):
                        S.p("matmul", ps[1][:, c:c + 1], w1b[c][0:64, r_, :], peT[:, c, r_:r_ + 1],
                            start=(r_ == 0), stop=(r_ == 31), r=["w1b%d" % c, "peT"], w=["ps1"])
                S.v("tensor_tensor", hb[:], ps[1][:, 0:2], b1c[:], ALU.add, r=["ps1", "b1c"], w=["hb"])
                if sm is not None:
                    S.v("tensor_copy", sm["hb"][:], hb[:], r=["hb"], w=["s_hb"])
                for c in range(2):
                    for hh in range(4):
                        pr, hl = hh // 2, hh % 2
                        bank = 2 + ((c * 4 + hh) % 2)
                        pk = "ps%d" % bank
                        src = KVcT[c][pr]
                        for r_ in range(32):
                            S.p("matmul", ps[bank][:, 0:NCc], w1b[c][hl * 64:(hl + 1) * 64, r_, :],
                                src[hl * 64:(hl + 1) * 64, r_:r_ + 16 * NCc:16],
                                start=(r_ == 0), stop=(r_ == 31), r=["w1b%d" % c, "KVcT%d%d" % (c, pr)], w=[pk])
                        S.a("activation", ShT[:, 0:NCc], ps[bank][:, 0:NCc], AF.Silu, bias=hb[:, c:c + 1], r=[pk, "hb"], w=["ShT"])
                        if c == 0:
                            S.p("matmul", ps[4][0:64, 0:256], w2b[:, 0, :], ShT[:], start=True, stop=True,
                                r=["w2b", "ShT"], w=["ps4"])
                            S.v("tensor_copy", KcT[hh][0:64, :], ps[4][0:64, 0:256], r=["ps4"], w=["KcT%d" % hh])
                        else:
                            for j in range(2):
                                S.p("matmul", ps[4][:, 256 + j * 64:256 + (j + 1) * 64], ShT[:, j * 128:(j + 1) * 128], w2b[:, 1, :],
                                    start=True, stop=True, r=["w2b", "ShT"], w=["ps4"])
                            S.v("tensor_copy", vc[:, :, hh, :], ps[4][:, 256:384].rearrange("p (j d) -> p j d", j=2),
                                r=["ps4"], w=["vc"])
                S.flush()

            if getattr(self, "stop", None) == "A":
                return
            with ExitStack() as Bk:
                def sbb(name, shape, dt):
                    return Bk.enter_context(nc.sbuf_tensor("%s_B%d" % (name, l), shape, dt))
                wq = sbb("wq", [128, 8, 1024], BF16)
                wo = sbb("wo", [128, 8, 1024], BF16)
                wst = [sbb("wstB%d" % i, [128, 1024], F32) for i in range(2)]
                k = 0
                for (dst, src, nm) in ((wq, w["w_in"], "wq"), (wo, w["w_out"], "wo")):
                    for c in range(8):
                        S.d("sync", wst[k % 2][:], src[c * 128:(c + 1) * 128, 0:1024], w=["wstB%d" % (k % 2)])
                        if k % 2 == 0:
                            S.v("tensor_copy", dst[:, c, :], wst[k % 2][:], r=["wstB%d" % (k % 2)], w=[nm])
                        else:
                            S.a("copy", dst[:, c, :], wst[k % 2][:], r=["wstB%d" % (k % 2)], w=[nm])
                        k += 1
                Eall = sbb("Eall", [NB, T], BF16)
                Mov = sbb("Mov", [128, 2, NB], BF16)
                S.d("sync", Eall[:], self.cin["Eall"], w=["Eall"])
                S.d("sync", Mov[:], self.cin["Mov"], w=["Mov"])
                QT = [[sbb("QT%d_%d" % (kk, i), [102, 4, 128], BF16) for i in range(2)] for kk in range(4)]
                for kk in range(4):
                    for i in range(2):
                        S.v("memset", QT[kk][i][64:96, :, :], 0.0, w=["QT%d_%d" % (kk, i)])
                        S.d("sync", QT[kk][i][96:102, :, :], self.cin["slope"][kk], w=["QT%d_%d" % (kk, i)])
                hT = [sbb("hTB%d" % i, [128, 8, 128], BF16) for i in range(2)]
                sz = [sbb("sz%d" % i, [128, 1024], BF16) for i in range(2)]
                xt = [sbb("xtB%d" % i, [128, 1024], F32) for i in range(2)]
                ksum = sbb("ksum", [102, 128], BF16)
                prod = sbb("prod", [102, 4, 128], BF16)
                Sm = sbb("Sm", [128, 4, 256], F32)
                mx = sbb("mx", [128, 16], F32)
                Pn = sbb("Pn", [128, 4, 256], BF16)
                PnT = sbb("PnT", [128, 2, 4, 128], BF16)
                sc = sbb("sc", [128, NB], F32)
                sc2 = sbb("sc2", [128, NB], F32)
                m8 = sbb("m8", [128, 16], F32)
                bb = sbb("bb", [128, NB], BF16)
                bbT = sbb("bbT", [NB, 4, 128], BF16)
                PT = [sbb("PT%d" % i, [128, 4, 128], BF16) for i in range(3)]
                acc = sbb("acc", [128, 16, 64], F32)
                gr = sbb("gr", [128, 8], F32)
                oz = sbb("oz", [128, 1024], BF16)
                ozT = sbb("ozT", [128, 8, 128], BF16)
                xo = sbb("xo", [128, 1024], F32)
                pti = 0
                for qb in range(NT):
                    b = qb % 2
                    pf = "B%d" % b
                    s0 = qb * 128
                    S.d("sync", hT[b][:].rearrange("p c n -> p (c n)"), hTs[qb], r=["hTs%d" % qb], w=[pf + "hT"])
                    S.d("sync", sz[b][:], zs[qb], r=["zs%d" % qb], w=[pf + "sz"])
                    S.d("sync", xt[b][:], xsrc[s0:s0 + 128, :], r=["xres_%d" % qb], w=[pf + "x"])
                    slot = qb % 6
                    S.d("sync", KwinR[0:64, :, slot * 128:(slot + 1) * 128], kwTs[qb], r=["kwTs%d" % qb], w=["KwinR_%d" % slot])
                    for hh in range(4):
                        S.d("sync", KwinR[96:102, hh, slot * 128:(slot + 1) * 128], self.cin["pos"][:, s0:s0 + 128], w=["KwinR_%d" % slot])
                    S.d("sync", VwinR[:, slot, :, 0:64], vws[qb].rearrange("p (h d) -> p h d", h=4), r=["vws%d" % qb], w=["VwinR_%d" % slot])
                    for kk in range(4):
                        qn = "QT%d_%d" % (kk, b)
                        Q = QT[kk][b]
                        Qf = Q[:].rearrange("p g n -> p (g n)")
                        bank = 1 + (kk % 2)
                        pk = "ps%d" % bank
                        for g_ in range(4):
                            hd = kk * 4 + g_
                            for c in range(8):
                                S.p("matmul", ps[bank][0:64, g_ * 128:(g_ + 1) * 128], wq[:, c, hd * 64:(hd + 1) * 64], hT[b][:, c, :],
                                    start=(c == 0), stop=(c == 7), r=["wq", pf + "hT"], w=[pk])
                        S.a("mul", Qf[0:64, :], ps[bank][0:64, :], HD ** -0.5, r=[pk], w=[qn])
                        S.v("tensor_tensor", ksum[:], KselT[kk][:, s0:s0 + 128], KwinR[:, kk, slot * 128:(slot + 1) * 128], ALU.add,
                            r=["KselT%d_%d" % (kk, qb), "KwinR_%d" % slot], w=["ksum"])
                        for g_ in range(4):
                            S.v("scalar_tensor_tensor", prod[:, g_, :], Q[:, g_, :], 0.5, ksum[:], ALU.mult, ALU.mult,
                                r=[qn, "ksum"], w=["prod"])
                        S.p("matmul", ps[3][:, :], self.selc[:, :], prod[:].rearrange("p g n -> p (g n)"), start=True, stop=True,
                            r=["selc", "prod"], w=["ps3"])
                        S.a("activation", Qf[64:65, :], ps[3][64:65, :], AF.Identity, scale=-1.0, bias=self.negm[64:65, 0:1],
                            r=["ps3", "negm"], w=[qn])
                        ncch = min(2, (8 * qb + 8 + 127) // 128)
                        NCv = ncch * 128
                        for g_ in range(4):
                            bank = 3 + g_ // 2
                            S.p("matmul", ps[bank][:, (g_ % 2) * 256:(g_ % 2) * 256 + NCv], Q[:, g_, :], KcT[kk][:, 0:NCv],
                                start=True, stop=True, r=[qn, "KcT%d" % kk], w=["ps%d" % bank])
                        moff = 256 - 8 * qb
                        for half in range(2):
                            bank = 3 + half
                            S.v("tensor_tensor", Sm[:, 2 * half:2 * half + 2, 0:NCv],
                                ps[bank][:].rearrange("p (g n) -> p g n", g=2)[:, :, 0:NCv],
                                bcast(self.maskc[:, moff:moff + NCv], 1, 2), ALU.add,
                                r=["ps%d" % bank, "maskc"], w=["Sm"])
                        S.v("tensor_reduce", mx[:, 0:4], Sm[:, :, 0:NCv], AX.X, ALU.max, r=["Sm"], w=["mx"])
                        S.v("tensor_scalar", mx[:, 4:8], mx[:, 0:4], -10000.0, -1.0, ALU.max, ALU.mult, r=["mx"], w=["mx"])
                        for g_ in range(4):
                            S.a("activation", Sm[:, g_, 0:NCv], Sm[:, g_, 0:NCv], AF.Exp, bias=mx[:, 4 + g_:5 + g_],
                                accum_out=mx[:, 8 + g_:9 + g_], r=["Sm", "mx"], w=["Sm", "mx"])
                        S.v("tensor_scalar", mx[:, 12:16], mx[:, 8:12], 1e-30, None, ALU.max, r=["mx"], w=["mx"])
                        S.v("reciprocal", mx[:, 12:16], mx[:, 12:16], r=["mx"], w=["mx"])
                        S.v("tensor_tensor", Pn[:, :, 0:NCv], Sm[:, :, 0:NCv], bcast(mx[:, 12:16], 2, NCv), ALU.mult,
                            r=["Sm", "mx"], w=["Pn"])
                        pb = self.psb(0)
                        for j in range(ncch):
                            for g_ in range(4):
                                S.p("transpose", pb[:, (j * 4 + g_) * 128:(j * 4 + g_ + 1) * 128], Pn[:, g_, j * 128:(j + 1) * 128],
                                    self.identb[:], r=["Pn", "identb"], w=["ps0"])
                        S.v("tensor_copy", PnT[:, 0:ncch, :, :].rearrange("p j g n -> p (j g n)"), pb[:, 0:ncch * 512],
                            r=["ps0"], w=["PnT"])
                        first = True
                        for g_ in range(4):
                            for j in range(ncch):
                                S.p("matmul", ps[5][:, g_ * 64:(g_ + 1) * 64], PnT[:, j, g_, :], vc[:, j, kk, :],
                                    start=first, stop=False, skip_group_check=True, r=["PnT", "vc"], w=["ps5"])
                                first = False
                                S.p("matmul", ps[5][:, 256:256 + NB], PnT[:, j, g_, :], Mov[:, j, :],
                                    start=False, stop=(g_ == 3 and j == ncch - 1), skip_group_check=True,
                                    r=["PnT", "Mov"], w=["ps5"])
                        gcol = gates[:, qb, :].rearrange("p (h t) -> p h t", t=3)
                        S.v("tensor_tensor", acc[:, kk * 4:(kk + 1) * 4, :], ps[5][:, 0:256].rearrange("p (g d) -> p g d", g=4),
                            bcast(gcol[:, kk * 4:(kk + 1) * 4, 0], 2, 64), ALU.mult,
                            r=["ps5", "gates_%d" % qb], w=["acc%d" % kk])
                        aoff = 64 - 2 * qb
                        S.v("tensor_tensor", sc[:], ps[5][:, 256:256 + NB], self.allow[:, aoff:aoff + NB], ALU.mult,
                            r=["ps5", "allow"], w=["sc"])
                        S.v("tensor_tensor", sc[:], sc[:], self.allowm1[:, aoff:aoff + NB], ALU.add, r=["sc", "allowm1"], w=["sc"])
                        S.v("tensor_tensor", sc[:], sc[:], self.force[:, aoff:aoff + NB], ALU.max, r=["sc", "force"], w=["sc"])
                        S.v("memset", sc[:, 0:1], 1.0e4, w=["sc"])
                        S.v("max", m8[:, 0:8], sc[:], r=["sc"], w=["m8"])
                        S.v("match_replace", sc2[:], m8[:, 0:8], sc[:], -1.0e30, r=["sc", "m8"], w=["sc2"])
                        S.v("max", m8[:, 8:16], sc2[:], r=["sc2"], w=["m8"])
                        S.v("tensor_scalar", sc2[:], sc[:], m8[:, 15:16], None, ALU.is_ge, r=["sc", "m8"], w=["sc2"])
                        S.v("scalar_tensor_tensor", sc2[:], sc[:], -0.5, sc2[:], ALU.is_gt, ALU.mult, r=["sc", "sc2"], w=["sc2"])
                        S.v("tensor_scalar", bb[:], sc2[:], BIG, -BIG, ALU.mult, ALU.add, r=["sc2"], w=["bb"])
                        S.p("transpose", pb[0:NB, 0:128], bb[:], self.identb[:], r=["bb", "identb"], w=["ps0"])
                        S.v("tensor_copy", bbT[:], bcast(pb[0:NB, 0:128], 1, 4), r=["ps0"], w=["bbT"])
                        bbTf = bbT[:].rearrange("p g n -> p (g n)")
                        for (br, obank, gi) in (("sel", 6, 1), ("win", 7, 2)):
                            kcs = list(range(0, qb + 1)) if br == "sel" else list(range(max(0, qb - 4), qb + 1))
                            ok = "ps%d" % obank
                            slots = []

                            def emit_qk(ci, kc, pti=pti, slots=slots, br=br, kk=kk, qb=qb):
                                sbank = 1 + ((pti + ci) % 2)
                                sk = "ps%d" % sbank
                                P_ = PT[(pti + ci) % 3]
                                pn = "PT%d" % ((pti + ci) % 3)
                                extra = []
                                if br == "sel":
                                    extra.append((Eall[:, kc * 128:(kc + 1) * 128], bbTf, ["Eall", "bbT"]))
                                if kc == qb:
                                    extra.append((self.identb[:], self.causal[:].rearrange("p g n -> p (g n)"), ["identb", "causal"]))
                                if br == "win" and kc == qb - 4:
                                    extra.append((self.identb[:], self.far[:].rearrange("p g n -> p (g n)"), ["identb", "far"]))
                                if br == "sel":
                                    Kap, kkey = KselT[kk][:, kc * 128:(kc + 1) * 128], "KselT%d_%d" % (kk, kc)
                                    Vap, vkey = Vsel[:, kc, kk, :], "Vsel_%d" % kc
                                else:
                                    sl_ = kc % 6
                                    Kap, kkey = KwinR[:, kk, sl_ * 128:(sl_ + 1) * 128], "KwinR_%d" % sl_
                                    Vap, vkey = VwinR[:, sl_, kk, :], "VwinR_%d" % sl_
                                S.p("matmul", ps[sbank][:, :], Kap, Qf, start=True, stop=(len(extra) == 0),
                                    r=[kkey, qn], w=[sk])
                                for ei, (lt, rh, rr) in enumerate(extra):
                                    S.p("matmul", ps[sbank][:, :], lt, rh, start=False, stop=(ei == len(extra) - 1), r=rr, w=[sk])
                                slots.append((sbank, sk, P_, pn, Vap, vkey))

                            emit_qk(0, kcs[0])
                            for ci, kc in enumerate(kcs):
                                if ci + 1 < len(kcs):
                                    emit_qk(ci + 1, kcs[ci + 1])
                                sbank, sk, P_, pn, Vap, vkey = slots[ci]
                                S.a("activation", P_[:].rearrange("p g n -> p (g n)"), ps[sbank][:, :], AF.Exp, r=[sk], w=[pn])
                                for g_ in range(4):
                                    S.p("matmul", ps[obank][:, g_ * 65:(g_ + 1) * 65], P_[:, g_, :], Vap,
                                        start=(ci == 0 and g_ == 0), stop=(ci == len(kcs) - 1 and g_ == 3), skip_group_check=True,
                                        r=[pn, vkey], w=[ok])
                            pti += len(kcs)
                            ov = ps[obank][:, 0:260].rearrange("p (g d) -> p g d", g=4)
                            S.v("reciprocal", gr[:, 0:4], ov[:, :, 64], r=[ok], w=["gr"])
                            S.v("tensor_tensor", gr[:, 4:8], gr[:, 0:4], gcol[:, kk * 4:(kk + 1) * 4, gi], ALU.mult,
                                r=["gr", "gates_%d" % qb], w=["gr"])
                            for g_ in range(4):
                                S.v("scalar_tensor_tensor", acc[:, kk * 4 + g_, :], ov[:, g_, 0:64], gr[:, 4 + g_:5 + g_],
                                    acc[:, kk * 4 + g_, :], ALU.mult, ALU.add, r=[ok, "gr", "acc%d" % kk], w=["acc%d" % kk])
                    S.v("tensor_tensor", oz[:], acc[:].rearrange("p h d -> p (h d)"), sz[b][:], ALU.mult,
                        r=["acc0", "acc1", "acc2", "acc3", pf + "sz"], w=["oz"])
                    pb = self.psb(0)
                    for c in range(8):
                        S.p("transpose", pb[:, c * 128:(c + 1) * 128], oz[:, c * 128:(c + 1) * 128], self.identb[:],
                            r=["oz", "identb"], w=["ps0"])
                    S.a("copy", ozT[:].rearrange("p c n -> p (c n)"), pb[:, 0:1024], r=["ps0"], w=["ozT"])
                    for half in range(2):
                        bank = 1 + half
                        for c in range(8):
                            S.p("matmul", ps[bank][:, :], ozT[:, c, :], wo[:, c, half * 512:(half + 1) * 512],
                                start=(c == 0), stop=(c == 7), r=["ozT", "wo"], w=["ps%d" % bank])
                        S.v("tensor_tensor", xo[:, half * 512:(half + 1) * 512], ps[bank][:, :], xt[b][:, half * 512:(half + 1) * 512],
                            ALU.add, r=["ps%d" % bank, pf + "x"], w=["xo"])
                    S.d("gpsimd", xdst[s0:s0 + 128, :], xo[:], r=["xo"], w=["xres_%d" % qb])
                if sm is not None:
                    self.nsa_samp_Q(sm, wq)
                S.flush()
            Lp.close()
            if sm is not None:
                self.nsa_samp_S(l, sm, w)


T_FULL = 4096
P_FULL = 8192
NPOOL_FULL = 2560
_CACHE = {}


def final_norm(self, xsrc, fw, ydst, nrows_total, pfx):
    nc, S = self.nc, self.S
    with ExitStack() as F:
        def sb(name, shape, dt):
            return F.enter_context(nc.sbuf_tensor("%s_f%s" % (name, pfx), shape, dt))
        fw_bc = sb("fw_bc", [128, 1024], F32)
        eps = sb("eps", [128, 1], F32)
        S.v("memset", eps[:], RMS_EPS, w=["f_eps" + pfx])
        S.d("sync", fw_bc[:], fw.partition_broadcast(128), w=["f_fw" + pfx])
        xt = [sb("xt%d" % i, [128, 1024], F32) for i in range(2)]
        yt = [sb("yt%d" % i, [128, 1024], F32) for i in range(2)]
        jk = sb("jk", [128, 1024], BF16)
        st_ = sb("st", [128, 4], F32)
        ntile = (nrows_total + 127) // 128
        for tt in range(ntile):
            b = tt % 2
            nr = min(128, nrows_total - tt * 128)
            rkey = ["xres_%d" % tt] if pfx == "p" else ["xsres"]
            S.d("sync", xt[b][0:nr, :], xsrc[tt * 128:tt * 128 + nr, :], r=rkey, w=["f_x%s%d" % (pfx, b)])
            S.a("activation", jk[0:nr, :], xt[b][0:nr, :], AF.Square, accum_out=st_[0:nr, 0:1], r=["f_x%s%d" % (pfx, b)], w=["f_jk" + pfx, "f_st" + pfx])
            S.a("activation", st_[0:nr, 1:2], st_[0:nr, 0:1], AF.Sqrt, scale=1.0 / D_MODEL, bias=eps[0:nr, 0:1], r=["f_st" + pfx, "f_eps" + pfx], w=["f_st1" + pfx])
            S.v("reciprocal", st_[0:nr, 2:3], st_[0:nr, 1:2], r=["f_st1" + pfx], w=["f_st2" + pfx])
            S.v("scalar_tensor_tensor", yt[b][0:nr, :], xt[b][0:nr, :], st_[0:nr, 2:3], fw_bc[0:nr, :], ALU.mult, ALU.mult,
                r=["f_x%s%d" % (pfx, b), "f_st2" + pfx, "f_fw" + pfx], w=["f_y%s%d" % (pfx, b)])
            S.d("gpsimd", ydst[tt * 128:tt * 128 + nr, :], yt[b][0:nr, :], r=["f_y%s%d" % (pfx, b)], w=["f_out%s%d" % (pfx, tt)])
        S.flush()


Builder.final_norm = final_norm


def build_full(T=T_FULL, P=P_FULL, NPOOL=NPOOL_FULL):
    key = (T, P, NPOOL)
    if key in _CACHE:
        return _CACHE[key]
    B = Builder(T, TS=P + 128, P=P)
    nc = B.nc
    NPG = P // 128
    xp = B.din("xp", [T, 1024])
    xs = B.din("xs", [4, 1024])
    pools = [B.din("pool%d" % i, [NPOOL * 256, 512]) for i in range(2)]
    cwin = B.din("cwin", [2, 4, 512, 512])
    sC = B.din("sC", [2, 4, 4, 512, 512])
    sn = B.din("sn", [2, 4, 2048])
    smm = B.din("smm", [2, 4, 4])
    sconv = B.din("sconv", [2, 4, 3, 2048])
    pt = B.din("pt", [4, NPG], I32)
    norm_w = B.din("norm_w", [4, 1024])
    fnw = B.din("final_norm_w", [1, 1024])
    nsa_w_in = B.din("nsa_w_in", [2, 1024, NSA_IN])
    nsa_w_out = B.din("nsa_w_out", [2, 1024, 1024])
    pe = B.din("nsa_cmp_pe", [2, 2, 32, 64])
    w1 = B.din("nsa_cmp_w1", [2, 2, 32, 64, 128])
    b1 = B.din("nsa_cmp_b1", [2, 2, 128])
    w2 = B.din("nsa_cmp_w2", [2, 2, 128, 64])
    m_w_in = B.din("m_w_in", [2, 1024, 4096])
    m_conv_w = B.din("m_conv_w", [2, 4, 2048])
    m_conv_b = B.din("m_conv_b", [2, 2048])
    m_w_qkv = B.din("m_w_qkv", [2, 3, 512, 4, 4])
    m_w_gate = B.din("m_w_gate", [2, 6144, 8])
    m_b_gate = B.din("m_b_gate", [2, 8])
    m_norm_w = B.din("m_norm_w", [2, 2048])
    m_skip = B.din("m_skip", [2, 2048])
    m_w_out = B.din("m_w_out", [2, 2048, 1024])
    y_p = B.dout("y_p", [T, 1024])
    y_s = B.dout("y_s", [4, 1024])
    kv_p = B.dout("kv_p", [2, T, 1024])
    kv_s = B.dout("kv_s", [2, 4, 1024])
    WR = min(512, T)
    win_p = B.dout("win_p", [2, WR, 512])
    win_s = B.dout("win_s", [2, 4, 512, 512])
    C_p = B.dout("C_p", [2, 4, 512, 512])
    C_s = B.dout("C_s", [2, 4, 4, 512, 512])
    n_p = B.dout("n_p", [2, 2048])
    n_s = B.dout("n_s", [2, 4, 2048])
    m_p = B.dout("m_p", [2, 4])
    m_s = B.dout("m_s", [2, 4, 4])
    cv_p = B.dout("cv_p", [2, 3, 2048])
    cv_s = B.dout("cv_s", [2, 4, 3, 2048])
    xres = B.dscr("xres", [T, 1024])
    xsres = B.dscr("xsres", [4, 1024])
    with ExitStack() as st:
        st.enter_context(nc.allow_non_contiguous_dma(reason="small transposed constant loads"))
        B.setup(st)
        for li in range(4):
            l = li // 2
            xsrc = xp if li == 0 else xres
            xssrc = xs if li == 0 else xsres
            if li % 2 == 0:
                w = {"norm_w": norm_w[li:li + 1, :], "w_in": nsa_w_in[l], "w_out": nsa_w_out[l], "pe": pe[l], "w1": w1[l],
                     "b1": b1[l], "w2": w2[l], "kv_out": kv_p[l], "win_out": win_p[l],
                     "samp": {"xs_src": xssrc, "xs_dst": xsres, "kv_out": kv_s[l], "win_out": win_s[l], "cwin": cwin[l],
                              "pool2": pools[l], "pt": pt}}
                B.nsa_layer(li, xsrc, xres, w)
            else:
                w = {"norm_w": norm_w[li:li + 1, :], "w_in": m_w_in[l], "w_out": m_w_out[l], "conv_w": m_conv_w[l],
                     "conv_b": m_conv_b[l], "w_qkv": m_w_qkv[l], "w_gate": m_w_gate[l], "b_gate": m_b_gate[l],
                     "mnorm_w": m_norm_w[l], "skip": m_skip[l], "C_out": C_p[l], "n_out": n_p[l], "m_out": m_p[l],
                     "conv_out": cv_p[l],
                     "samp": {"xs_src": xssrc, "xs_dst": xsres, "conv_in": sconv[l], "conv_out": cv_s[l], "C_in": sC[l],
                              "C_out": C_s[l], "n_in": sn[l], "n_out": n_s[l], "m_in": smm[l], "m_out": m_s[l]}}
                B.mlstm_layer(li, xsrc, xres, w)
        B.final_norm(xres, fnw, y_p, T, "p")
        B.final_norm(xsres, fnw, y_s, 4, "s")
        B.S.flush(final=True)
    _CACHE[key] = B
    return B


def make_in_maps(B, T, P, NPOOL, inp):
    f = np.float32
    cst = make_consts(T, P + 128, P)
    g = {k: np.asarray(v) for k, v in inp.items()}
    NB_ = g["x_prompt"].shape[0]
    shared = {"norm_w": g["norm_w"].astype(f), "final_norm_w": g["final_norm_w"].astype(f).reshape(1, 1024),
              "nsa_w_in": g["nsa_w_in"].astype(f), "nsa_w_out": g["nsa_w_out"].astype(f), "nsa_cmp_pe": g["nsa_cmp_pe"].astype(f),
              "nsa_cmp_w1": g["nsa_cmp_w1"].astype(f), "nsa_cmp_b1": g["nsa_cmp_b1"].astype(f), "nsa_cmp_w2": g["nsa_cmp_w2"].astype(f),
              "m_w_in": g["m_w_in"].astype(f), "m_conv_w": g["m_conv_w"].astype(f), "m_conv_b": g["m_conv_b"].astype(f),
              "m_w_qkv": g["m_w_qkv"].astype(f), "m_w_gate": g["m_w_gate"].astype(f), "m_b_gate": g["m_b_gate"].astype(f),
              "m_norm_w": g["m_norm_w"].astype(f), "m_skip": g["m_skip"].astype(f), "m_w_out": g["m_w_out"].astype(f)}
    ckv = g["cache_kv"].astype(f, copy=False)
    shared["pool0"] = ckv[0].reshape(NPOOL * 256, 512)
    shared["pool1"] = ckv[1].reshape(NPOOL * 256, 512)
    for k, v in cst.items():
        shared["c_" + k] = v
    in_maps = []
    for c in range(8):
        b = c % NB_
        ss_ = slice(4 * c, 4 * c + 4)
        m = dict(shared)
        m["xp"] = np.ascontiguousarray(g["x_prompt"][b].astype(f))
        m["xs"] = np.ascontiguousarray(g["x_sample"][ss_, 0].astype(f))
        m["cwin"] = np.ascontiguousarray(g["cache_win"][:, ss_].astype(f)).reshape(2, 4, 512, 512)
        m["sC"] = np.ascontiguousarray(g["state_C"][:, ss_].astype(f))
        m["sn"] = np.ascontiguousarray(g["state_n"][:, ss_].astype(f)).reshape(2, 4, 2048)
        m["smm"] = np.ascontiguousarray(g["state_m"][:, ss_].astype(f))
        m["sconv"] = np.ascontiguousarray(g["state_conv"][:, ss_].astype(f))
        m["pt"] = np.ascontiguousarray(g["page_table"][ss_].astype(np.int32))
        in_maps.append(m)
    return in_maps


def assemble(R, T, NB_=4):
    f = np.float32
    cat = lambda nm, ax: np.concatenate([R[c][nm] for c in range(8)], axis=ax)
    y_prompt = np.stack([R[b]["y_p"] for b in range(NB_)]).astype(f)
    y_sample = cat("y_s", 0).reshape(32, 1, 1024).astype(f)
    kv_prompt = np.stack([R[b]["kv_p"] for b in range(NB_)], 1).reshape(2, NB_, T, 4, 4, 64).astype(f)
    kv_sample = cat("kv_s", 1).reshape(2, 32, 1, 4, 4, 64).astype(f)
    WR = R[0]["win_p"].shape[1]
    win_prompt = np.stack([R[b]["win_p"] for b in range(NB_)], 1).reshape(2, NB_, WR, 2, 4, 64).astype(f)
    win_sample = cat("win_s", 1).reshape(2, 32, 512, 2, 4, 64).astype(f)
    C_p = np.stack([R[b]["C_p"] for b in range(NB_)], 1).astype(f)
    C_s = cat("C_s", 1).astype(f)
    n_p = np.stack([R[b]["n_p"] for b in range(NB_)], 1).reshape(2, NB_, 4, 512).astype(f)
    n_s = cat("n_s", 1).reshape(2, 32, 4, 512).astype(f)
    m_p = np.stack([R[b]["m_p"] for b in range(NB_)], 1).astype(f)
    m_s = cat("m_s", 1).astype(f)
    cv_p = np.stack([R[b]["cv_p"] for b in range(NB_)], 1).astype(f)
    cv_s = cat("cv_s", 1).astype(f)
    return (y_prompt, y_sample, kv_prompt, kv_sample, win_prompt, win_sample, C_p, C_s, n_p, n_s, m_p, m_s, cv_p, cv_s)


def kernel(**inputs):
    B = build_full()
    in_maps = make_in_maps(B, T_FULL, P_FULL, NPOOL_FULL, inputs)
    res = run_bass_kernel_spmd(B.nc, in_maps, core_ids=list(range(8)))
    return assemble(res.results, T_FULL)


def mlstm_layer(self, l, xsrc, xdst, w):
    nc, S, T, NT = self.nc, self.S, self.T, self.NT
    ps = self.ps
    SC = MHD ** -0.5
    poTs = self.dscr("poTs_l%d" % l, [NT, 128, 2048], BF16)
    hTs = self.dscr("mhTs_l%d" % l, [NT, 128, 1024], BF16)
    with ExitStack() as L:
        def sb(name, shape, dt):
            return L.enter_context(nc.sbuf_tensor("%s_m%d" % (name, l), shape, dt))
        nw_bc = sb("nw_bc", [128, 1024], F32)
        self.h_bf = sb("h_bf", [128, 1024], BF16)
        self.epsc = sb("epsc", [128, 1], F32)
        junk = sb("junk", [128, 1024], BF16)
        ss = sb("ss", [128, 4], F32)
        S.v("memset", self.epsc[:], RMS_EPS, w=["epsc"])
        S.d("sync", nw_bc[:], w["norm_w"].partition_broadcast(128), w=["nw_bc"])
        sm = None
        if "samp" in w:
            sm = {"xs": sb("s_xs", [4, 1024], F32), "hT": sb("s_hT", [128, 8, 4], BF16), "po": sb("s_po", [128, 16, 4], BF16),
                  "onec": sb("s_onec", [128, 1], F32)}
            S.v("memset", sm["onec"][:], 1.0, w=["s_onec"])
        with ExitStack() as P1:
            def s1(name, shape, dt):
                return P1.enter_context(nc.sbuf_tensor("%s_p%d" % (name, l), shape, dt))
            wx = s1("wx", [128, 8, 2048], BF16)
            wst = [s1("wst%d" % i, [128, 1024], F32) for i in range(2)]
            k = 0
            for c in range(8):
                for hf in range(2):
                    S.d("sync", wst[k % 2][:], w["w_in"][c * 128:(c + 1) * 128, hf * 1024:(hf + 1) * 1024], w=["mwst%d" % (k % 2)])
                    if k % 2 == 0:
                        S.v("tensor_copy", wx[:, c, hf * 1024:(hf + 1) * 1024], wst[k % 2][:], r=["mwst%d" % (k % 2)], w=["wx"])
                    else:
                        S.a("copy", wx[:, c, hf * 1024:(hf + 1) * 1024], wst[k % 2][:], r=["mwst%d" % (k % 2)], w=["wx"])
                    k += 1
            C = [s1("C%d" % h, [128, 4, 512], F32) for h in range(4)]
            ncol = s1("ncol", [128, 4, 4], F32)
            for h in range(4):
                S.v("memset", C[h][:], 0.0, w=["C%d" % h])
            S.v("memset", ncol[:], 0.0, w=["ncol"])
            cw = s1("cw", [128, 16, 4], F32)
            cb = s1("cb", [128, 16], F32)
            nwc = s1("nwc", [128, 16], F32)
            skc = s1("skc", [128, 16], F32)
            for j_ in range(4):
                S.d("sync", cw[:, :, j_], w["conv_w"][j_].rearrange("(c p) -> p c", p=128), w=["cw"])
            S.d("sync", cb[:], w["conv_b"].rearrange("(c p) -> p c", p=128), w=["cb"])
            S.d("sync", nwc[:], w["mnorm_w"].rearrange("(c p) -> p c", p=128), w=["nwc"])
            S.d("sync", skc[:], w["skip"].rearrange("(c p) -> p c", p=128), w=["skc"])
            wrow = s1("wrow", [128, 3, 16, 4], F32)
            S.d("sync", wrow[:], w["w_qkv"].rearrange("x (c n) j i -> (n j) x c i", c=16), w=["wrow"])
            Wblk = s1("Wblk", [128, 3, 16, 128], BF16)
            for x_ in range(3):
                for cc in range(16):
                    S.v("tensor_tensor", Wblk[:, x_, cc, :].rearrange("p (n i) -> p n i", i=4),
                        bcast(wrow[:, x_, cc, :], 1, 32), self.maskbd[:].rearrange("p (n i) -> p n i", i=4), ALU.mult,
                        r=["wrow", "maskbd"], w=["Wblk"])
            wgs = s1("wgs", [128, 48, 8], F32)
            Wg = s1("Wg", [128, 48, 8], BF16)
            S.d("sync", wgs[:], w["w_gate"].rearrange("(k p) g -> p k g", p=128), w=["wgs"])
            S.v("tensor_copy", Wg[:], wgs[:], r=["wgs"], w=["Wg"])
            bg = s1("bg", [4, 2], F32)
            S.d("sync", bg[:], w["b_gate"].rearrange("(x g) -> g x", x=2), w=["bg"])
            ones4 = s1("ones4", [4, 128], F32)
            S.v("memset", ones4[:], 1.0, w=["ones4"])
            onec = s1("onec", [128, 1], F32)
            S.v("memset", onec[:], 1.0, w=["onec"])
            Bprev = s1("Bprev", [4, 1], F32)
            uprev = s1("uprev", [4, 1], F32)
            upb = s1("upb", [128, 4], F32)
            S.v("memset", Bprev[:], 0.0, w=["Bprev"])
            S.v("memset", uprev[:], -1.0e30, w=["uprev"])
            S.v("memset", upb[:], -1.0e30, w=["upb"])
            P1a = ExitStack()
            s1_keep = s1

            def s1(name, shape, dt):
                return P1a.enter_context(nc.sbuf_tensor("%s_pa%d" % (name, l), shape, dt))
            xmT = s1("xmT", [128, 16, 131], F32)
            S.v("memset", xmT[:, :, 0:3], 0.0, w=["xmT"])
            xt = s1("xt", [128, 1024], F32)
            hT = s1("hT", [128, 8, 128], BF16)
            xmb = s1("xmb", [128, 16, 128], BF16)
            cv = s1("cv", [128, 16, 128], F32)
            cT = s1("cT", [128, 16, 128], BF16)
            csT = s1("csT", [128, 16, 128], BF16)
            qT = s1("qT", [128, 16, 128], BF16)
            kT = s1("kT", [128, 16, 128], BF16)
            vTc = s1("vTc", [128, 4, 128], BF16)
            ktok = s1("ktok", [128, 2048], BF16)
            vtok = s1("vtok", [128, 2048], BF16)
            gsb = s1("gsb", [4, 8, 128], F32)
            ngu = s1("ngu", [4, 128], F32)
            acol = s1("acol", [128, 8], F32)
            Eb = s1("Eb", [128, 4, 128], F32)
            GT = s1("GT", [128, 4, 128], F32)
            dbc = s1("dbc", [128, 4, 128], F32)
            StT = s1("StT", [128, 4, 128], BF16)
            qdT = [s1("qdT%d" % i, [128, 4, 128], F32) for i in range(2)]
            dn = s1("dn", [128, 16], F32)
            hn = s1("hn", [128, 512], BF16)
            poT = s1("poT", [128, 16, 128], BF16)
            kw = s1("kw", [128, 512], BF16)
            for tt in range(NT):
                S.d("sync", xt[:], xsrc[tt * 128:(tt + 1) * 128, :], r=["xres_%d" % tt], w=["Mx"])
                self.front(xt, nw_bc, hT, junk, ss, "M")
                S.d("gpsimd", hTs[tt], hT[:].rearrange("p c n -> p (c n)"), r=["MhT"], w=["mhTs%d" % tt])
                for g4 in range(4):
                    bank = 1 + g4 % 2
                    for j in range(4):
                        cc = g4 * 4 + j
                        for c in range(8):
                            S.p("matmul", ps[bank][:, j * 128:(j + 1) * 128], wx[:, c, cc * 128:(cc + 1) * 128], hT[:, c, :],
                                start=(c == 0), stop=(c == 7), r=["wx", "MhT"], w=["ps%d" % bank])
                    S.v("tensor_copy", xmT[:, g4 * 4:(g4 + 1) * 4, 3:131], ps[bank][:].rearrange("p (j n) -> p j n", j=4),
                        r=["ps%d" % bank], w=["xmT"])
                S.v("tensor_copy", xmb[:], xmT[:, :, 3:131], r=["xmT"], w=["xmb"])
                S.v("tensor_tensor", cv[:], xmT[:, :, 0:128], bcast(cw[:, :, 0], 2, 128), ALU.mult, r=["xmT", "cw"], w=["cv"])
                for j in range(1, 4):
                    S.v("tensor_tensor", cT[:], xmT[:, :, j:j + 128], bcast(cw[:, :, j], 2, 128), ALU.mult, r=["xmT", "cw"], w=["cT"])
                    S.v("tensor_tensor", cv[:], cv[:], cT[:], ALU.add, r=["cv", "cT"], w=["cv"])
                S.v("tensor_tensor", cv[:], cv[:], bcast(cb[:], 2, 128), ALU.add, r=["cv", "cb"], w=["cv"])
                if tt == NT - 1:
                    for j_ in range(3):
                        S.d("gpsimd", w["conv_out"][j_].rearrange("(c p) -> p c", p=128), xmT[:, :, 128 + j_], r=["xmT"], w=["convout%d_%d" % (l, j_)])
                S.v("tensor_copy", xmT[:, :, 0:3], xmT[:, :, 128:131], r=["xmT"], w=["xmT"])
                S.a("activation", cT[:], cv[:], AF.Silu, r=["cv"], w=["cT"])
                S.v("tensor_tensor", csT[:], cT[:], bcast(skc[:], 2, 128), ALU.mult, r=["cT", "skc"], w=["csT"])
                for (x_, dst, nm) in ((0, qT, "qT"), (1, kT, "kT")):
                    for g4 in range(4):
                        bank = 1 + g4 % 2
                        for j in range(4):
                            cc = g4 * 4 + j
                            S.p("matmul", ps[bank][:, j * 128:(j + 1) * 128], Wblk[:, x_, cc, :], cT[:, cc, :], start=True, stop=True,
                                r=["Wblk", "cT"], w=["ps%d" % bank])
                        S.v("tensor_copy", dst[:, g4 * 4:(g4 + 1) * 4, :], ps[bank][:].rearrange("p (j n) -> p j n", j=4),
                            r=["ps%d" % bank], w=[nm])
                for (x_, src, dst, nm) in ((1, cT, ktok, "ktok"), (2, xmb, vtok, "vtok")):
                    for g4 in range(4):
                        bank = 1 + g4 % 2
                        for j in range(4):
                            cc = g4 * 4 + j
                            S.p("matmul", ps[bank][:, j * 128:(j + 1) * 128], src[:, cc, :], Wblk[:, x_, cc, :], start=True, stop=True,
                                r=["Wblk", "cT", "xmb"], w=["ps%d" % bank])
                        S.v("tensor_copy", dst[:, g4 * 512:(g4 + 1) * 512], ps[bank][:], r=["ps%d" % bank], w=[nm])
                for gi in range(2):
                    first = True
                    for (x_, src, nm) in ((0, qT, "qT"), (1, kT, "kT")):
                        for cc in range(16):
                            S.p("matmul", ps[3][0:4, gi * 128:(gi + 1) * 128], Wg[:, x_ * 16 + cc, gi * 4:(gi + 1) * 4], src[:, cc, :],
                                start=(first and gi == 0), stop=False, skip_group_check=True, r=["Wg", nm], w=["ps3"])
                            first = False
                for g4 in range(4):
                    bank = 1 + g4 % 2
                    for j in range(4):
                        cc = g4 * 4 + j
                        S.p("matmul", ps[bank][:, j * 128:(j + 1) * 128], Wblk[:, 2, cc, :], xmb[:, cc, :], start=True, stop=True,
                            r=["Wblk", "xmb"], w=["ps%d" % bank])
                    S.v("tensor_copy", vTc[:], ps[bank][:].rearrange("p (j n) -> p j n", j=4), r=["ps%d" % bank], w=["vTc"])
                    for gi in range(2):
                        for j in range(4):
                            cc = g4 * 4 + j
                            S.p("matmul", ps[3][0:4, gi * 128:(gi + 1) * 128], Wg[:, 32 + cc, gi * 4:(gi + 1) * 4], vTc[:, j, :],
                                start=False, stop=(g4 == 3 and j == 3), skip_group_check=True, r=["Wg", "vTc"], w=["ps3"])
                S.a("activation", gsb[:, 0, :], ps[3][0:4, 0:128], AF.Identity, bias=bg[:, 0:1], r=["ps3", "bg"], w=["gsb"])
                S.a("activation", gsb[:, 1, :], ps[3][0:4, 128:256], AF.Identity, bias=bg[:, 1:2], r=["ps3", "bg"], w=["gsb"])
                S.v("tensor_scalar", gsb[:, 2, :], gsb[:, 1, :], -1.0, None, ALU.mult, r=["gsb"], w=["gsb"])
                S.v("tensor_tensor", gsb[:, 2, :], gsb[:, 2, :], gsb[:, 1, :], ALU.max, r=["gsb"], w=["gsb"])
                S.a("activation", gsb[:, 2, :], gsb[:, 2, :], AF.Exp, scale=-1.0, r=["gsb"], w=["gsb"])
                S.a("activation", gsb[:, 2, :], gsb[:, 2, :], AF.Ln, bias=onec[0:4, 0:1], r=["gsb", "onec"], w=["gsb"])
                S.v("tensor_scalar", gsb[:, 3, :], gsb[:, 1, :], 0.0, None, ALU.min, r=["gsb"], w=["gsb"])
                S.v("tensor_tensor", gsb[:, 3, :], gsb[:, 3, :], gsb[:, 2, :], ALU.subtract, r=["gsb"], w=["gsb"])
                S.v("tensor_tensor_scan", gsb[:, 4, :], ones4[:], gsb[:, 3, :], Bprev[:, 0:1], ALU.mult, ALU.add,
                    r=["gsb", "ones4", "Bprev"], w=["gsb"])
                S.v("tensor_tensor", gsb[:, 5, :], gsb[:, 0, :], gsb[:, 4, :], ALU.subtract, r=["gsb"], w=["gsb"])
                S.v("tensor_tensor_scan", gsb[:, 6, :], ones4[:], gsb[:, 5, :], uprev[:, 0:1], ALU.mult, ALU.max,
                    r=["gsb", "ones4", "uprev"], w=["gsb"])
                S.v("tensor_tensor", gsb[:, 7, :], gsb[:, 4, :], gsb[:, 6, :], ALU.add, r=["gsb"], w=["gsb"])
                S.v("tensor_scalar", ngu[:], gsb[:, 6, :], -1.0, None, ALU.mult, r=["gsb"], w=["ngu"])
                S.v("tensor_copy", Bprev[:], gsb[:, 4, 127:128], r=["gsb"], w=["Bprev"])
                S.v("tensor_copy", uprev[:], gsb[:, 6, 127:128], r=["gsb"], w=["uprev"])
                if tt == NT - 1:
                    S.d("gpsimd", w["m_out"].rearrange("(h o) -> h o", o=1), gsb[:, 7, 127:128], r=["gsb"], w=["mout%d" % l])
                for h in range(4):
                    S.p("matmul", ps[4][:, h * 128:(h + 1) * 128], self.selh[:, h, :], ngu[:], start=True, stop=True,
                        r=["selh", "ngu"], w=["ps4"])
                S.p("transpose", ps[3][:, 256:260], gsb[:, 5, :], self.ident[0:4, 0:4], r=["gsb", "ident"], w=["ps3"])
                S.p("transpose", ps[3][:, 260:264], gsb[:, 7, :], self.ident[0:4, 0:4], r=["gsb", "ident"], w=["ps3"])
                S.v("tensor_copy", acol[:, 0:4], ps[3][:, 256:260], r=["ps3"], w=["acol"])
                S.a("activation", acol[:, 4:8], ps[3][:, 260:264], AF.Exp, scale=-1.0, r=["ps3"], w=["acol"])
                S.v("tensor_tensor", Eb[:], ps[4][:].rearrange("p (h n) -> p h n", h=4), bcast(self.trib[:], 1, 4), ALU.add,
                    r=["ps4", "trib"], w=["Eb"])
                for h in range(4):
                    S.a("activation", GT[:, h, :], Eb[:, h, :], AF.Exp, bias=acol[:, h:h + 1], r=["Eb", "acol"], w=["GT"])
                    S.a("activation", dbc[:, h, :], ps[4][:, h * 128:(h + 1) * 128], AF.Exp, bias=upb[:, h:h + 1], r=["ps4", "upb"], w=["dbc"])
                S.v("tensor_scalar", upb[:], ps[4][:].rearrange("p (h n) -> p h n", h=4)[:, :, 127], -1.0, None, ALU.mult,
                    r=["ps4", "dbc"], w=["upb"])
                for h in range(4):
                    for dc in range(4):
                        S.p("matmul", ps[5][:, h * 128:(h + 1) * 128], kT[:, 4 * h + dc, :], qT[:, 4 * h + dc, :],
                            start=(dc == 0), stop=(dc == 3), r=["kT", "qT"], w=["ps5"])
                S.v("scalar_tensor_tensor", StT[:].rearrange("p h n -> p (h n)"), ps[5][:], SC, GT[:].rearrange("p h n -> p (h n)"),
                    ALU.mult, ALU.mult, r=["ps5", "GT"], w=["StT"])
                for h in range(4):
                    qd = qdT[h % 2]
                    qn = "qdT%d" % (h % 2)
                    bank = 6 + h % 2
                    S.v("tensor_tensor", qd[:], qT[:, 4 * h:4 * h + 4, :], bcast(dbc[:, h, :], 1, 4), ALU.mult, r=["qT", "dbc"], w=[qn])
                    S.p("matmul", ps[bank][:, :], StT[:, h, :], vtok[:, h * 512:(h + 1) * 512], start=True, stop=False,
                        r=["StT", "vtok"], w=["ps%d" % bank])
                    for dc in range(4):
                        S.p("matmul", ps[bank][:, :], qd[:, dc, :], C[h][:, dc, :], start=False, stop=(dc == 3),
                            r=[qn, "C%d" % h], w=["ps%d" % bank])
                    S.p("matmul", ps[3][:, 264 + h:265 + h], StT[:, h, :], self.onesb[:, 0:1], start=True, stop=False,
                        r=["StT", "onesb"], w=["ps3"])
                    for dc in range(4):
                        S.p("matmul", ps[3][:, 264 + h:265 + h], qd[:, dc, :], ncol[:, h, dc:dc + 1], start=False, stop=(dc == 3),
                            r=[qn, "ncol"], w=["ps3"])
                    S.v("tensor_scalar", dn[:, 7:8], ps[3][:, 264 + h:265 + h], -1.0, None, ALU.mult, r=["ps3"], w=["dn"])
                    S.v("tensor_tensor", dn[:, 0:1], dn[:, 7:8], ps[3][:, 264 + h:265 + h], ALU.max, r=["ps3", "dn"], w=["dn"])
                    S.v("tensor_tensor", dn[:, 0:1], dn[:, 0:1], acol[:, 4 + h:5 + h], ALU.max, r=["dn", "acol"], w=["dn"])
                    S.v("reciprocal", dn[:, 1:2], dn[:, 0:1], r=["dn"], w=["dn"])
                    S.a("activation", junk[:, 0:512], ps[bank][:, :], AF.Square, accum_out=dn[:, 2:3], r=["ps%d" % bank], w=["junk", "dn"])
                    S.v("tensor_tensor", dn[:, 3:4], dn[:, 1:2], dn[:, 1:2], ALU.mult, r=["dn"], w=["dn"])
                    S.v("tensor_tensor", dn[:, 3:4], dn[:, 3:4], dn[:, 2:3], ALU.mult, r=["dn"], w=["dn"])
                    S.a("activation", dn[:, 4:5], dn[:, 3:4], AF.Sqrt, scale=1.0 / MHD, bias=self.epsc[:, 0:1], r=["dn", "epsc"], w=["dn"])
                    S.v("reciprocal", dn[:, 5:6], dn[:, 4:5], r=["dn"], w=["dn"])
                    S.v("tensor_tensor", dn[:, 6:7], dn[:, 5:6], dn[:, 1:2], ALU.mult, r=["dn"], w=["dn"])
                    S.v("tensor_scalar", hn[:], ps[bank][:, :], dn[:, 6:7], None, ALU.mult, r=["ps%d" % bank, "dn"], w=["hn"])
                    pb = self.psb(0)
                    for ec in range(4):
                        S.p("transpose", pb[:, ec * 128:(ec + 1) * 128], hn[:, ec * 128:(ec + 1) * 128], self.identb[:],
                            r=["hn", "identb"], w=["ps0"])
                    S.v("tensor_tensor", poT[:, 4 * h:4 * h + 4, :], pb[:, 0:512].rearrange("p (j n) -> p j n", j=4),
                        bcast(nwc[:, 4 * h:4 * h + 4], 2, 128), ALU.mult, r=["ps0", "nwc"], w=["poT"])
                    S.v("tensor_tensor", poT[:, 4 * h:4 * h + 4, :], poT[:, 4 * h:4 * h + 4, :], csT[:, 4 * h:4 * h + 4, :], ALU.add,
                        r=["poT", "csT"], w=["poT"])
                    S.v("tensor_scalar", kw[:], ktok[:, h * 512:(h + 1) * 512], GT[:, h, 127:128], SC, ALU.mult, ALU.mult,
                        r=["ktok", "GT"], w=["kw"])
                    for dc in range(4):
                        cbk = 1 + dc % 2
                        S.p("matmul", ps[cbk][:, :], kw[:, dc * 128:(dc + 1) * 128], vtok[:, h * 512:(h + 1) * 512], start=True, stop=True,
                            r=["kw", "vtok"], w=["ps%d" % cbk])
                        S.v("scalar_tensor_tensor", C[h][:, dc, :], C[h][:, dc, :], dbc[:, h, 127:128], ps[cbk][:, :], ALU.mult, ALU.add,
                            r=["C%d" % h, "dbc", "ps%d" % cbk], w=["C%d" % h])
                        S.p("matmul", ps[3][:, 272 + dc:273 + dc], kw[:, dc * 128:(dc + 1) * 128], self.onesb[:, 0:1], start=True, stop=True,
                            r=["kw", "onesb"], w=["ps3"])
                    S.v("scalar_tensor_tensor", ncol[:, h, :], ncol[:, h, :], dbc[:, h, 127:128], ps[3][:, 272:276], ALU.mult, ALU.add,
                        r=["ncol", "dbc", "ps3"], w=["ncol"])
                S.d("gpsimd", poTs[tt], poT[:].rearrange("p c n -> p (c n)"), r=["poT"], w=["poTs%d" % tt])
            S.flush()
            P1a.close()
            s1 = s1_keep
            if sm is not None:
                self.mlstm_samp_1(l, sm, w, s1, wx, Wblk, Wg, cw, cb, nwc, skc, nw_bc, junk, ss)
            for h in range(4):
                S.d("gpsimd", w["C_out"][h].rearrange("(c p) e -> p c e", p=128), C[h][:], r=["C%d" % h], w=["Cout%d_%d" % (l, h)])
            S.d("gpsimd", w["n_out"].rearrange("(h c p) -> p h c", h=4, p=128), ncol[:], r=["ncol"], w=["nout%d" % l])
            S.flush()
        with ExitStack() as P2:
            def s2(name, shape, dt):
                return P2.enter_context(nc.sbuf_tensor("%s_q%d" % (name, l), shape, dt))
            wz = s2("wz", [128, 8, 2048], BF16)
            wo = s2("wo", [128, 16, 1024], BF16)
            wst = [s2("wst%d" % i, [128, 1024], F32) for i in range(2)]
            k = 0
            for c in range(8):
                for hf in range(2):
                    S.d("sync", wst[k % 2][:], w["w_in"][c * 128:(c + 1) * 128, 2048 + hf * 1024:2048 + (hf + 1) * 1024], w=["zwst%d" % (k % 2)])
                    if k % 2 == 0:
                        S.v("tensor_copy", wz[:, c, hf * 1024:(hf + 1) * 1024], wst[k % 2][:], r=["zwst%d" % (k % 2)], w=["wz"])
                    else:
                        S.a("copy", wz[:, c, hf * 1024:(hf + 1) * 1024], wst[k % 2][:], r=["zwst%d" % (k % 2)], w=["wz"])
                    k += 1
            for c in range(16):
                S.d("sync", wst[k % 2][:], w["w_out"][c * 128:(c + 1) * 128, :], w=["zwst%d" % (k % 2)])
                if k % 2 == 0:
                    S.v("tensor_copy", wo[:, c, :], wst[k % 2][:], r=["zwst%d" % (k % 2)], w=["wo"])
                else:
                    S.a("copy", wo[:, c, :], wst[k % 2][:], r=["zwst%d" % (k % 2)], w=["wo"])
                k += 1
            hT = [s2("hT%d" % i, [128, 8, 128], BF16) for i in range(2)]
            po = [s2("po%d" % i, [128, 16, 128], BF16) for i in range(2)]
            xt = [s2("xt%d" % i, [128, 1024], F32) for i in range(2)]
            szT = s2("szT", [128, 16, 128], BF16)
            oT = s2("oT", [128, 16, 128], BF16)
            xo = s2("xo", [128, 1024], F32)
            for tt in range(NT):
                b = tt % 2
                S.d("sync", hT[b][:].rearrange("p c n -> p (c n)"), hTs[tt], r=["mhTs%d" % tt], w=["ZhT%d" % b])
                S.d("sync", po[b][:].rearrange("p c n -> p (c n)"), poTs[tt], r=["poTs%d" % tt], w=["Zpo%d" % b])
                S.d("sync", xt[b][:], xsrc[tt * 128:(tt + 1) * 128, :], r=["xres_%d" % tt], w=["Zx%d" % b])
                for g4 in range(4):
                    bank = 1 + g4 % 2
                    for j in range(4):
                        cc = g4 * 4 + j
                        for c in range(8):
                            S.p("matmul", ps[bank][:, j * 128:(j + 1) * 128], wz[:, c, cc * 128:(cc + 1) * 128], hT[b][:, c, :],
                                start=(c == 0), stop=(c == 7), r=["wz", "ZhT%d" % b], w=["ps%d" % bank])
                    S.a("activation", szT[:, g4 * 4:(g4 + 1) * 4, :], ps[bank][:].rearrange("p (j n) -> p j n", j=4), AF.Silu,
                        r=["ps%d" % bank], w=["szT"])
                S.v("tensor_tensor", oT[:], po[b][:], szT[:], ALU.mult, r=["Zpo%d" % b, "szT"], w=["oT"])
                for half in range(2):
                    bank = 6 + half
                    for cc in range(16):
                        S.p("matmul", ps[bank][:, :], oT[:, cc, :], wo[:, cc, half * 512:(half + 1) * 512], start=(cc == 0), stop=(cc == 15),
                            r=["oT", "wo"], w=["ps%d" % bank])
                    S.v("tensor_tensor", xo[:, half * 512:(half + 1) * 512], ps[bank][:, :], xt[b][:, half * 512:(half + 1) * 512], ALU.add,
                        r=["ps%d" % bank, "Zx%d" % b], w=["xo"])
                S.d("gpsimd", xdst[tt * 128:(tt + 1) * 128, :], xo[:], r=["xo"], w=["xres_%d" % tt])
            if sm is not None:
                self.mlstm_samp_2(l, sm, w, s2, wz, wo)
            S.flush()


Builder.mlstm_layer = mlstm_layer


def nsa_samp_alloc(self, l, sb, w):
    S, P = self.S, self.P
    sm = {}
    sm["hT"] = sb("s_hT", [128, 8, 4], BF16)
    sm["KnewT"] = sb("s_KnewT", [102, 2, 4, 4], BF16)
    sm["Vnew"] = sb("s_Vnew", [4, 2, 4, 65], BF16)
    sm["gates"] = sb("s_gates", [4, 48], F32)
    sm["sz"] = sb("s_sz", [4, 1024], BF16)
    sm["QTs"] = sb("s_QTs", [102, 4, 4, 4], BF16)
    sm["xs"] = sb("s_xs", [4, 1024], F32)
    sm["hb"] = sb("s_hb", [128, 2], F32)
    sm["kvts"] = sb("s_kvts", [4, 1024], F32)
    sm["wnts"] = sb("s_wnts", [4, 512], F32)
    S.v("memset", sm["KnewT"][64:96, :, :, :], 0.0, w=["s_KnewT"])
    S.v("memset", sm["KnewT"][64:65, :, :, :], 1.0, w=["s_KnewT"])
    S.v("memset", sm["QTs"][64:96, :, :, :], 0.0, w=["s_QTs"])
    S.v("memset", sm["Vnew"][:, :, :, 64:65], 1.0, w=["s_Vnew"])
    for br in range(2):
        for kk in range(4):
            S.d("sync", sm["KnewT"][96:102, br, kk, :], self.cin["posP"], w=["s_KnewT"])
    for kk in range(4):
        for s in range(4):
            S.d("sync", sm["QTs"][96:102, kk, s, :], self.cin["slope4"][kk], w=["s_QTs"])
    sm["gsc"] = self.dscr("gsc_l%d" % l, [4, 48], F32)
    sm["oscr"] = self.dscr("oscr_l%d" % l, [4, 1024], F32)
    return sm


def nsa_samp_A(self, l, sm, w, wkv, nw_bc, junk, ss, OKV, OWIN, OG, OZ):
    S, ps = self.S, self.ps
    sp = w["samp"]
    S.d("sync", sm["xs"][:], sp["xs_src"], r=["xsres"], w=["Sx"])
    self.front(sm["xs"], nw_bc, sm["hT"], junk, ss, "S", nrows=4)
    hT = sm["hT"]
    for bi, off in ((0, OKV + 512), (1, OWIN)):
        bank = 1 + bi
        for hh in range(4):
            for c in range(8):
                S.p("matmul", ps[bank][0:64, hh * 4:(hh + 1) * 4], wkv[:, c, off + hh * 64:off + (hh + 1) * 64], hT[:, c, :],
                    start=(c == 0), stop=(c == 7), r=["wkv", "ShT"], w=["ps%d" % bank])
        S.v("tensor_copy", sm["KnewT"][0:64, bi, :, :], ps[bank][0:64, 0:16].rearrange("p (h s) -> p h s", h=4),
            r=["ps%d" % bank], w=["s_KnewT"])
    tm = [(OKV, 512, 5), (OKV + 512, 512, 6), (OWIN, 512, 7), (OG, 48, 5), (OZ, 512, 6), (OZ + 512, 512, 7)]
    for ti, (off, ncol, bank) in enumerate(tm):
        pk = "ps%d" % bank
        for c in range(8):
            S.p("matmul", ps[bank][0:4, 0:ncol], hT[:, c, :], wkv[:, c, off:off + ncol], start=(c == 0), stop=(c == 7),
                r=["wkv", "ShT"], w=[pk])
        if ti == 0:
            S.v("tensor_copy", sm["kvts"][:, 0:512], ps[bank][0:4, 0:512], r=[pk], w=["s_kvts"])
        elif ti == 1:
            S.v("tensor_copy", sm["kvts"][:, 512:1024], ps[bank][0:4, 0:512], r=[pk], w=["s_kvts"])
            S.v("tensor_copy", sm["Vnew"][:, 0, :, 0:64], ps[bank][0:4, 256:512].rearrange("p (h d) -> p h d", h=4), r=[pk], w=["s_Vnew"])
            S.d("gpsimd", sp["kv_out"], sm["kvts"][:], r=["s_kvts"], w=["s_kvout%d" % l])
        elif ti == 2:
            S.v("tensor_copy", sm["wnts"][:], ps[bank][0:4, 0:512], r=[pk], w=["s_wnts"])
            S.v("tensor_copy", sm["Vnew"][:, 1, :, 0:64], ps[bank][0:4, 256:512].rearrange("p (h d) -> p h d", h=4), r=[pk], w=["s_Vnew"])
            WR = sp["win_out"].shape[1]
            S.d("gpsimd", sp["win_out"][:, WR - 1, :], sm["wnts"][:], r=["s_wnts"], w=["s_winout%d" % l])
            S.d("gpsimd", sp["win_out"][:, 0:WR - 1, :], sp["cwin"][:, 1:WR, :], w=["s_winsh%d" % l])
        elif ti == 3:
            S.a("activation", sm["gates"][:], ps[bank][0:4, 0:48], AF.Sigmoid, r=[pk], w=["s_gates"])
            S.d("gpsimd", sm["gsc"], sm["gates"][:], r=["s_gates"], w=["s_gsc"])
        elif ti == 4:
            S.a("activation", sm["sz"][:, 0:512], ps[bank][0:4, 0:512], AF.Silu, r=[pk], w=["s_sz"])
        else:
            S.a("activation", sm["sz"][:, 512:1024], ps[bank][0:4, 0:512], AF.Silu, r=[pk], w=["s_sz"])


def nsa_samp_Q(self, sm, wq):
    S, ps = self.S, self.ps
    for kk in range(4):
        bank = 1 + kk % 2
        for g_ in range(4):
            hd = kk * 4 + g_
            for c in range(8):
                S.p("matmul", ps[bank][0:64, g_ * 4:(g_ + 1) * 4], wq[:, c, hd * 64:(hd + 1) * 64], sm["hT"][:, c, :],
                    start=(c == 0), stop=(c == 7), r=["wq", "ShT"], w=["ps%d" % bank])
        S.a("mul", sm["QTs"][0:64, kk, :, :].rearrange("p s g -> p g s"), ps[bank][0:64, 0:16].rearrange("p (g s) -> p g s", g=4),
            HD ** -0.5, r=["ps%d" % bank], w=["s_QTs"])


def nsa_samp_S(self, l, sm, w):
    nc, S, ps, P = self.nc, self.S, self.ps, self.P
    sp = w["samp"]
    NPG = P // 128
    TSP = P + 32
    NCs = P // 16
    NJ = max(1, NCs // 128)
    NCp = NJ * 128 if NCs >= 128 else 128
    NBs = P // 64 + 1
    NBp = ((NBs + 3) // 4) * 4
    NE = P // 64
    with ExitStack() as Sx:
        def sb(name, shape, dt):
            return Sx.enter_context(nc.sbuf_tensor("%s_S%d" % (name, l), shape, dt))
        wo = sb("wo", [128, 8, 1024], BF16)
        wst = sb("wst0", [128, 1024], F32)
        for c in range(8):
            S.d("sync", wst[:], w["w_out"][c * 128:(c + 1) * 128, :], w=["Swst0"])
            if c % 2 == 0:
                S.v("tensor_copy", wo[:, c, :], wst[:], r=["Swst0"], w=["Swo"])
            else:
                S.a("copy", wo[:, c, :], wst[:], r=["Swst0"], w=["Swo"])
        w1s = sb("w1s", [128, 16, 128], F32)
        w1b = [sb("w1b%d" % c, [128, 32, 128], BF16) for c in range(2)]
        w2s = sb("w2s", [128, 2, 64], F32)
        w2b = sb("w2b", [128, 2, 64], BF16)
        for c in range(2):
            for rh in range(2):
                for half in range(2):
                    S.d("sync", w1s[half * 64:(half + 1) * 64, :, :], w["w1"][c][rh * 16:(rh + 1) * 16].rearrange("r d e -> d r e"), w=["Sw1s"])
                S.v("tensor_copy", w1b[c][:, rh * 16:(rh + 1) * 16, :], w1s[:], r=["Sw1s"], w=["Sw1b%d" % c])
        S.d("sync", w2s[:], w["w2"].rearrange("c e d -> e c d"), w=["Sw2s"])
        S.v("tensor_copy", w2b[:], w2s[:], r=["Sw2s"], w=["Sw2b"])
        KVc = sb("KVc", [128, 4, TSP], BF16)
        S.v("memset", KVc[:, :, P:TSP], 0.0, w=["KVc"])
        ShT = sb("ShT", [128, NCp], BF16)
        S.v("memset", ShT[:], 0.0, w=["SShT"])
        KcTs = sb("KcTs", [102, 4, NCp], BF16)
        S.v("memset", KcTs[0:96, :, :], 0.0, w=["KcTs"])
        for kk in range(4):
            S.d("sync", KcTs[96:102, kk, :], self.cin["cend"][:, 0:NCp], w=["KcTs"])
        vcs = sb("vcs", [128, NJ, 4, 64], BF16)
        S.v("memset", vcs[:], 0.0, w=["vcs"])
        EallS = sb("EallS", [NE, P], BF16)
        MovS = sb("MovS", [128, NJ, NBp], BF16)
        S.d("sync", EallS[:], self.cin["EallS"], w=["EallS"])
        S.d("sync", MovS[:], self.cin["MovS"], w=["MovS"])
        ownm = sb("ownm", [4, 4, 4], BF16)
        S.d("sync", ownm[:], self.cin["ownm"], w=["ownm"])
        maskcs = sb("maskcs", [4, NCp], F32)
        S.d("sync", maskcs[:], self.cin["maskcs"], w=["maskcs"])
        pcol2 = sb("pcol2", [128, 2], F32)
        S.d("sync", pcol2[:], self.cin["pcol2"], w=["pcol2"])
        onec4 = sb("onec4", [4, 1], F32)
        S.v("memset", onec4[:], 1.0, w=["onec4"])
        gT = sb("gT", [4, 4, 4, 3], F32)
        S.d("sync", gT[:], sm["gsc"].rearrange("s (k g t) -> g s k t", k=4, g=4), r=["s_gsc"], w=["gT"])
        NPB = 4
        pgt = [sb("pgt%d" % i, [128, 512], F32) for i in range(NPB)]
        Kch = [sb("Kch%d" % i, [102, 4, 128], BF16) for i in range(2)]
        Vch = [sb("Vch%d" % i, [128, 4, 65], BF16) for i in range(2)]
        for i in range(2):
            S.v("memset", Kch[i][64:96, :, :], 0.0, w=["Kch%d" % i])
            S.v("memset", Kch[i][64:65, :, :], 1.0, w=["Kch%d" % i])
            S.v("memset", Vch[i][:, :, 64:65], 1.0, w=["Vch%d" % i])
        PTs = [sb("PTs%d" % i, [128, 16], BF16) for i in range(2)]
        ptb = sb("ptb", [128, NPG], I32)
        idxc = sb("idxc", [128, NPG], I32)
        idxs = sb("idxs", [128, NPG], I32)
        ksum = sb("ksum", [102, 1], F32)
        prod = sb("prod", [102, 4], BF16)
        Sm = sb("Sm", [4, NCp], F32)
        mx = sb("mx", [4, 8], F32)
        Pn = sb("Pn", [4, NCp], BF16)
        PnT = sb("PnT", [128, NJ, 4], BF16)
        impg = sb("impg", [4, NBp], F32)
        scs = sb("scs", [1, NBp], F32)
        sc2 = sb("sc2", [1, NBp], F32)
        m8 = sb("m8", [1, 16], F32)
        bbs = sb("bbs", [1, 128], BF16)
        bbT = sb("bbT", [128, 4, 4], BF16)
        S.v("memset", bbs[:], 0.0, w=["bbs"])
        S.v("memset", bbT[:], 0.0, w=["SbbT"])
        acc = sb("acc", [4, 4, 64], F32)
        gr = sb("gr", [4, 8], F32)
        oall = sb("oall", [4, 1024], F32)
        ozs = sb("ozs", [4, 1024], BF16)
        ozT = sb("ozT", [128, 8, 4], BF16)
        xo = sb("xo", [4, 1024], F32)
        pool2 = sp["pool2"]
        pti = 0
        for s in range(4):
            S.d("sync", ptb[:], sp["pt"][s:s + 1, :].partition_broadcast(128), w=["ptb"])
            S.v("tensor_scalar", idxc[:], ptb[:], 256.0, pcol2[:, 0:1], ALU.mult, ALU.add, r=["ptb", "pcol2"], w=["idxc"])
            S.v("tensor_scalar", idxs[:], ptb[:], 256.0, pcol2[:, 1:2], ALU.mult, ALU.add, r=["ptb", "pcol2"], w=["idxs"])
            for pg in range(NPG):
                t_ = pgt[pg % NPB]
                tn = "pgt%d" % (pg % NPB)
                bank = 1 + pg % 2
                S.dm("gpsimd", "indirect_dma_start", (t_[:], None, pool2, bass.IndirectOffsetOnAxis(ap=idxc[:, pg:pg + 1], axis=0)), {},
                     r=["idxc"], w=[tn])
                for q4 in range(4):
                    S.p("transpose", ps[bank][:, q4 * 128:(q4 + 1) * 128], t_[:, q4 * 128:(q4 + 1) * 128], self.ident[:],
                        r=[tn, "ident"], w=["ps%d" % bank])
                eng = S.v if pg % 2 == 0 else S.a
                eng("tensor_copy" if pg % 2 == 0 else "copy", KVc[:, :, pg * 128:(pg + 1) * 128],
                    ps[bank][:].rearrange("p (q n) -> p q n", q=4), r=["ps%d" % bank], w=["KVc"])
            for c in range(2):
                for hh in range(4):
                    pr, hl = hh // 2, hh % 2
                    bank = 1 + ((c * 4 + hh) % 2)
                    pk = "ps%d" % bank
                    for r_ in range(32):
                        S.p("matmul", ps[bank][:, 0:NCs], w1b[c][hl * 64:(hl + 1) * 64, r_, :],
                            KVc[hl * 64:(hl + 1) * 64, c * 2 + pr, r_:r_ + 16 * NCs:16],
                            start=(r_ == 0), stop=(r_ == 31), r=["Sw1b%d" % c, "KVc"], w=[pk])
                    S.a("activation", ShT[:, 0:NCs], ps[bank][:, 0:NCs], AF.Silu, bias=sm["hb"][:, c:c + 1], r=[pk, "s_hb"], w=["SShT"])
                    if c == 0:
                        S.p("matmul", ps[4][0:64, 0:NCp], w2b[:, 0, :], ShT[:], start=True, stop=True, r=["Sw2b", "SShT"], w=["ps4"])
                        S.v("tensor_copy", KcTs[0:64, hh, :], ps[4][0:64, 0:NCp], r=["ps4"], w=["KcTs"])
                    else:
                        for j in range(NJ):
                            S.p("matmul", ps[3][:, j * 64:(j + 1) * 64], ShT[:, j * 128:(j + 1) * 128], w2b[:, 1, :],
                                start=True, stop=True, r=["Sw2b", "SShT"], w=["ps3"])
                        S.v("tensor_copy", vcs[:, :, hh, :], ps[3][:, 0:NJ * 64].rearrange("p (j d) -> p j d", j=NJ), r=["ps3"], w=["vcs"])
            for kk in range(4):
                Q = sm["QTs"][:, kk, s, :]
                S.v("tensor_tensor", ksum[:], sm["KnewT"][:, 0, kk, s:s + 1], sm["KnewT"][:, 1, kk, s:s + 1], ALU.add, r=["s_KnewT"], w=["Sksum"])
                S.v("tensor_scalar", prod[:], Q, ksum[:, 0:1], 0.5, ALU.mult, ALU.mult, r=["s_QTs", "Sksum"], w=["Sprod"])
                S.p("matmul", ps[3][:, 256:260], self.selc[:, :], prod[:], start=True, stop=True, r=["selc", "Sprod"], w=["ps3"])
                S.a("activation", sm["QTs"][64:65, kk, s, :], ps[3][64:65, 256:260], AF.Identity, scale=-1.0, bias=self.negm[64:65, 0:1],
                    r=["ps3", "negm"], w=["s_QTs"])
                S.p("matmul", ps[4][0:4, 0:NCp], Q, KcTs[:, kk, :], start=True, stop=True, r=["s_QTs", "KcTs"], w=["ps4"])
                S.v("tensor_tensor", Sm[:], ps[4][0:4, 0:NCp], maskcs[:], ALU.add, r=["ps4", "maskcs"], w=["SSm"])
                S.v("tensor_reduce", mx[:, 0:1], Sm[:], AX.X, ALU.max, r=["SSm"], w=["Smx"])
                S.v("tensor_scalar", mx[:, 1:2], mx[:, 0:1], -10000.0, -1.0, ALU.max, ALU.mult, r=["Smx"], w=["Smx"])
                S.a("activation", Sm[:], Sm[:], AF.Exp, bias=mx[:, 1:2], accum_out=mx[:, 2:3], r=["SSm", "Smx"], w=["SSm", "Smx"])
                S.v("tensor_scalar", mx[:, 3:4], mx[:, 2:3], 1e-30, None, ALU.max, r=["Smx"], w=["Smx"])
                S.v("reciprocal", mx[:, 3:4], mx[:, 3:4], r=["Smx"], w=["Smx"])
                S.v("tensor_scalar", Pn[:], Sm[:], mx[:, 3:4], None, ALU.mult, r=["SSm", "Smx"], w=["SPn"])
                pb = self.psb(0)
                for j in range(NJ):
                    S.p("transpose", pb[:, j * 4:(j + 1) * 4], Pn[:, j * 128:(j + 1) * 128], self.identb[0:4, 0:4],
                        r=["SPn", "identb"], w=["ps0"])
                S.v("tensor_copy", PnT[:].rearrange("p j g -> p (j g)"), pb[:, 0:NJ * 4], r=["ps0"], w=["SPnT"])
                first = True
                for j in range(NJ):
                    S.p("matmul", ps[5][0:4, 0:64], PnT[:, j, :], vcs[:, j, kk, :], start=first, stop=False, skip_group_check=True,
                        r=["SPnT", "vcs"], w=["ps5"])
                    first = False
                    S.p("matmul", ps[5][0:4, 64:64 + NBp], PnT[:, j, :], MovS[:, j, :], start=False, stop=(j == NJ - 1), skip_group_check=True,
                        r=["SPnT", "MovS"], w=["ps5"])
                S.v("tensor_scalar", acc[:, kk, :], ps[5][0:4, 0:64], gT[:, s, kk, 0:1], None, ALU.mult, r=["ps5", "gT"], w=["Sacc"])
                S.v("tensor_copy", impg[:], ps[5][0:4, 64:64 + NBp], r=["ps5"], w=["impg"])
                S.p("matmul", ps[5][0:1, 256:256 + NBp], onec4[:], impg[:], start=True, stop=True, r=["onec4", "impg"], w=["ps5"])
                S.v("tensor_copy", scs[:], ps[5][0:1, 256:256 + NBp], r=["ps5"], w=["scs"])
                if NBp > NBs:
                    S.v("memset", scs[:, NBs:NBp], -1.0, w=["scs"])
                S.v("memset", scs[:, 0:1], 1.0e4, w=["scs"])
                S.v("memset", scs[:, NBs - 2:NBs], 1.0e4, w=["scs"])
                S.v("max", m8[:, 0:8], scs[:], r=["scs"], w=["Sm8"])
                S.v("match_replace", sc2[:], m8[:, 0:8], scs[:], -1.0e30, r=["scs", "Sm8"], w=["Ssc2"])
                S.v("max", m8[:, 8:16], sc2[:], r=["Ssc2"], w=["Sm8"])
                S.v("tensor_scalar", sc2[:], scs[:], m8[:, 15:16], None, ALU.is_ge, r=["scs", "Sm8"], w=["Ssc2"])
                S.v("tensor_scalar", bbs[:, 0:NE], sc2[:, 0:NE], BIG, -BIG, ALU.mult, ALU.add, r=["Ssc2"], w=["bbs"])
                S.p("transpose", pb[0:NE, 64:65], bbs[:, 0:NE], self.identb[0:1, 0:1], r=["bbs", "identb"], w=["ps0"])
                S.v("tensor_copy", bbT[0:NE, kk, :], bcast(pb[0:NE, 64], 1, 4), r=["ps0"], w=["SbbT"])
            for (br, obank, gi) in (("sel", 6, 1), ("win", 7, 2)):
                nch = NPG if br == "sel" else min(512, P) // 128
                ok = "ps%d" % obank
                for pg in range(nch):
                    t_ = pgt[pti % NPB]
                    tn = "pgt%d" % (pti % NPB)
                    b2 = pti % 2
                    pti += 1
                    bank = 1 + b2
                    if br == "sel":
                        S.dm("gpsimd", "indirect_dma_start", (t_[:], None, pool2, bass.IndirectOffsetOnAxis(ap=idxs[:, pg:pg + 1], axis=0)), {},
                             r=["idxs"], w=[tn])
                        p0 = pg * 128
                    else:
                        S.d("sync", t_[:], sp["cwin"][s, pg * 128:(pg + 1) * 128, :], w=[tn])
                        p0 = P - nch * 128 + pg * 128
                    for hh in range(4):
                        S.p("transpose", ps[bank][0:64, hh * 128:(hh + 1) * 128], t_[:, hh * 64:(hh + 1) * 64], self.ident[:],
                            r=[tn, "ident"], w=["ps%d" % bank])
                    S.v("tensor_copy", Kch[b2][0:64, :, :], ps[bank][0:64, :].rearrange("p (h n) -> p h n", h=4), r=["ps%d" % bank], w=["Kch%d" % b2])
                    S.d("sync", Kch[b2][96:102, :, :], self.cin["pos4"][:, :, p0:p0 + 128], w=["Kch%d" % b2])
                    S.a("copy", Vch[b2][:, :, 0:64], t_[:, 256:512].rearrange("p (h d) -> p h d", h=4), r=[tn], w=["Vch%d" % b2])
                    sbank = 3 + b2
                    for kk in range(4):
                        S.p("matmul", ps[sbank][:, kk * 4:(kk + 1) * 4], Kch[b2][:, kk, :], sm["QTs"][:, kk, s, :], start=True, stop=(br != "sel"),
                            r=["Kch%d" % b2, "s_QTs"], w=["ps%d" % sbank])
                        if br == "sel":
                            S.p("matmul", ps[sbank][:, kk * 4:(kk + 1) * 4], EallS[:, pg * 128:(pg + 1) * 128], bbT[0:NE, kk, :], start=False, stop=True,
                                r=["EallS", "SbbT"], w=["ps%d" % sbank])
                    S.a("activation", PTs[b2][:], ps[sbank][:, 0:16], AF.Exp, r=["ps%d" % sbank], w=["PTs%d" % b2])
                    for kk in range(4):
                        S.p("matmul", ps[obank][0:4, kk * 65:(kk + 1) * 65], PTs[b2][:, kk * 4:(kk + 1) * 4], Vch[b2][:, kk, :],
                            start=(pg == 0 and kk == 0), stop=False, skip_group_check=True, r=["PTs%d" % b2, "Vch%d" % b2], w=[ok])
                bi = 0 if br == "sel" else 1
                b2 = pti % 2
                pti += 1
                sbank = 3 + b2
                for kk in range(4):
                    S.p("matmul", ps[sbank][0:4, kk * 4:(kk + 1) * 4], sm["KnewT"][:, bi, kk, :], sm["QTs"][:, kk, s, :], start=True, stop=False,
                        r=["s_KnewT", "s_QTs"], w=["ps%d" % sbank])
                    S.p("matmul", ps[sbank][0:4, kk * 4:(kk + 1) * 4], self.identb[0:4, 0:4], ownm[:, s, :], start=False, stop=True,
                        r=["identb", "ownm"], w=["ps%d" % sbank])
                S.a("activation", PTs[b2][0:4, :], ps[sbank][0:4, 0:16], AF.Exp, r=["ps%d" % sbank], w=["PTs%d" % b2])
                for kk in range(4):
                    S.p("matmul", ps[obank][0:4, kk * 65:(kk + 1) * 65], PTs[b2][0:4, kk * 4:(kk + 1) * 4], sm["Vnew"][:, bi, kk, :],
                        start=False, stop=(kk == 3), skip_group_check=True, r=["PTs%d" % b2, "s_Vnew"], w=[ok])
                ov = ps[obank][0:4, 0:260].rearrange("p (k d) -> p k d", k=4)
                S.v("reciprocal", gr[:, 0:4], ov[:, :, 64], r=[ok], w=["Sgr"])
                S.v("tensor_tensor", gr[:, 4:8], gr[:, 0:4], gT[:, s, :, gi], ALU.mult, r=["Sgr", "gT"], w=["Sgr"])
                for kk in range(4):
                    S.v("scalar_tensor_tensor", acc[:, kk, :], ov[:, kk, 0:64], gr[:, 4 + kk:5 + kk], acc[:, kk, :], ALU.mult, ALU.add,
                        r=[ok, "Sgr", "Sacc"], w=["Sacc"])
            S.d("gpsimd", sm["oscr"][s].rearrange("(k g d) -> g k d", k=4, g=4), acc[:], r=["Sacc"], w=["oscr%d" % s])
        S.d("sync", oall[:], sm["oscr"], r=["oscr%d" % s for s in range(4)], w=["oall"])
        S.v("tensor_tensor", ozs[:], oall[:], sm["sz"][:], ALU.mult, r=["oall", "s_sz"], w=["ozs"])
        pb = self.psb(0)
        for c in range(8):
            S.p("transpose", pb[:, c * 4:(c + 1) * 4], ozs[:, c * 128:(c + 1) * 128], self.identb[0:4, 0:4], r=["ozs", "identb"], w=["ps0"])
        S.v("tensor_copy", ozT[:].rearrange("p c n -> p (c n)"), pb[:, 0:32], r=["ps0"], w=["SozT"])
        for half in range(2):
            bank = 1 + half
            for c in range(8):
                S.p("matmul", ps[bank][0:4, :], ozT[:, c, :], wo[:, c, half * 512:(half + 1) * 512], start=(c == 0), stop=(c == 7),
                    r=["SozT", "Swo"], w=["ps%d" % bank])
            S.v("tensor_tensor", xo[:, half * 512:(half + 1) * 512], ps[bank][0:4, :], sm["xs"][:, half * 512:(half + 1) * 512], ALU.add,
                r=["ps%d" % bank, "Sx"], w=["Sxo"])
        S.d("gpsimd", sp["xs_dst"], xo[:], r=["Sxo"], w=["xsres"])
        S.flush()


Builder.nsa_samp_alloc = nsa_samp_alloc
Builder.nsa_samp_A = nsa_samp_A
Builder.nsa_samp_Q = nsa_samp_Q
Builder.nsa_samp_S = nsa_samp_S


def mlstm_samp_1(self, l, sm, w, s1, wx, Wblk, Wg, cw, cb, nwc, skc, nw_bc, junk, ss):
    nc, S, ps = self.nc, self.S, self.ps
    sp = w["samp"]
    SC = MHD ** -0.5
    xs, hT = sm["xs"], sm["hT"]
    S.d("sync", xs[:], sp["xs_src"], r=["xsres"], w=["Sx"])
    self.front(xs, nw_bc, hT, junk, ss, "S", nrows=4)
    xmT = s1("s_xmT", [128, 16, 4], F32)
    xmb = s1("s_xmb", [128, 16, 4], BF16)
    xmtok = s1("s_xmtok", [4, 2048], F32)
    scv = s1("s_scv", [12, 2048], F32)
    cbuf = s1("s_cbuf", [128, 16, 12], F32)
    cv = s1("s_cv", [128, 16, 4], F32)
    tmp = s1("s_tmp", [128, 16, 4], F32)
    cT = s1("s_cT", [128, 16, 4], BF16)
    csT = s1("s_csT", [128, 16, 4], BF16)
    qT = s1("s_qT", [128, 16, 4], BF16)
    kT = s1("s_kT", [128, 16, 4], BF16)
    vT = s1("s_vT", [128, 16, 4], BF16)
    qf = s1("s_qf", [128, 16, 4], F32)
    qm = s1("s_qm", [128, 16, 4, 4], F32)
    tok = [s1("s_tok%d" % i, [4, 2048], F32) for i in range(3)]
    n0 = s1("s_n0", [4, 2048], F32)
    t2k = s1("s_t2k", [4, 2048], F32)
    g8 = s1("s_g8", [4, 8], F32)
    bgr = s1("s_bgr", [4, 8], F32)
    sc = s1("s_sc", [4, 64], F32)
    eye4 = s1("s_eye4", [4, 4, 4], F32)
    Dm = s1("s_Dm", [4, 4, 4], F32)
    ones4p = s1("s_ones4p", [4, 128], F32)
    decb = s1("s_decb", [128, 16], F32)
    coef = s1("s_coef", [4, 4, 4], F32)
    kwm = s1("s_kwm", [4, 512], F32)
    Cs = [s1("s_C%d" % i, [128, 4, 512], F32) for i in range(2)]
    hn = s1("s_hn", [4, 2048], BF16)
    hcf = s1("s_hcf", [4, 512], F32)
    S.v("memset", ones4p[:], 1.0, w=["s_ones4p"])
    S.d("sync", eye4[:], self.cin["eye4"], w=["s_eye4"])
    S.d("sync", bgr[:], w["b_gate"].rearrange("(o g) -> o g", o=1).partition_broadcast(4), w=["s_bgr"])
    S.d("sync", n0[:], sp["n_in"], w=["s_n0"])
    S.d("sync", sc[:, 0:4], sp["m_in"], w=["s_sc"])
    for cc in range(16):
        for c in range(8):
            S.p("matmul", ps[1][:, cc * 4:(cc + 1) * 4], wx[:, c, cc * 128:(cc + 1) * 128], hT[:, c, :], start=(c == 0), stop=(c == 7),
                r=["wx", "ShT"], w=["ps1"])
    S.v("tensor_copy", xmT[:].rearrange("p c n -> p (c n)"), ps[1][:, 0:64], r=["ps1"], w=["s_xmT"])
    S.v("tensor_copy", xmb[:], xmT[:], r=["s_xmT"], w=["s_xmb"])
    for q4 in range(4):
        bank = 4 + q4
        for c in range(8):
            S.p("matmul", ps[bank][0:4, :], hT[:, c, :], wx[:, c, q4 * 512:(q4 + 1) * 512], start=(c == 0), stop=(c == 7),
                r=["wx", "ShT"], w=["ps%d" % bank])
        S.v("tensor_copy", xmtok[:, q4 * 512:(q4 + 1) * 512], ps[bank][0:4, :], r=["ps%d" % bank], w=["s_xmtok"])
    S.d("gpsimd", sp["conv_out"][:, 2, :], xmtok[:], r=["s_xmtok"], w=["s_convout%d" % l])
    S.d("gpsimd", sp["conv_out"][:, 0:2, :], sp["conv_in"][:, 1:3, :], w=["s_convsh%d" % l])
    S.d("sync", scv[:], sp["conv_in"].rearrange("s j c -> (s j) c"), w=["s_scv"])
    for cc in range(16):
        S.p("transpose", ps[2][:, cc * 12:(cc + 1) * 12], scv[:, cc * 128:(cc + 1) * 128], self.ident[0:12, 0:12], r=["s_scv", "ident"], w=["ps2"])
    S.v("tensor_copy", cbuf[:].rearrange("p c n -> p (c n)"), ps[2][:, 0:192], r=["ps2"], w=["s_cbuf"])
    cb4 = cbuf[:].rearrange("p c (s j) -> p c s j", j=3)
    S.v("tensor_tensor", cv[:], xmT[:], bcast(cw[:, :, 3], 2, 4), ALU.mult, r=["s_xmT", "cw"], w=["s_cv"])
    for j in range(3):
        S.v("tensor_tensor", tmp[:], cb4[:, :, :, j], bcast(cw[:, :, j], 2, 4), ALU.mult, r=["s_cbuf", "cw"], w=["s_tmp"])
        S.v("tensor_tensor", cv[:], cv[:], tmp[:], ALU.add, r=["s_cv", "s_tmp"], w=["s_cv"])
    S.v("tensor_tensor", cv[:], cv[:], bcast(cb[:], 2, 4), ALU.add, r=["s_cv", "cb"], w=["s_cv"])
    S.a("activation", cT[:], cv[:], AF.Silu, r=["s_cv"], w=["s_cT"])
    S.v("tensor_tensor", csT[:], cT[:], bcast(skc[:], 2, 4), ALU.mult, r=["s_cT", "skc"], w=["s_csT"])
    for (x_, src, dst, nm) in ((0, cT, qT, "s_qT"), (1, cT, kT, "s_kT"), (2, xmb, vT, "s_vT")):
        for cc in range(16):
            S.p("matmul", ps[1][:, cc * 4:(cc + 1) * 4], Wblk[:, x_, cc, :], src[:, cc, :], start=True, stop=True,
                r=["Wblk", "s_cT", "s_xmb"], w=["ps1"])
        S.v("tensor_copy", dst[:].rearrange("p c n -> p (c n)"), ps[1][:, 0:64], r=["ps1"], w=[nm])
        if x_ == 0:
            S.v("tensor_copy", qf[:].rearrange("p c n -> p (c n)"), ps[1][:, 0:64], r=["ps1"], w=["s_qf"])
        for q4 in range(4):
            bank = 4 + q4
            for j in range(4):
                cc = q4 * 4 + j
                S.p("matmul", ps[bank][0:4, j * 128:(j + 1) * 128], src[:, cc, :], Wblk[:, x_, cc, :], start=True, stop=True,
                    r=["Wblk", "s_cT", "s_xmb"], w=["ps%d" % bank])
            S.v("tensor_copy", tok[x_][:, q4 * 512:(q4 + 1) * 512], ps[bank][0:4, :], r=["ps%d" % bank], w=["s_tok%d" % x_])
    first = True
    for (x_, src, nm) in ((0, qT, "s_qT"), (1, kT, "s_kT"), (2, vT, "s_vT")):
        for cc in range(16):
            S.p("matmul", ps[3][0:4, 0:8], src[:, cc, :], Wg[:, x_ * 16 + cc, :], start=first, stop=(x_ == 2 and cc == 15),
                r=["Wg", nm], w=["ps3"])
            first = False
    S.v("tensor_tensor", g8[:], ps[3][0:4, 0:8], bgr[:], ALU.add, r=["ps3", "s_bgr"], w=["s_g8"])
    S.v("tensor_scalar", sc[:, 4:8], g8[:, 4:8], -1.0, None, ALU.mult, r=["s_g8"], w=["s_sc"])
    S.v("tensor_tensor", sc[:, 4:8], sc[:, 4:8], g8[:, 4:8], ALU.max, r=["s_sc", "s_g8"], w=["s_sc"])
    S.a("activation", sc[:, 4:8], sc[:, 4:8], AF.Exp, scale=-1.0, r=["s_sc"], w=["s_sc"])
    S.a("activation", sc[:, 4:8], sc[:, 4:8], AF.Ln, bias=sm["onec"][0:4, 0:1], r=["s_sc", "s_onec"], w=["s_sc"])
    S.v("tensor_scalar", sc[:, 8:12], g8[:, 4:8], 0.0, None, ALU.min, r=["s_g8"], w=["s_sc"])
    S.v("tensor_tensor", sc[:, 8:12], sc[:, 8:12], sc[:, 4:8], ALU.subtract, r=["s_sc"], w=["s_sc"])
    S.v("tensor_tensor", sc[:, 4:8], sc[:, 8:12], sc[:, 0:4], ALU.add, r=["s_sc"], w=["s_sc"])
    S.v("tensor_tensor", sc[:, 12:16], sc[:, 4:8], g8[:, 0:4], ALU.max, r=["s_sc", "s_g8"], w=["s_sc"])
    S.v("tensor_tensor", sc[:, 16:20], sc[:, 4:8], sc[:, 12:16], ALU.subtract, r=["s_sc"], w=["s_sc"])
    S.a("activation", sc[:, 16:20], sc[:, 16:20], AF.Exp, r=["s_sc"], w=["s_sc"])
    S.v("tensor_tensor", sc[:, 20:24], g8[:, 0:4], sc[:, 12:16], ALU.subtract, r=["s_sc", "s_g8"], w=["s_sc"])
    S.a("activation", sc[:, 20:24], sc[:, 20:24], AF.Exp, r=["s_sc"], w=["s_sc"])
    S.a("activation", sc[:, 40:44], sc[:, 12:16], AF.Exp, scale=-1.0, r=["s_sc"], w=["s_sc"])
    S.d("gpsimd", sp["m_out"], sc[:, 12:16], r=["s_sc"], w=["s_mout%d" % l])
    S.v("tensor_tensor", t2k[:], tok[0][:], tok[1][:], ALU.mult, r=["s_tok0", "s_tok1"], w=["s_t2k"])
    S.v("tensor_reduce", sc[:, 24:28], t2k[:].rearrange("p (h d) -> p h d", h=4), AX.X, ALU.add, r=["s_t2k"], w=["s_sc"])
    S.v("tensor_scalar", sc[:, 24:28], sc[:, 24:28], SC, None, ALU.mult, r=["s_sc"], w=["s_sc"])
    S.v("tensor_tensor", t2k[:], tok[0][:], n0[:], ALU.mult, r=["s_tok0", "s_n0"], w=["s_t2k"])
    S.v("tensor_reduce", sc[:, 28:32], t2k[:].rearrange("p (h d) -> p h d", h=4), AX.X, ALU.add, r=["s_t2k"], w=["s_sc"])
    S.v("tensor_tensor", sc[:, 44:48], sc[:, 20:24], sc[:, 24:28], ALU.mult, r=["s_sc"], w=["s_sc"])
    S.v("tensor_tensor", sc[:, 32:36], sc[:, 16:20], sc[:, 28:32], ALU.mult, r=["s_sc"], w=["s_sc"])
    S.v("tensor_tensor", sc[:, 32:36], sc[:, 32:36], sc[:, 44:48], ALU.add, r=["s_sc"], w=["s_sc"])
    S.v("tensor_scalar", sc[:, 36:40], sc[:, 32:36], -1.0, None, ALU.mult, r=["s_sc"], w=["s_sc"])
    S.v("tensor_tensor", sc[:, 36:40], sc[:, 36:40], sc[:, 32:36], ALU.max, r=["s_sc"], w=["s_sc"])
    S.v("tensor_tensor", sc[:, 36:40], sc[:, 36:40], sc[:, 40:44], ALU.max, r=["s_sc"], w=["s_sc"])
    S.v("reciprocal", sc[:, 36:40], sc[:, 36:40], r=["s_sc"], w=["s_sc"])
    for h in range(4):
        S.v("tensor_scalar", t2k[:, h * 512:(h + 1) * 512], tok[1][:, h * 512:(h + 1) * 512], sc[:, 20 + h:21 + h], SC, ALU.mult, ALU.mult,
            r=["s_tok1", "s_sc"], w=["s_t2k"])
        S.v("scalar_tensor_tensor", n0[:, h * 512:(h + 1) * 512], n0[:, h * 512:(h + 1) * 512], sc[:, 16 + h:17 + h],
            t2k[:, h * 512:(h + 1) * 512], ALU.mult, ALU.add, r=["s_n0", "s_sc", "s_t2k"], w=["s_n0"])
    S.d("gpsimd", sp["n_out"], n0[:], r=["s_n0"], w=["s_nout%d" % l])
    S.v("tensor_tensor", Dm[:], bcast(sc[:, 16:20], 1, 4), eye4[:], ALU.mult, r=["s_sc", "s_eye4"], w=["s_Dm"])
    S.p("matmul", ps[3][:, 16:32], ones4p[:], Dm[:].rearrange("p s h -> p (s h)"), start=True, stop=True, r=["s_ones4p", "s_Dm"], w=["ps3"])
    S.v("tensor_copy", decb[:], ps[3][:, 16:32], r=["ps3"], w=["s_decb"])
    S.v("tensor_tensor", coef[:], bcast(sc[:, 20:24], 1, 4), eye4[:], ALU.mult, r=["s_sc", "s_eye4"], w=["s_coef"])
    S.v("tensor_scalar", coef[:], coef[:], SC, None, ALU.mult, r=["s_coef"], w=["s_coef"])
    S.v("memset", qm[:], 0.0, w=["s_qm"])
    for s in range(4):
        S.v("tensor_copy", qm[:, :, s, s], qf[:, :, s], r=["s_qf"], w=["s_qm"])
    k_ = 0
    for h in range(4):
        bank = 6 + h % 2
        for s in range(4):
            Cb = Cs[k_ % 2]
            cn = "s_C%d" % (k_ % 2)
            k_ += 1
            S.d("sync", Cb[:], sp["C_in"][s, h].rearrange("(c p) e -> p c e", p=128), w=[cn])
            for dc in range(4):
                S.p("matmul", ps[bank][0:4, :], qm[:, 4 * h + dc, s, :], Cb[:, dc, :], start=(s == 0 and dc == 0), stop=(s == 3 and dc == 3),
                    r=["s_qm", cn], w=["ps%d" % bank])
            S.v("tensor_scalar", kwm[:], tok[1][:, h * 512:(h + 1) * 512], coef[:, s, h:h + 1], None, ALU.mult, r=["s_tok1", "s_coef"], w=["s_kwm"])
            for dc in range(4):
                cbk = 1 + dc % 2
                S.p("matmul", ps[cbk][:, :], kwm[:, dc * 128:(dc + 1) * 128], tok[2][:, h * 512:(h + 1) * 512], start=True, stop=True,
                    r=["s_kwm", "s_tok2"], w=["ps%d" % cbk])
                S.v("scalar_tensor_tensor", Cb[:, dc, :], Cb[:, dc, :], decb[:, s * 4 + h:s * 4 + h + 1], ps[cbk][:, :], ALU.mult, ALU.add,
                    r=[cn, "s_decb", "ps%d" % cbk], w=[cn])
            S.d("gpsimd", sp["C_out"][s, h].rearrange("(c p) e -> p c e", p=128), Cb[:], r=[cn], w=["s_Cout%d_%d_%d" % (l, s, h)])
        S.v("tensor_scalar", hcf[:], tok[2][:, h * 512:(h + 1) * 512], sc[:, 44 + h:45 + h], None, ALU.mult, r=["s_tok2", "s_sc"], w=["s_hcf"])
        S.v("scalar_tensor_tensor", hcf[:], ps[bank][0:4, :], sc[:, 16 + h:17 + h], hcf[:], ALU.mult, ALU.add,
            r=["ps%d" % bank, "s_sc", "s_hcf"], w=["s_hcf"])
        S.v("tensor_scalar", hcf[:], hcf[:], sc[:, 36 + h:37 + h], None, ALU.mult, r=["s_hcf", "s_sc"], w=["s_hcf"])
        S.a("activation", junk[0:4, 0:512], hcf[:], AF.Square, accum_out=sc[:, 48 + h:49 + h], r=["s_hcf"], w=["junk", "s_sc"])
        S.a("activation", sc[:, 52 + h:53 + h], sc[:, 48 + h:49 + h], AF.Sqrt, scale=1.0 / MHD, bias=self.epsc[0:4, 0:1], r=["s_sc", "epsc"], w=["s_sc"])
        S.v("reciprocal", sc[:, 56 + h:57 + h], sc[:, 52 + h:53 + h], r=["s_sc"], w=["s_sc"])
        S.v("tensor_scalar", hn[:, h * 512:(h + 1) * 512], hcf[:], sc[:, 56 + h:57 + h], None, ALU.mult, r=["s_hcf", "s_sc"], w=["s_hn"])
    pb = self.psb(0)
    for cc in range(16):
        S.p("transpose", pb[:, cc * 4:(cc + 1) * 4], hn[:, cc * 128:(cc + 1) * 128], self.identb[0:4, 0:4], r=["s_hn", "identb"], w=["ps0"])
    po = sm["po"]
    S.v("tensor_tensor", po[:], pb[:, 0:64].rearrange("p (c n) -> p c n", c=16), bcast(nwc[:], 2, 4), ALU.mult, r=["ps0", "nwc"], w=["s_po"])
    S.v("tensor_tensor", po[:], po[:], csT[:], ALU.add, r=["s_po", "s_csT"], w=["s_po"])


def mlstm_samp_2(self, l, sm, w, s2, wz, wo):
    S, ps = self.S, self.ps
    sp = w["samp"]
    hT = sm["hT"]
    szT = s2("s_szT", [128, 16, 4], BF16)
    oT = s2("s_oT", [128, 16, 4], BF16)
    xo = s2("s_xo", [4, 1024], F32)
    for cc in range(16):
        for c in range(8):
            S.p("matmul", ps[1][:, cc * 4:(cc + 1) * 4], wz[:, c, cc * 128:(cc + 1) * 128], hT[:, c, :], start=(c == 0), stop=(c == 7),
                r=["wz", "ShT"], w=["ps1"])
    S.a("activation", szT[:].rearrange("p c n -> p (c n)"), ps[1][:, 0:64], AF.Silu, r=["ps1"], w=["s_szT"])
    S.v("tensor_tensor", oT[:], sm["po"][:], szT[:], ALU.mult, r=["s_po", "s_szT"], w=["s_oT"])
    for half in range(2):
        bank = 6 + half
        for cc in range(16):
            S.p("matmul", ps[bank][0:4, :], oT[:, cc, :], wo[:, cc, half * 512:(half + 1) * 512], start=(cc == 0), stop=(cc == 15),
                r=["s_oT", "wo"], w=["ps%d" % bank])
        S.v("tensor_tensor", xo[:, half * 512:(half + 1) * 512], ps[bank][0:4, :], sm["xs"][:, half * 512:(half + 1) * 512], ALU.add,
            r=["ps%d" % bank, "Sx"], w=["s_xo"])
    S.d("gpsimd", sp["xs_dst"], xo[:], r=["s_xo"], w=["xsres"])


Builder.mlstm_samp_1 = mlstm_samp_1
Builder.mlstm_samp_2 = mlstm_samp_2
```

```python
import numpy as np
import ml_dtypes
from contextlib import ExitStack
import concourse.bass as bass
import concourse.mybir as mybir
from concourse.bass_utils import run_bass_kernel_spmd

F32 = mybir.dt.float32
BF16 = mybir.dt.bfloat16
I32 = mybir.dt.int32
AF = mybir.ActivationFunctionType
ALU = mybir.AluOpType
AX = mybir.AxisListType
NPBF = ml_dtypes.bfloat16

ENGS = ("tensor", "vector", "scalar", "gpsimd", "sync")
BIG = 30000.0
MARGIN = 30.0

D_MODEL = 1024
N_HEADS = 16
HD = 64
NKV = 4
GRP = 4
NSA_IN = 3632
D_INNER = 2048
M_HEADS = 4
MHD = 512
RMS_EPS = 1e-6


class Sched:
    def __init__(self, nc, stack, n_dma_sems=6, same_engine_sync=True):
        self.nc = nc
        self.same = same_engine_sync
        self.semobj = {}
        self.ecount = {e: 0 for e in ENGS}
        for e in ENGS:
            self.semobj["s_" + e] = stack.enter_context(nc.semaphore("s_" + e))
        self.dq = {}
        self.dcount = {}
        for q in ("sync", "gpsimd", "scalar"):
            self.dq[q] = []
            for i in range(n_dma_sems):
                sn = "d_%s%d" % (q, i)
                self.semobj[sn] = stack.enter_context(nc.semaphore(sn))
                self.dq[q].append(sn)
                self.dcount[sn] = 0
        self.dnext = {q: 0 for q in self.dq}
        self.waited = {e: {} for e in ENGS}
        self.lastw = {}
        self.readers = {}
        self.pending = {e: [] for e in ENGS}
        self.nops = 0

    def _deps(self, reads, writes, eng=None):
        deps = []
        for r in reads:
            t = self.lastw.get(r)
            if t is not None:
                deps.append(t)
            if r.startswith("ps"):
                for t2 in self.readers.get(r, ()):
                    if t2[2] != eng:
                        deps.append(t2)
        for w in writes:
            t = self.lastw.get(w)
            if t is not None:
                deps.append(t)
            deps.extend(self.readers.get(w, ()))
        return deps

    def _filter(self, eng, deps):
        best = {}
        for (sn, val, src) in deps:
            if src == eng and not sn.startswith("d_"):
                if eng == "tensor" or not self.same:
                    continue
            if self.waited[eng].get(sn, 0) >= val:
                continue
            if best.get(sn, 0) < val:
                best[sn] = val
        for sn, val in best.items():
            self.waited[eng][sn] = val
        return list(best.items())

    def _commit(self, tok, reads, writes):
        for r in reads:
            self.readers.setdefault(r, []).append(tok)
        for w in writes:
            self.lastw[w] = tok
            self.readers[w] = []

    def op(self, eng, m, a, k, reads, writes):
        waits = self._filter(eng, self._deps(reads, writes, eng))
        self.ecount[eng] += 1
        tok = ("s_" + eng, self.ecount[eng], eng)
        self.pending[eng].append((waits, m, a, k, "s_" + eng, 1))
        self._commit(tok, reads, writes)
        self.nops += 1

    def v(self, m, *a, r=(), w=(), **k):
        self.op("vector", m, a, k, r, w)

    def a(self, m, *a, r=(), w=(), **k):
        self.op("scalar", m, a, k, r, w)

    def p(self, m, *a, r=(), w=(), **k):
        self.op("tensor", m, a, k, r, w)

    def g(self, m, *a, r=(), w=(), **k):
        self.op("gpsimd", m, a, k, r, w)

    def d(self, q, out, in_, r=(), w=(), m="dma_start", **k):
        import os as _os
        if _os.environ.get("NOGPQ") == "1" and q == "gpsimd":
            q = "sync"
        i = self.dnext[q]
        self.dnext[q] += 1
        sn = self.dq[q][i % len(self.dq[q])]
        deps = self._deps(r, w)
        prev = self.dcount[sn]
        if prev > 0:
            deps.append((sn, 16 * prev, "dma"))
        waits = self._filter(q, deps)
        self.dcount[sn] = prev + 1
        tok = (sn, 16 * (prev + 1), "dma")
        kk = dict(k)
        if m == "dma_start":
            kk["out"] = out
            kk["in_"] = in_
            args = ()
        else:
            args = (out, in_)
        self.pending[q].append((waits, m, args, kk, sn, 16))
        self._commit(tok, r, w)
        self.nops += 1

    def barrier(self):
        for e in ENGS:
            waits = []
            for e2 in ENGS:
                sn = "s_" + e2
                if e2 != e and self.ecount[e2] > self.waited[e].get(sn, 0):
                    waits.append((sn, self.ecount[e2]))
                    self.waited[e][sn] = self.ecount[e2]
            for sn, c in self.dcount.items():
                if 16 * c > self.waited[e].get(sn, 0):
                    waits.append((sn, 16 * c))
                    self.waited[e][sn] = 16 * c
            if waits:
                self.pending[e].append((waits, None, None, None, None, 0))

    def dm(self, q, m, args, kwargs, r=(), w=()):
        i = self.dnext[q]
        self.dnext[q] += 1
        sn = self.dq[q][i % len(self.dq[q])]
        deps = self._deps(r, w)
        prev = self.dcount[sn]
        if prev > 0:
            deps.append((sn, 16 * prev, "dma"))
        waits = self._filter(q, deps)
        self.dcount[sn] = prev + 1
        tok = (sn, 16 * (prev + 1), "dma")
        self.pending[q].append((waits, m, tuple(args), dict(kwargs), sn, 16))
        self._commit(tok, r, w)
        self.nops += 1

    def flush(self, final=False, barrier=True):
        if barrier and not final:
            self.barrier()
        if final:
            for sn, c in self.dcount.items():
                if c > 0 and self.waited["sync"].get(sn, 0) < 16 * c:
                    self.pending["sync"].append(([(sn, 16 * c)], None, None, None, None, 0))
            for e in ENGS:
                if e != "sync" and self.ecount[e] > 0:
                    self.pending["sync"].append(([("s_" + e, self.ecount[e])], None, None, None, None, 0))
        pend = self.pending
        self.pending = {e: [] for e in ENGS}
        semobj = self.semobj
        import os as _os
        if _os.environ.get("DUMP") == "1":
            for e in ENGS:
                print("==== engine", e)
                for waits, m, a, k, sn, inc in pend[e]:
                    print("   ", waits, m, sn, inc)

        def run(engine, lst):
            for waits, m, a, k, sn, inc in lst:
                import os as _os
                if _os.environ.get("WORDER") == "1":
                    waits = sorted(waits, key=lambda t: (t[0] == "s_tensor", t[0]))
                for wn, val in waits:
                    engine.wait_ge(semobj[wn], val)
                if m is not None:
                    ins = getattr(engine, m)(*a, **k)
                    ins.then_inc(semobj[sn], inc)

        with self.nc.Block() as block:
            if pend["tensor"]:
                @block.tensor
                def _(e):
                    run(e, pend["tensor"])
            if pend["vector"]:
                @block.vector
                def _(e):
                    run(e, pend["vector"])
            if pend["scalar"]:
                @block.scalar
                def _(e):
                    run(e, pend["scalar"])
            if pend["gpsimd"]:
                @block.gpsimd
                def _(e):
                    run(e, pend["gpsimd"])
            if pend["sync"]:
                @block.sync
                def _(e):
                    run(e, pend["sync"])


def bcast(ap, axis, n):
    sh = list(ap.shape)
    a = ap.unsqueeze(axis)
    sh.insert(axis, n)
    return a.broadcast_to(sh)


def bf16_split3(x):
    x = np.asarray(x, np.float32)
    a = x.astype(NPBF).astype(np.float32)
    b = (x - a).astype(NPBF).astype(np.float32)
    c = (x - a - b).astype(NPBF).astype(np.float32)
    return a, b, c


def make_consts(T, TS, P=0):
    c = {}
    if P:
        NE = P // 64
        keyp = np.arange(P)
        c["EallS"] = (keyp[None, :] // 64 == np.arange(NE)[:, None]).astype(np.float32).astype(NPBF)
        NCs = P // 16
        NJ = max(1, NCs // 128)
        NCp = NJ * 128
        NBs = P // 64 + 1
        NBp = ((NBs + 3) // 4) * 4
        nn = np.arange(NCp)
        jj = np.arange(NBp)
        ovs = ((16 * nn[:, None] < 64 * jj[None, :] + 64) & (16 * nn[:, None] + 32 > 64 * jj[None, :]) & (jj[None, :] < NBs))
        c["MovS"] = np.ascontiguousarray(ovs.astype(np.float32).reshape(NJ, 128, NBp).transpose(1, 0, 2)).astype(NPBF)
        own = np.full((4, 4, 4), -BIG, np.float32)
        for s_ in range(4):
            own[s_, s_, :] = 0.0
        c["ownm"] = own.astype(NPBF)
        c["maskcs"] = np.ascontiguousarray(np.broadcast_to(np.where(16 * nn + 31 <= P, 0.0, -BIG).astype(np.float32)[None], (4, NCp)))
        c["pcol2"] = np.stack([2 * np.arange(128), 2 * np.arange(128) + 1], 1).astype(np.float32)
        Ap = float(64 * (P // 64))
        Bp = float(P % 64)
        c["posP"] = np.ascontiguousarray(np.broadcast_to(np.array([Ap, Ap, Ap, Bp, Bp, Bp], np.float32)[:, None], (6, 4))).astype(NPBF)
        slp = np.exp2(-8.0 * np.arange(1, N_HEADS + 1) / N_HEADS).astype(np.float32)
        a1, a2, a3 = bf16_split3(slp)
        c["slope4"] = np.ascontiguousarray(np.stack([a1, a2, a3, a1, a2, a3]).reshape(6, 4, 4).transpose(1, 0, 2)).astype(NPBF)
    c["ident"] = np.eye(128, dtype=np.float32)
    c["identb"] = np.eye(128, dtype=np.float32).astype(NPBF)
    npos = max(T, TS)
    pos = np.arange(npos)
    A = (64 * (pos // 64)).astype(np.float32)
    B = (pos % 64).astype(np.float32)
    c["pos"] = np.stack([A, A, A, B, B, B]).astype(NPBF)
    ce = 16 * np.arange(512) + 31
    Ac = (64 * (ce // 64)).astype(np.float32)
    Bc = (ce % 64).astype(np.float32)
    c["cend"] = np.stack([Ac, Ac, Ac, Bc, Bc, Bc]).astype(NPBF)
    slopes = np.exp2(-8.0 * np.arange(1, N_HEADS + 1) / N_HEADS).astype(np.float32)
    s1, s2, s3 = bf16_split3(slopes)
    sl = np.stack([s1, s2, s3, s1, s2, s3])
    c["slope"] = np.ascontiguousarray(
        np.broadcast_to(sl.reshape(6, 4, 4, 1), (6, 4, 4, 128)).transpose(1, 0, 2, 3)).astype(NPBF)
    kk = np.arange(128)[:, None]
    qq = np.arange(128)[None, :]
    causal = np.where(kk > qq, -BIG, 0.0).astype(np.float32)
    far = np.where(kk < qq, -BIG, 0.0).astype(np.float32)
    c["causal"] = np.ascontiguousarray(np.broadcast_to(causal[:, None, :], (128, 4, 128))).astype(NPBF)
    c["far"] = np.ascontiguousarray(np.broadcast_to(far[:, None, :], (128, 4, 128))).astype(NPBF)
    m = np.arange(512)[None, :] - 256
    ql = np.arange(128)[:, None]
    c["maskc"] = np.where(16 * m + 31 <= ql, 0.0, -BIG).astype(np.float32)
    jr = np.arange(128)[None, :] - 64
    tbl = ql // 64
    allow = (jr <= tbl).astype(np.float32)
    c["allow"] = allow
    c["allowm1"] = allow - 1.0
    c["force"] = np.where((jr == tbl) | (jr == tbl - 1), 1.0e4, -1.0e30).astype(np.float32)
    pp = np.arange(128)
    c["maskbd"] = (pp[:, None] // 4 == pp[None, :] // 4).astype(np.float32)
    selh = np.zeros((4, 4, 128), np.float32)
    for h in range(4):
        selh[h, h, :] = 1.0
    c["selh"] = selh
    c["trib"] = np.where(kk > qq, -BIG, 0.0).astype(np.float32)
    c["eye4"] = np.ascontiguousarray(np.broadcast_to(np.eye(4, dtype=np.float32)[:, :, None], (4, 4, 4)))
    NB = T // 64
    key = np.arange(T)
    c["Eall"] = (key[None, :] // 64 == np.arange(NB)[:, None]).astype(np.float32).astype(NPBF)
    n = np.arange(256)
    j = np.arange(NB)
    ov = ((16 * n[:, None] < 64 * j[None, :] + 64) & (16 * n[:, None] + 32 > 64 * j[None, :])).astype(np.float32)
    c["Mov"] = np.ascontiguousarray(ov.reshape(2, 128, NB).transpose(1, 0, 2)).astype(NPBF)
    return c


class Builder:
    def __init__(self, T, TS=128, P=0):
        self.T = T
        self.TS = TS
        self.P = P
        self.NT = T // 128
        self.NB = T // 64
        self.nc = bass.Bass("TRN2", target_bir_lowering=False)
        self.ins = {}
        self.outs = {}

    def din(self, name, shape, dt=F32):
        self.ins[name] = self.nc.dram_tensor(name, list(shape), dt, kind="ExternalInput").ap()
        return self.ins[name]

    def dout(self, name, shape, dt=F32):
        self.outs[name] = self.nc.dram_tensor(name, list(shape), dt, kind="ExternalOutput").ap()
        return self.outs[name]

    def dscr(self, name, shape, dt=F32):
        return self.nc.dram_tensor(name, list(shape), dt).ap()

    def setup(self, st):
        nc = self.nc
        self.st = st
        import os as _os
        self.S = Sched(nc, st, same_engine_sync=(_os.environ.get("SAMESYNC", "1") == "1"))
        S = self.S
        T, NB = self.T, self.NB
        cin = {}
        cshapes = {"ident": ([128, 128], F32), "identb": ([128, 128], BF16), "pos": ([6, max(T, self.TS)], BF16),
                   "cend": ([6, 512], BF16), "slope": ([4, 6, 4, 128], BF16), "causal": ([128, 4, 128], BF16),
                   "far": ([128, 4, 128], BF16), "maskc": ([128, 512], F32), "allow": ([128, 128], F32),
                   "allowm1": ([128, 128], F32), "force": ([128, 128], F32), "Eall": ([NB, T], BF16),
                   "Mov": ([128, 2, NB], BF16), "maskbd": ([128, 128], F32), "selh": ([4, 4, 128], F32),
                   "trib": ([128, 128], F32), "eye4": ([4, 4, 4], F32)}
        if self.P:
            P_ = self.P
            NCs_ = P_ // 16
            NJ_ = max(1, NCs_ // 128)
            NBp_ = ((P_ // 64 + 1 + 3) // 4) * 4
            cshapes.update({"EallS": ([P_ // 64, P_], BF16), "MovS": ([128, NJ_, NBp_], BF16), "ownm": ([4, 4, 4], BF16),
                            "maskcs": ([4, NJ_ * 128], F32), "pcol2": ([128, 2], F32), "posP": ([6, 4], BF16),
                            "slope4": ([4, 6, 4], BF16)})
        for k, (shp, dt) in cshapes.items():
            cin[k] = self.din("c_" + k, shp, dt)
        self.cin = cin
        self.ps = [st.enter_context(nc.psum_tensor("ps%d" % i, [128, 512], F32)) for i in range(8)]

        def sb(name, shape, dt):
            return st.enter_context(nc.sbuf_tensor(name, shape, dt))
        self.sb = sb
        self.ident = sb("ident", [128, 128], F32)
        self.identb = sb("identb", [128, 128], BF16)
        self.causal = sb("causal", [128, 4, 128], BF16)
        self.far = sb("far", [128, 4, 128], BF16)
        self.maskc = sb("maskc", [128, 512], F32)
        self.allow = sb("allow", [128, 128], F32)
        self.allowm1 = sb("allowm1", [128, 128], F32)
        self.force = sb("force", [128, 128], F32)
        self.selc = sb("selc", [102, 128], BF16)
        self.onesb = sb("onesb", [128, 128], BF16)
        self.maskbd = sb("maskbd", [128, 128], F32)
        self.selh = sb("selh", [4, 4, 128], F32)
        self.trib = sb("trib", [128, 128], F32)
        for nm in ("ident", "identb", "causal", "far", "maskc", "allow", "allowm1", "force", "maskbd", "selh", "trib"):
            t = getattr(self, nm)
            S.d("sync", t[:], cin[nm], w=[nm])
        self.negm = sb("negm", [128, 1], F32)
        S.v("memset", self.negm[:], -MARGIN, w=["negm"])
        S.v("memset", self.selc[:], 1.0, w=["selc"])
        S.v("memset", self.selc[64:96, :], 0.0, w=["selc"])
        S.v("memset", self.onesb[:], 1.0, w=["onesb"])

    def psb(self, i):
        return self.ps[i][:].bitcast(BF16)

    def front(self, xt, nw_bc, hT, junk, ss, pfx, nrows=128):
        S = self.S
        h = self.h_bf
        import os as _os
        if _os.environ.get("NOACC") == "1":
            S.v("tensor_tensor", junk[0:nrows, :], xt[0:nrows, :], xt[0:nrows, :], ALU.mult, r=[pfx + "x"], w=["junk"])
            S.v("tensor_reduce", ss[0:nrows, 0:1], junk[0:nrows, :], AX.X, ALU.add, r=["junk"], w=["ss"])
        else:
            S.a("activation", junk[0:nrows, :], xt[0:nrows, :], AF.Square, accum_out=ss[0:nrows, 0:1],
                r=[pfx + "x"], w=["junk", "ss"])
        if _os.environ.get("NOSQRT") == "1":
            S.a("activation", ss[0:nrows, 1:2], ss[0:nrows, 0:1], AF.Ln, scale=1.0 / D_MODEL, bias=self.epsc[0:nrows, 0:1],
                r=["ss", "epsc"], w=["ss1"])
            S.a("activation", ss[0:nrows, 2:3], ss[0:nrows, 1:2], AF.Exp, scale=-0.5, r=["ss1"], w=["ss2"])
        else:
            S.a("activation", ss[0:nrows, 1:2], ss[0:nrows, 0:1], AF.Sqrt, scale=1.0 / D_MODEL, bias=self.epsc[0:nrows, 0:1],
                r=["ss", "epsc"], w=["ss1"])
            S.v("reciprocal", ss[0:nrows, 2:3], ss[0:nrows, 1:2], r=["ss1"], w=["ss2"])
        S.v("scalar_tensor_tensor", h[0:nrows, :], xt[0:nrows, :], ss[0:nrows, 2:3], nw_bc[0:nrows, :], ALU.mult, ALU.mult,
            r=[pfx + "x", "ss2", "nw_bc"], w=["h_bf"])
        pb = self.psb(0)
        for c in range(8):
            S.p("transpose", pb[:, c * 128:c * 128 + nrows], h[0:nrows, c * 128:(c + 1) * 128], self.identb[0:nrows, 0:nrows],
                r=["h_bf", "identb"], w=["ps0"])
        S.v("tensor_copy", hT[:, :, 0:nrows], pb[:, 0:1024].rearrange("p (c n) -> p c n", c=8)[:, :, 0:nrows],
            r=["ps0"], w=[pfx + "hT"])

    def nsa_layer(self, l, xsrc, xdst, w):
        nc, S, T, NT, NB = self.nc, self.S, self.T, self.NT, self.NB
        ps = self.ps
        with ExitStack() as L:
            def sb(name, shape, dt):
                return L.enter_context(nc.sbuf_tensor("%s_l%d" % (name, l), shape, dt))
            nw_bc = sb("nw_bc", [128, 1024], F32)
            self.h_bf = sb("h_bf", [128, 1024], BF16)
            self.epsc = sb("epsc", [128, 1], F32)
            junk = sb("junk", [128, 1024], BF16)
            ss = sb("ss", [128, 4], F32)
            sm = self.nsa_samp_alloc(l, sb, w) if "samp" in w else None
            Lp = ExitStack()

            def sbp(name, shape, dt):
                return Lp.enter_context(nc.sbuf_tensor("%s_l%d" % (name, l), shape, dt))
            KselT = [sbp("KselT%d" % k, [102, T], BF16) for k in range(4)]
            KwinR = sbp("KwinR", [102, 4, 6 * 128], BF16)
            Vsel = sbp("Vsel", [128, NT, 4, 65], BF16)
            VwinR = sbp("VwinR", [128, 6, 4, 65], BF16)
            KcT = [sbp("KcT%d" % k, [102, 256], BF16) for k in range(4)]
            vc = sbp("vc", [128, 2, 4, 64], BF16)
            gates = sbp("gates", [128, NT, 48], F32)
            S.v("memset", self.epsc[:], RMS_EPS, w=["epsc"])
            S.d("sync", nw_bc[:], w["norm_w"].partition_broadcast(128), w=["nw_bc"])
            for k in range(4):
                for (Kt, nm) in ((KselT[k], "KselT%d" % k),):
                    wr = [nm + "_%d" % t for t in range(NT)]
                    S.v("memset", Kt[64:96, :], 0.0, w=wr)
                    S.v("memset", Kt[64:65, :], 1.0, w=wr)
                    S.d("sync", Kt[96:102, :], self.cin["pos"][:, 0:T], w=wr)
                S.v("memset", KcT[k][64:96, :], 0.0, w=["KcT%d" % k])
                S.d("sync", KcT[k][96:102, :], self.cin["cend"][:, 0:256], w=["KcT%d" % k])
            S.v("memset", Vsel[:, :, :, 64:65], 1.0, w=["Vsel_%d" % t for t in range(NT)])
            S.v("memset", VwinR[:, :, :, 64:65], 1.0, w=["VwinR_%d" % t for t in range(6)])
            S.v("memset", KwinR[64:96, :, :], 0.0, w=["KwinR_%d" % t for t in range(6)])
            S.v("memset", KwinR[64:65, :, :], 1.0, w=["KwinR_%d" % t for t in range(6)])
            kwTs = self.dscr("kwTs_l%d" % l, [NT, 64, 4, 128], BF16)
            vws = self.dscr("vws_l%d" % l, [NT, 128, 256], BF16)
            if getattr(self, "stop", None) == "init":
                S.flush()
                return
            hTs = self.dscr("hTs_l%d" % l, [NT, 128, 1024], BF16)
            zs = self.dscr("zs_l%d" % l, [NT, 128, 1024], BF16)

            with ExitStack() as A:
                def sba(name, shape, dt):
                    return A.enter_context(nc.sbuf_tensor("%s_A%d" % (name, l), shape, dt))
                KVcT = [[sba("KVcT%d%d" % (c, pr), [128, T + 32], BF16) for pr in range(2)] for c in range(2)]
                A1 = ExitStack()
                sba_outer = sba

                def sba(name, shape, dt):
                    return A1.enter_context(nc.sbuf_tensor("%s_A%d" % (name, l), shape, dt))
                NCOL = NSA_IN - 1024
                wkv = sba("wkv", [128, 8, NCOL], BF16)
                HC = NCOL // 4
                wst = [sba("wst%d" % i, [128, HC], F32) for i in range(2)]
                for c in range(8):
                    for q_ in range(4):
                        hf = q_ % 2
                        S.d("sync", wst[hf][:], w["w_in"][c * 128:(c + 1) * 128, 1024 + q_ * HC:1024 + (q_ + 1) * HC], w=["wst%d" % hf])
                        if hf == 0:
                            S.v("tensor_copy", wkv[:, c, q_ * HC:(q_ + 1) * HC], wst[hf][:], r=["wst%d" % hf], w=["wkv"])
                        else:
                            S.a("copy", wkv[:, c, q_ * HC:(q_ + 1) * HC], wst[hf][:], r=["wst%d" % hf], w=["wkv"])
                if getattr(self, "stop", None) == "A1w":
                    S.flush()
                    return
                for c in range(2):
                    for pr in range(2):
                        S.v("memset", KVcT[c][pr][:, T:T + 32], 0.0, w=["KVcT%d%d" % (c, pr)])
                xt = [sba("xtA%d" % i, [128, 1024], F32) for i in range(2)]
                hT = [sba("hTA%d" % i, [128, 8, 128], BF16) for i in range(2)]
                kvt = [sba("kvt0", [128, 1024], F32)] * 2
                wnt = [sba("wnt0", [128, 512], F32)] * 2
                zt = [sba("zt0", [128, 1024], BF16)] * 2
                kwst = [sba("kwst%d" % i, [64, 4, 128], BF16) for i in range(2)]
                vwst = [sba("vwst%d" % i, [128, 256], BF16) for i in range(2)]
                OKV, OWIN, OG, OZ = 0, 1024, 1536, 1584
                for tt in range(NT):
                    b = tt % 2
                    pf = "A%d" % b
                    S.d("sync", xt[b][:], xsrc[tt * 128:(tt + 1) * 128, :], r=["xres_%d" % tt], w=[pf + "x"])
                    self.front(xt[b], nw_bc, hT[b], junk, ss, pf)
                    if getattr(self, "stop", None) == "A1f":
                        S.flush()
                        return
                    S.d("gpsimd", hTs[tt], hT[b][:].rearrange("p c n -> p (c n)"), r=[pf + "hT"], w=["hTs%d" % tt])
                    if getattr(self, "stop", None) == "A1h":
                        S.flush()
                        return
                    groups = [("kcmp", OKV + 0), ("vcmp", OKV + 256), ("ksel", OKV + 512), ("kwin", OWIN + 0)]
                    for gi, (gn, off) in enumerate(groups):
                        if getattr(self, "stop", None) == "A1g%d" % gi:
                            S.flush()
                            return
                        bank = 1 + (gi % 4)
                        pk = "ps%d" % bank
                        if gn in ("kcmp", "vcmp"):
                            for pr in range(2):
                                for c in range(8):
                                    S.p("matmul", ps[bank][:, pr * 128:(pr + 1) * 128],
                                        wkv[:, c, off + pr * 128: off + (pr + 1) * 128], hT[b][:, c, :],
                                        start=(c == 0), stop=(c == 7), r=["wkv", pf + "hT"], w=[pk])
                            ci = 0 if gn == "kcmp" else 1
                            import os as _os
                            if _os.environ.get("NOEVAC") == "1":
                                S.flush()
                                return
                            for pr in range(2):
                                if _os.environ.get("NOEVAC") == "2" and pr == 1:
                                    S.flush()
                                    return
                                if pr == 1 and _os.environ.get("ACTV") == "ident":
                                    S.a("activation", KVcT[ci][pr][:, tt * 128:(tt + 1) * 128], ps[bank][:, pr * 128:(pr + 1) * 128], AF.Identity,
                                        r=[pk], w=["KVcT%d%d" % (ci, pr)])
                                    continue
                                if _os.environ.get("ACTV") == "serial" and pr == 1:
                                    S.a("copy", KVcT[ci][pr][:, tt * 128:(tt + 1) * 128], ps[bank][:, pr * 128:(pr + 1) * 128],
                                        r=[pk, "KVcT%d%d" % (ci, 0)], w=["KVcT%d%d" % (ci, pr)])
                                    continue
                                if _os.environ.get("ACTV") == "swap":
                                    if pr == 0:
                                        S.a("copy", KVcT[ci][pr][:, tt * 128:(tt + 1) * 128], ps[bank][:, pr * 128:(pr + 1) * 128],
                                            r=[pk], w=["KVcT%d%d" % (ci, pr)])
                                    else:
                                        S.v("tensor_copy", KVcT[ci][pr][:, tt * 128:(tt + 1) * 128], ps[bank][:, pr * 128:(pr + 1) * 128],
                                            r=[pk], w=["KVcT%d%d" % (ci, pr)])
                                    continue
                                if pr == 1 and _os.environ.get("ACTV") == "dve":
                                    S.v("tensor_copy", KVcT[ci][pr][:, tt * 128:(tt + 1) * 128], ps[bank][:, pr * 128:(pr + 1) * 128],
                                        r=[pk], w=["KVcT%d%d" % (ci, pr)])
                                    continue
                                if pr == 1 and _os.environ.get("ACTV") == "nokey":
                                    S.a("copy", KVcT[ci][pr][:, tt * 128:(tt + 1) * 128], ps[bank][:, pr * 128:(pr + 1) * 128],
                                        r=[pk], w=["KVcTx%d%d" % (ci, pr)])
                                    continue
                                eng = S.v if pr == 0 else S.a
                                mm = "tensor_copy" if pr == 0 else "copy"
                                eng(mm, KVcT[ci][pr][:, tt * 128:(tt + 1) * 128], ps[bank][:, pr * 128:(pr + 1) * 128],
                                    r=[pk], w=["KVcT%d%d" % (ci, pr)])
                        else:
                            for hh in range(4):
                                for c in range(8):
                                    S.p("matmul", ps[bank][0:64, hh * 128:(hh + 1) * 128],
                                        wkv[:, c, off + hh * 64: off + (hh + 1) * 64], hT[b][:, c, :],
                                        start=(c == 0), stop=(c == 7), r=["wkv", pf + "hT"], w=[pk])
                            if gn == "ksel":
                                for hh in range(4):
                                    eng = S.v if hh % 2 == 0 else S.a
                                    mm = "tensor_copy" if hh % 2 == 0 else "copy"
                                    eng(mm, KselT[hh][0:64, tt * 128:(tt + 1) * 128], ps[bank][0:64, hh * 128:(hh + 1) * 128],
                                        r=[pk], w=["KselT%d_%d" % (hh, tt)])
                            else:
                                S.a("copy", kwst[b][:].rearrange("p h n -> p (h n)"), ps[bank][0:64, :], r=[pk], w=[pf + "kwst"])
                                S.d("gpsimd", kwTs[tt], kwst[b][:], r=[pf + "kwst"], w=["kwTs%d" % tt])
                    if getattr(self, "stop", None) == "A1g":
                        S.flush()
                        return
                    tm = [(OKV, 512, 5), (OKV + 512, 512, 6), (OWIN, 512, 7), (OG, 48, 5), (OZ, 512, 6), (OZ + 512, 512, 7)]
                    for ti, (off, ncol, bank) in enumerate(tm):
                        pk = "ps%d" % bank
                        for c in range(8):
                            S.p("matmul", ps[bank][:, 0:ncol], hT[b][:, c, :], wkv[:, c, off:off + ncol],
                                start=(c == 0), stop=(c == 7), r=["wkv", pf + "hT"], w=[pk])
                        if ti == 0:
                            S.v("tensor_copy", kvt[b][:, 0:512], ps[bank][:, 0:512], r=[pk], w=["Akvt"])
                        elif ti == 1:
                            S.a("copy", kvt[b][:, 512:1024], ps[bank][:, 0:512], r=[pk], w=["Akvt"])
                            S.v("tensor_copy", Vsel[:, tt, :, 0:64],
                                ps[bank][:, 256:512].rearrange("p (h d) -> p h d", h=4), r=[pk], w=["Vsel_%d" % tt])
                            S.d("gpsimd", w["kv_out"][tt * 128:(tt + 1) * 128, :], kvt[b][:], r=["Akvt"], w=["kvout%d_%d" % (l, tt)])
                        elif ti == 2:
                            S.a("copy", wnt[b][:], ps[bank][:, 0:512], r=[pk], w=["Awnt"])
                            S.v("tensor_copy", vwst[b][:], ps[bank][:, 256:512], r=[pk], w=[pf + "vwst"])
                            S.d("gpsimd", vws[tt], vwst[b][:], r=[pf + "vwst"], w=["vws%d" % tt])
                            wr = min(512, T)
                            if tt * 128 >= T - wr:
                                r0 = tt * 128 - (T - wr)
                                S.d("gpsimd", w["win_out"][r0:r0 + 128, :], wnt[b][:], r=["Awnt"], w=["winout%d_%d" % (l, tt)])
                        elif ti == 3:
                            S.a("activation", gates[:, tt, :], ps[bank][:, 0:48], AF.Sigmoid, r=[pk], w=["gates_%d" % tt])
                        elif ti == 4:
                            S.a("activation", zt[b][:, 0:512], ps[bank][:, 0:512], AF.Silu, r=[pk], w=["Azt"])
                        else:
                            S.a("activation", zt[b][:, 512:1024], ps[bank][:, 0:512], AF.Silu, r=[pk], w=["Azt"])
                            S.d("gpsimd", zs[tt], zt[b][:], r=["Azt"], w=["zs%d" % tt])
                if getattr(self, "stop", None) == "A1":
                    S.flush()
                    return
                if sm is not None:
                    self.nsa_samp_A(l, sm, w, wkv, nw_bc, junk, ss, OKV, OWIN, OG, OZ)
                S.flush()
                A1.close()
                sba = sba_outer
                w1s = sba("w1s", [128, 32, 128], F32)
                w1b = [sba("w1b%d" % c, [128, 32, 128], BF16) for c in range(2)]
                w2s = sba("w2s", [128, 2, 64], F32)
                w2b = sba("w2b", [128, 2, 64], BF16)
                peTs = sba("peTs", [64, 2, 32], F32)
                peT = sba("peT", [64, 2, 32], BF16)
                hb = sba("hb", [128, 2], F32)
                b1c = sba("b1c", [128, 2], F32)
                ShT = sba("ShT", [128, 256], BF16)
                NCc = T // 16
                S.v("memset", ShT[:], 0.0, w=["ShT"])
                for c in range(2):
                    for half in range(2):
                        S.d("sync", w1s[half * 64:(half + 1) * 64, :, :], w["w1"][c].rearrange("r d e -> d r e"), w=["w1s"])
                    S.v("tensor_copy", w1b[c][:], w1s[:], r=["w1s"], w=["w1b%d" % c])
                S.d("sync", w2s[:], w["w2"].rearrange("c e d -> e c d"), w=["w2s"])
                S.v("tensor_copy", w2b[:], w2s[:], r=["w2s"], w=["w2b"])
                S.d("sync", peTs[:], w["pe"].rearrange("c r d -> d c r"), w=["peTs"])
                S.d("sync", b1c[:], w["b1"].rearrange("c e -> e c"), w=["b1c"])
                S.v("tensor_copy", peT[:], peTs[:], r=["peTs"], w=["peT"])
                for c in range(2):
                    for r_ in range(32):
                        S.p("matmul", ps[1][:, c:c + 1], w1b[c][0:64, r_, :], peT[:, c, r_:r_ + 1],
                            start=(r_ == 0), stop=(r_ == 31), r=["w1b%d" % c, "peT"], w=["ps1"])
                S.v("tensor_tensor", hb[:], ps[1][:, 0:2], b1c[:], ALU.add, r=["ps1", "b1c"], w=["hb"])
                if sm is not None:
                    S.v("tensor_copy", sm["hb"][:], hb[:], r=["hb"], w=["s_hb"])
                for c in range(2):
                    for hh in range(4):
                        pr, hl = hh // 2, hh % 2
                        bank = 2 + ((c * 4 + hh) % 2)
                        pk = "ps%d" % bank
                        src = KVcT[c][pr]
                        for r_ in range(32):
                            S.p("matmul", ps[bank][:, 0:NCc], w1b[c][hl * 64:(hl + 1) * 64, r_, :],
                                src[hl * 64:(hl + 1) * 64, r_:r_ + 16 * NCc:16],
                                start=(r_ == 0), stop=(r_ == 31), r=["w1b%d" % c, "KVcT%d%d" % (c, pr)], w=[pk])
                        S.a("activation", ShT[:, 0:NCc], ps[bank][:, 0:NCc], AF.Silu, bias=hb[:, c:c + 1], r=[pk, "hb"], w=["ShT"])
                        if c == 0:
                            S.p("matmul", ps[4][0:64, 0:256], w2b[:, 0, :], ShT[:], start=True, stop=True,
                                r=["w2b", "ShT"], w=["ps4"])
                            S.v("tensor_copy", KcT[hh][0:64, :], ps[4][0:64, 0:256], r=["ps4"], w=["KcT%d" % hh])
                        else:
                            for j in range(2):
                                S.p("matmul", ps[4][:, 256 + j * 64:256 + (j + 1) * 64], ShT[:, j * 128:(j + 1) * 128], w2b[:, 1, :],
                                    start=True, stop=True, r=["w2b", "ShT"], w=["ps4"])
                            S.v("tensor_copy", vc[:, :, hh, :], ps[4][:, 256:384].rearrange("p (j d) -> p j d", j=2),
                                r=["ps4"], w=["vc"])
                S.flush()

            if getattr(self, "stop", None) == "A":
                return
            with ExitStack() as Bk:
                def sbb(name, shape, dt):
                    return Bk.enter_context(nc.sbuf_tensor("%s_B%d" % (name, l), shape, dt))
                wq = sbb("wq", [128, 8, 1024], BF16)
                wo = sbb("wo", [128, 8, 1024], BF16)
                wst = [sbb("wstB%d" % i, [128, 1024], F32) for i in range(2)]
                k = 0
                for (dst, src, nm) in ((wq, w["w_in"], "wq"), (wo, w["w_out"], "wo")):
                    for c in range(8):
                        S.d("sync", wst[k % 2][:], src[c * 128:(c + 1) * 128, 0:1024], w=["wstB%d" % (k % 2)])
                        if k % 2 == 0:
                            S.v("tensor_copy", dst[:, c, :], wst[k % 2][:], r=["wstB%d" % (k % 2)], w=[nm])
                        else:
                            S.a("copy", dst[:, c, :], wst[k % 2][:], r=["wstB%d" % (k % 2)], w=[nm])
                        k += 1
                Eall = sbb("Eall", [NB, T], BF16)
                Mov = sbb("Mov", [128, 2, NB], BF16)
                S.d("sync", Eall[:], self.cin["Eall"], w=["Eall"])
                S.d("sync", Mov[:], self.cin["Mov"], w=["Mov"])
                QT = [[sbb("QT%d_%d" % (kk, i), [102, 4, 128], BF16) for i in range(2)] for kk in range(4)]
                for kk in range(4):
                    for i in range(2):
                        S.v("memset", QT[kk][i][64:96, :, :], 0.0, w=["QT%d_%d" % (kk, i)])
                        S.d("sync", QT[kk][i][96:102, :, :], self.cin["slope"][kk], w=["QT%d_%d" % (kk, i)])
                hT = [sbb("hTB%d" % i, [128, 8, 128], BF16) for i in range(2)]
                sz = [sbb("sz%d" % i, [128, 1024], BF16) for i in range(2)]
                xt = [sbb("xtB%d" % i, [128, 1024], F32) for i in range(2)]
                ksum = sbb("ksum", [102, 128], BF16)
                prod = sbb("prod", [102, 4, 128], BF16)
                Sm = sbb("Sm", [128, 4, 256], F32)
                mx = sbb("mx", [128, 16], F32)
                Pn = sbb("Pn", [128, 4, 256], BF16)
                PnT = sbb("PnT", [128, 2, 4, 128], BF16)
                sc = sbb("sc", [128, NB], F32)
                sc2 = sbb("sc2", [128, NB], F32)
                m8 = sbb("m8", [128, 16], F32)
                bb = sbb("bb", [128, NB], BF16)
                bbT = sbb("bbT", [NB, 4, 128], BF16)
                PT = [sbb("PT%d" % i, [128, 4, 128], BF16) for i in range(3)]
                acc = sbb("acc", [128, 16, 64], F32)
                gr = sbb("gr", [128, 8], F32)
                oz = sbb("oz", [128, 1024], BF16)
                ozT = sbb("ozT", [128, 8, 128], BF16)
                xo = sbb("xo", [128, 1024], F32)
                pti = 0
                for qb in range(NT):
                    b = qb % 2
                    pf = "B%d" % b
                    s0 = qb * 128
                    S.d("sync", hT[b][:].rearrange("p c n -> p (c n)"), hTs[qb], r=["hTs%d" % qb], w=[pf + "hT"])
                    S.d("sync", sz[b][:], zs[qb], r=["zs%d" % qb], w=[pf + "sz"])
                    S.d("sync", xt[b][:], xsrc[s0:s0 + 128, :], r=["xres_%d" % qb], w=[pf + "x"])
                    slot = qb % 6
                    S.d("sync", KwinR[0:64, :, slot * 128:(slot + 1) * 128], kwTs[qb], r=["kwTs%d" % qb], w=["KwinR_%d" % slot])
                    for hh in range(4):
                        S.d("sync", KwinR[96:102, hh, slot * 128:(slot + 1) * 128], self.cin["pos"][:, s0:s0 + 128], w=["KwinR_%d" % slot])
                    S.d("sync", VwinR[:, slot, :, 0:64], vws[qb].rearrange("p (h d) -> p h d", h=4), r=["vws%d" % qb], w=["VwinR_%d" % slot])
                    for kk in range(4):
                        qn = "QT%d_%d" % (kk, b)
                        Q = QT[kk][b]
                        Qf = Q[:].rearrange("p g n -> p (g n)")
                        bank = 1 + (kk % 2)
                        pk = "ps%d" % bank
                        for g_ in range(4):
                            hd = kk * 4 + g_
                            for c in range(8):
                                S.p("matmul", ps[bank][0:64, g_ * 128:(g_ + 1) * 128], wq[:, c, hd * 64:(hd + 1) * 64], hT[b][:, c, :],
                                    start=(c == 0), stop=(c == 7), r=["wq", pf + "hT"], w=[pk])
                        S.a("mul", Qf[0:64, :], ps[bank][0:64, :], HD ** -0.5, r=[pk], w=[qn])
                        S.v("tensor_tensor", ksum[:], KselT[kk][:, s0:s0 + 128], KwinR[:, kk, slot * 128:(slot + 1) * 128], ALU.add,
                            r=["KselT%d_%d" % (kk, qb), "KwinR_%d" % slot], w=["ksum"])
                        for g_ in range(4):
                            S.v("scalar_tensor_tensor", prod[:, g_, :], Q[:, g_, :], 0.5, ksum[:], ALU.mult, ALU.mult,
                                r=[qn, "ksum"], w=["prod"])
                        S.p("matmul", ps[3][:, :], self.selc[:, :], prod[:].rearrange("p g n -> p (g n)"), start=True, stop=True,
                            r=["selc", "prod"], w=["ps3"])
                        S.a("activation", Qf[64:65, :], ps[3][64:65, :], AF.Identity, scale=-1.0, bias=self.negm[64:65, 0:1],
                            r=["ps3", "negm"], w=[qn])
                        ncch = min(2, (8 * qb + 8 + 127) // 128)
                        NCv = ncch * 128
                        for g_ in range(4):
                            bank = 3 + g_ // 2
                            S.p("matmul", ps[bank][:, (g_ % 2) * 256:(g_ % 2) * 256 + NCv], Q[:, g_, :], KcT[kk][:, 0:NCv],
                                start=True, stop=True, r=[qn, "KcT%d" % kk], w=["ps%d" % bank])
                        moff = 256 - 8 * qb
                        for half in range(2):
                            bank = 3 + half
                            S.v("tensor_tensor", Sm[:, 2 * half:2 * half + 2, 0:NCv],
                                ps[bank][:].rearrange("p (g n) -> p g n", g=2)[:, :, 0:NCv],
                                bcast(self.maskc[:, moff:moff + NCv], 1, 2), ALU.add,
                                r=["ps%d" % bank, "maskc"], w=["Sm"])
                        S.v("tensor_reduce", mx[:, 0:4], Sm[:, :, 0:NCv], AX.X, ALU.max, r=["Sm"], w=["mx"])
                        S.v("tensor_scalar", mx[:, 4:8], mx[:, 0:4], -10000.0, -1.0, ALU.max, ALU.mult, r=["mx"], w=["mx"])
                        for g_ in range(4):
                            S.a("activation", Sm[:, g_, 0:NCv], Sm[:, g_, 0:NCv], AF.Exp, bias=mx[:, 4 + g_:5 + g_],
                                accum_out=mx[:, 8 + g_:9 + g_], r=["Sm", "mx"], w=["Sm", "mx"])
                        S.v("tensor_scalar", mx[:, 12:16], mx[:, 8:12], 1e-30, None, ALU.max, r=["mx"], w=["mx"])
                        S.v("reciprocal", mx[:, 12:16], mx[:, 12:16], r=["mx"], w=["mx"])
                        S.v("tensor_tensor", Pn[:, :, 0:NCv], Sm[:, :, 0:NCv], bcast(mx[:, 12:16], 2, NCv), ALU.mult,
                            r=["Sm", "mx"], w=["Pn"])
                        pb = self.psb(0)
                        for j in range(ncch):
                            for g_ in range(4):
                                S.p("transpose", pb[:, (j * 4 + g_) * 128:(j * 4 + g_ + 1) * 128], Pn[:, g_, j * 128:(j + 1) * 128],
                                    self.identb[:], r=["Pn", "identb"], w=["ps0"])
                        S.v("tensor_copy", PnT[:, 0:ncch, :, :].rearrange("p j g n -> p (j g n)"), pb[:, 0:ncch * 512],
                            r=["ps0"], w=["PnT"])
                        first = True
                        for g_ in range(4):
                            for j in range(ncch):
                                S.p("matmul", ps[5][:, g_ * 64:(g_ + 1) * 64], PnT[:, j, g_, :], vc[:, j, kk, :],
                                    start=first, stop=False, skip_group_check=True, r=["PnT", "vc"], w=["ps5"])
                                first = False
                                S.p("matmul", ps[5][:, 256:256 + NB], PnT[:, j, g_, :], Mov[:, j, :],
                                    start=False, stop=(g_ == 3 and j == ncch - 1), skip_group_check=True,
                                    r=["PnT", "Mov"], w=["ps5"])
                        gcol = gates[:, qb, :].rearrange("p (h t) -> p h t", t=3)
                        S.v("tensor_tensor", acc[:, kk * 4:(kk + 1) * 4, :], ps[5][:, 0:256].rearrange("p (g d) -> p g d", g=4),
                            bcast(gcol[:, kk * 4:(kk + 1) * 4, 0], 2, 64), ALU.mult,
                            r=["ps5", "gates_%d" % qb], w=["acc%d" % kk])
                        aoff = 64 - 2 * qb
                        S.v("tensor_tensor", sc[:], ps[5][:, 256:256 + NB], self.allow[:, aoff:aoff + NB], ALU.mult,
                            r=["ps5", "allow"], w=["sc"])
                        S.v("tensor_tensor", sc[:], sc[:], self.allowm1[:, aoff:aoff + NB], ALU.add, r=["sc", "allowm1"], w=["sc"])
                        S.v("tensor_tensor", sc[:], sc[:], self.force[:, aoff:aoff + NB], ALU.max, r=["sc", "force"], w=["sc"])
                        S.v("memset", sc[:, 0:1], 1.0e4, w=["sc"])
                        S.v("max", m8[:, 0:8], sc[:], r=["sc"], w=["m8"])
                        S.v("match_replace", sc2[:], m8[:, 0:8], sc[:], -1.0e30, r=["sc", "m8"], w=["sc2"])
                        S.v("max", m8[:, 8:16], sc2[:], r=["sc2"], w=["m8"])
                        S.v("tensor_scalar", sc2[:], sc[:], m8[:, 15:16], None, ALU.is_ge, r=["sc", "m8"], w=["sc2"])
                        S.v("scalar_tensor_tensor", sc2[:], sc[:], -0.5, sc2[:], ALU.is_gt, ALU.mult, r=["sc", "sc2"], w=["sc2"])
                        S.v("tensor_scalar", bb[:], sc2[:], BIG, -BIG, ALU.mult, ALU.add, r=["sc2"], w=["bb"])
                        S.p("transpose", pb[0:NB, 0:128], bb[:], self.identb[:], r=["bb", "identb"], w=["ps0"])
                        S.v("tensor_copy", bbT[:], bcast(pb[0:NB, 0:128], 1, 4), r=["ps0"], w=["bbT"])
                        bbTf = bbT[:].rearrange("p g n -> p (g n)")
                        for (br, obank, gi) in (("sel", 6, 1), ("win", 7, 2)):
                            kcs = list(range(0, qb + 1)) if br == "sel" else list(range(max(0, qb - 4), qb + 1))
                            ok = "ps%d" % obank
                            slots = []

                            def emit_qk(ci, kc, pti=pti, slots=slots, br=br, kk=kk, qb=qb):
                                sbank = 1 + ((pti + ci) % 2)
                                sk = "ps%d" % sbank
                                P_ = PT[(pti + ci) % 3]
                                pn = "PT%d" % ((pti + ci) % 3)
                                extra = []
                                if br == "sel":
                                    extra.append((Eall[:, kc * 128:(kc + 1) * 128], bbTf, ["Eall", "bbT"]))
                                if kc == qb:
                                    extra.append((self.identb[:], self.causal[:].rearrange("p g n -> p (g n)"), ["identb", "causal"]))
                                if br == "win" and kc == qb - 4:
                                    extra.append((self.identb[:], self.far[:].rearrange("p g n -> p (g n)"), ["identb", "far"]))
                                if br == "sel":
                                    Kap, kkey = KselT[kk][:, kc * 128:(kc + 1) * 128], "KselT%d_%d" % (kk, kc)
                                    Vap, vkey = Vsel[:, kc, kk, :], "Vsel_%d" % kc
                                else:
                                    sl_ = kc % 6
                                    Kap, kkey = KwinR[:, kk, sl_ * 128:(sl_ + 1) * 128], "KwinR_%d" % sl_
                                    Vap, vkey = VwinR[:, sl_, kk, :], "VwinR_%d" % sl_
                                S.p("matmul", ps[sbank][:, :], Kap, Qf, start=True, stop=(len(extra) == 0),
                                    r=[kkey, qn], w=[sk])
                                for ei, (lt, rh, rr) in enumerate(extra):
                                    S.p("matmul", ps[sbank][:, :], lt, rh, start=False, stop=(ei == len(extra) - 1), r=rr, w=[sk])
                                slots.append((sbank, sk, P_, pn, Vap, vkey))

                            emit_qk(0, kcs[0])
                            for ci, kc in enumerate(kcs):
                                if ci + 1 < len(kcs):
                                    emit_qk(ci + 1, kcs[ci + 1])
                                sbank, sk, P_, pn, Vap, vkey = slots[ci]
                                S.a("activation", P_[:].rearrange("p g n -> p (g n)"), ps[sbank][:, :], AF.Exp, r=[sk], w=[pn])
                                for g_ in range(4):
                                    S.p("matmul", ps[obank][:, g_ * 65:(g_ + 1) * 65], P_[:, g_, :], Vap,
                                        start=(ci == 0 and g_ == 0), stop=(ci == len(kcs) - 1 and g_ == 3), skip_group_check=True,
                                        r=[pn, vkey], w=[ok])
                            pti += len(kcs)
                            ov = ps[obank][:, 0:260].rearrange("p (g d) -> p g d", g=4)
                            S.v("reciprocal", gr[:, 0:4], ov[:, :, 64], r=[ok], w=["gr"])
                            S.v("tensor_tensor", gr[:, 4:8], gr[:, 0:4], gcol[:, kk * 4:(kk + 1) * 4, gi], ALU.mult,
                                r=["gr", "gates_%d" % qb], w=["gr"])
                            for g_ in range(4):
                                S.v("scalar_tensor_tensor", acc[:, kk * 4 + g_, :], ov[:, g_, 0:64], gr[:, 4 + g_:5 + g_],
                                    acc[:, kk * 4 + g_, :], ALU.mult, ALU.add, r=[ok, "gr", "acc%d" % kk], w=["acc%d" % kk])
                    S.v("tensor_tensor", oz[:], acc[:].rearrange("p h d -> p (h d)"), sz[b][:], ALU.mult,
                        r=["acc0", "acc1", "acc2", "acc3", pf + "sz"], w=["oz"])
                    pb = self.psb(0)
                    for c in range(8):
                        S.p("transpose", pb[:, c * 128:(c + 1) * 128], oz[:, c * 128:(c + 1) * 128], self.identb[:],
                            r=["oz", "identb"], w=["ps0"])
                    S.a("copy", ozT[:].rearrange("p c n -> p (c n)"), pb[:, 0:1024], r=["ps0"], w=["ozT"])
                    for half in range(2):
                        bank = 1 + half
                        for c in range(8):
                            S.p("matmul", ps[bank][:, :], ozT[:, c, :], wo[:, c, half * 512:(half + 1) * 512],
                                start=(c == 0), stop=(c == 7), r=["ozT", "wo"], w=["ps%d" % bank])
                        S.v("tensor_tensor", xo[:, half * 512:(half + 1) * 512], ps[bank][:, :], xt[b][:, half * 512:(half + 1) * 512],
                            ALU.add, r=["ps%d" % bank, pf + "x"], w=["xo"])
                    S.d("gpsimd", xdst[s0:s0 + 128, :], xo[:], r=["xo"], w=["xres_%d" % qb])
                if sm is not None:
                    self.nsa_samp_Q(sm, wq)
                S.flush()
            Lp.close()
            if sm is not None:
                self.nsa_samp_S(l, sm, w)


T_FULL = 4096
P_FULL = 8192
NPOOL_FULL = 2560
_CACHE = {}


def final_norm(self, xsrc, fw, ydst, nrows_total, pfx):
    nc, S = self.nc, self.S
    with ExitStack() as F:
        def sb(name, shape, dt):
            return F.enter_context(nc.sbuf_tensor("%s_f%s" % (name, pfx), shape, dt))
        fw_bc = sb("fw_bc", [128, 1024], F32)
        eps = sb("eps", [128, 1], F32)
        S.v("memset", eps[:], RMS_EPS, w=["f_eps" + pfx])
        S.d("sync", fw_bc[:], fw.partition_broadcast(128), w=["f_fw" + pfx])
        xt = [sb("xt%d" % i, [128, 1024], F32) for i in range(2)]
        yt = [sb("yt%d" % i, [128, 1024], F32) for i in range(2)]
        jk = sb("jk", [128, 1024], BF16)
        st_ = sb("st", [128, 4], F32)
        ntile = (nrows_total + 127) // 128
        for tt in range(ntile):
            b = tt % 2
            nr = min(128, nrows_total - tt * 128)
            rkey = ["xres_%d" % tt] if pfx == "p" else ["xsres"]
            S.d("sync", xt[b][0:nr, :], xsrc[tt * 128:tt * 128 + nr, :], r=rkey, w=["f_x%s%d" % (pfx, b)])
            S.a("activation", jk[0:nr, :], xt[b][0:nr, :], AF.Square, accum_out=st_[0:nr, 0:1], r=["f_x%s%d" % (pfx, b)], w=["f_jk" + pfx, "f_st" + pfx])
            S.a("activation", st_[0:nr, 1:2], st_[0:nr, 0:1], AF.Sqrt, scale=1.0 / D_MODEL, bias=eps[0:nr, 0:1], r=["f_st" + pfx, "f_eps" + pfx], w=["f_st1" + pfx])
            S.v("reciprocal", st_[0:nr, 2:3], st_[0:nr, 1:2], r=["f_st1" + pfx], w=["f_st2" + pfx])
            S.v("scalar_tensor_tensor", yt[b][0:nr, :], xt[b][0:nr, :], st_[0:nr, 2:3], fw_bc[0:nr, :], ALU.mult, ALU.mult,
                r=["f_x%s%d" % (pfx, b), "f_st2" + pfx, "f_fw" + pfx], w=["f_y%s%d" % (pfx, b)])
            S.d("gpsimd", ydst[tt * 128:tt * 128 + nr, :], yt[b][0:nr, :], r=["f_y%s%d" % (pfx, b)], w=["f_out%s%d" % (pfx, tt)])
        S.flush()


Builder.final_norm = final_norm


def build_full(T=T_FULL, P=P_FULL, NPOOL=NPOOL_FULL):
    key = (T, P, NPOOL)
    if key in _CACHE:
        return _CACHE[key]
    B = Builder(T, TS=P + 128, P=P)
    nc = B.nc
    NPG = P // 128
    xp = B.din("xp", [T, 1024])
    xs = B.din("xs", [4, 1024])
    pools = [B.din("pool%d" % i, [NPOOL * 256, 512]) for i in range(2)]
    cwin = B.din("cwin", [2, 4, 512, 512])
    sC = B.din("sC", [2, 4, 4, 512, 512])
    sn = B.din("sn", [2, 4, 2048])
    smm = B.din("smm", [2, 4, 4])
    sconv = B.din("sconv", [2, 4, 3, 2048])
    pt = B.din("pt", [4, NPG], I32)
    norm_w = B.din("norm_w", [4, 1024])
    fnw = B.din("final_norm_w", [1, 1024])
    nsa_w_in = B.din("nsa_w_in", [2, 1024, NSA_IN])
    nsa_w_out = B.din("nsa_w_out", [2, 1024, 1024])
    pe = B.din("nsa_cmp_pe", [2, 2, 32, 64])
    w1 = B.din("nsa_cmp_w1", [2, 2, 32, 64, 128])
    b1 = B.din("nsa_cmp_b1", [2, 2, 128])
    w2 = B.din("nsa_cmp_w2", [2, 2, 128, 64])
    m_w_in = B.din("m_w_in", [2, 1024, 4096])
    m_conv_w = B.din("m_conv_w", [2, 4, 2048])
    m_conv_b = B.din("m_conv_b", [2, 2048])
    m_w_qkv = B.din("m_w_qkv", [2, 3, 512, 4, 4])
    m_w_gate = B.din("m_w_gate", [2, 6144, 8])
    m_b_gate = B.din("m_b_gate", [2, 8])
    m_norm_w = B.din("m_norm_w", [2, 2048])
    m_skip = B.din("m_skip", [2, 2048])
    m_w_out = B.din("m_w_out", [2, 2048, 1024])
    y_p = B.dout("y_p", [T, 1024])
    y_s = B.dout("y_s", [4, 1024])
    kv_p = B.dout("kv_p", [2, T, 1024])
    kv_s = B.dout("kv_s", [2, 4, 1024])
    WR = min(512, T)
    win_p = B.dout("win_p", [2, WR, 512])
    win_s = B.dout("win_s", [2, 4, 512, 512])
    C_p = B.dout("C_p", [2, 4, 512, 512])
    C_s = B.dout("C_s", [2, 4, 4, 512, 512])
    n_p = B.dout("n_p", [2, 2048])
    n_s = B.dout("n_s", [2, 4, 2048])
    m_p = B.dout("m_p", [2, 4])
    m_s = B.dout("m_s", [2, 4, 4])
    cv_p = B.dout("cv_p", [2, 3, 2048])
    cv_s = B.dout("cv_s", [2, 4, 3, 2048])
    xres = B.dscr("xres", [T, 1024])
    xsres = B.dscr("xsres", [4, 1024])
    with ExitStack() as st:
        st.enter_context(nc.allow_non_contiguous_dma(reason="small transposed constant loads"))
        B.setup(st)
        for li in range(4):
            l = li // 2
            xsrc = xp if li == 0 else xres
            xssrc = xs if li == 0 else xsres
            if li % 2 == 0:
                w = {"norm_w": norm_w[li:li + 1, :], "w_in": nsa_w_in[l], "w_out": nsa_w_out[l], "pe": pe[l], "w1": w1[l],
                     "b1": b1[l], "w2": w2[l], "kv_out": kv_p[l], "win_out": win_p[l],
                     "samp": {"xs_src": xssrc, "xs_dst": xsres, "kv_out": kv_s[l], "win_out": win_s[l], "cwin": cwin[l],
                              "pool2": pools[l], "pt": pt}}
                B.nsa_layer(li, xsrc, xres, w)
            else:
                w = {"norm_w": norm_w[li:li + 1, :], "w_in": m_w_in[l], "w_out": m_w_out[l], "conv_w": m_conv_w[l],
                     "conv_b": m_conv_b[l], "w_qkv": m_w_qkv[l], "w_gate": m_w_gate[l], "b_gate": m_b_gate[l],
                     "mnorm_w": m_norm_w[l], "skip": m_skip[l], "C_out": C_p[l], "n_out": n_p[l], "m_out": m_p[l],
                     "conv_out": cv_p[l],
                     "samp": {"xs_src": xssrc, "xs_dst": xsres, "conv_in": sconv[l], "conv_out": cv_s[l], "C_in": sC[l],
                              "C_out": C_s[l], "n_in": sn[l], "n_out": n_s[l], "m_in": smm[l], "m_out": m_s[l]}}
                B.mlstm_layer(li, xsrc, xres, w)
        B.final_norm(xres, fnw, y_p, T, "p")
        B.final_norm(xsres, fnw, y_s, 4, "s")
        B.S.flush(final=True)
    _CACHE[key] = B
    return B


def make_in_maps(B, T, P, NPOOL, inp):
    f = np.float32
    cst = make_consts(T, P + 128, P)
    g = {k: np.asarray(v) for k, v in inp.items()}
    NB_ = g["x_prompt"].shape[0]
    shared = {"norm_w": g["norm_w"].astype(f), "final_norm_w": g["final_norm_w"].astype(f).reshape(1, 1024),
              "nsa_w_in": g["nsa_w_in"].astype(f), "nsa_w_out": g["nsa_w_out"].astype(f), "nsa_cmp_pe": g["nsa_cmp_pe"].astype(f),
              "nsa_cmp_w1": g["nsa_cmp_w1"].astype(f), "nsa_cmp_b1": g["nsa_cmp_b1"].astype(f), "nsa_cmp_w2": g["nsa_cmp_w2"].astype(f),
              "m_w_in": g["m_w_in"].astype(f), "m_conv_w": g["m_conv_w"].astype(f), "m_conv_b": g["m_conv_b"].astype(f),
              "m_w_qkv": g["m_w_qkv"].astype(f), "m_w_gate": g["m_w_gate"].astype(f), "m_b_gate": g["m_b_gate"].astype(f),
              "m_norm_w": g["m_norm_w"].astype(f), "m_skip": g["m_skip"].astype(f), "m_w_out": g["m_w_out"].astype(f)}
    ckv = g["cache_kv"].astype(f, copy=False)
    shared["pool0"] = ckv[0].reshape(NPOOL * 256, 512)
    shared["pool1"] = ckv[1].reshape(NPOOL * 256, 512)
    for k, v in cst.items():
        shared["c_" + k] = v
    in_maps = []
    for c in range(8):
        b = c % NB_
        ss_ = slice(4 * c, 4 * c + 4)
        m = dict(shared)
        m["xp"] = np.ascontiguousarray(g["x_prompt"][b].astype(f))
        m["xs"] = np.ascontiguousarray(g["x_sample"][ss_, 0].astype(f))
        m["cwin"] = np.ascontiguousarray(g["cache_win"][:, ss_].astype(f)).reshape(2, 4, 512, 512)
        m["sC"] = np.ascontiguousarray(g["state_C"][:, ss_].astype(f))
        m["sn"] = np.ascontiguousarray(g["state_n"][:, ss_].astype(f)).reshape(2, 4, 2048)
        m["smm"] = np.ascontiguousarray(g["state_m"][:, ss_].astype(f))
        m["sconv"] = np.ascontiguousarray(g["state_conv"][:, ss_].astype(f))
        m["pt"] = np.ascontiguousarray(g["page_table"][ss_].astype(np.int32))
        in_maps.append(m)
    return in_maps


def assemble(R, T, NB_=4):
    f = np.float32
    cat = lambda nm, ax: np.concatenate([R[c][nm] for c in range(8)], axis=ax)
    y_prompt = np.stack([R[b]["y_p"] for b in range(NB_)]).astype(f)
    y_sample = cat("y_s", 0).reshape(32, 1, 1024).astype(f)
    kv_prompt = np.stack([R[b]["kv_p"] for b in range(NB_)], 1).reshape(2, NB_, T, 4, 4, 64).astype(f)
    kv_sample = cat("kv_s", 1).reshape(2, 32, 1, 4, 4, 64).astype(f)
    WR = R[0]["win_p"].shape[1]
    win_prompt = np.stack([R[b]["win_p"] for b in range(NB_)], 1).reshape(2, NB_, WR, 2, 4, 64).astype(f)
    win_sample = cat("win_s", 1).reshape(2, 32, 512, 2, 4, 64).astype(f)
    C_p = np.stack([R[b]["C_p"] for b in range(NB_)], 1).astype(f)
    C_s = cat("C_s", 1).astype(f)
    n_p = np.stack([R[b]["n_p"] for b in range(NB_)], 1).reshape(2, NB_, 4, 512).astype(f)
    n_s = cat("n_s", 1).reshape(2, 32, 4, 512).astype(f)
    m_p = np.stack([R[b]["m_p"] for b in range(NB_)], 1).astype(f)
    m_s = cat("m_s", 1).astype(f)
    cv_p = np.stack([R[b]["cv_p"] for b in range(NB_)], 1).astype(f)
    cv_s = cat("cv_s", 1).astype(f)
    return (y_prompt, y_sample, kv_prompt, kv_sample, win_prompt, win_sample, C_p, C_s, n_p, n_s, m_p, m_s, cv_p, cv_s)


def kernel(**inputs):
    B = build_full()
    in_maps = make_in_maps(B, T_FULL, P_FULL, NPOOL_FULL, inputs)
    res = run_bass_kernel_spmd(B.nc, in_maps, core_ids=list(range(8)))
    return assemble(res.results, T_FULL)


def mlstm_layer(self, l, xsrc, xdst, w):
    nc, S, T, NT = self.nc, self.S, self.T, self.NT
    ps = self.ps
    SC = MHD ** -0.5
    poTs = self.dscr("poTs_l%d" % l, [NT, 128, 2048], BF16)
    hTs = self.dscr("mhTs_l%d" % l, [NT, 128, 1024], BF16)
    with ExitStack() as L:
        def sb(name, shape, dt):
            return L.enter_context(nc.sbuf_tensor("%s_m%d" % (name, l), shape, dt))
        nw_bc = sb("nw_bc", [128, 1024], F32)
        self.h_bf = sb("h_bf", [128, 1024], BF16)
        self.epsc = sb("epsc", [128, 1], F32)
        junk = sb("junk", [128, 1024], BF16)
        ss = sb("ss", [128, 4], F32)
        S.v("memset", self.epsc[:], RMS_EPS, w=["epsc"])
        S.d("sync", nw_bc[:], w["norm_w"].partition_broadcast(128), w=["nw_bc"])
        sm = None
        if "samp" in w:
            sm = {"xs": sb("s_xs", [4, 1024], F32), "hT": sb("s_hT", [128, 8, 4], BF16), "po": sb("s_po", [128, 16, 4], BF16),
                  "onec": sb("s_onec", [128, 1], F32)}
            S.v("memset", sm["onec"][:], 1.0, w=["s_onec"])
        with ExitStack() as P1:
            def s1(name, shape, dt):
                return P1.enter_context(nc.sbuf_tensor("%s_p%d" % (name, l), shape, dt))
            wx = s1("wx", [128, 8, 2048], BF16)
            wst = [s1("wst%d" % i, [128, 1024], F32) for i in range(2)]
            k = 0
            for c in range(8):
                for hf in range(2):
                    S.d("sync", wst[k % 2][:], w["w_in"][c * 128:(c + 1) * 128, hf * 1024:(hf + 1) * 1024], w=["mwst%d" % (k % 2)])
                    if k % 2 == 0:
                        S.v("tensor_copy", wx[:, c, hf * 1024:(hf + 1) * 1024], wst[k % 2][:], r=["mwst%d" % (k % 2)], w=["wx"])
                    else:
                        S.a("copy", wx[:, c, hf * 1024:(hf + 1) * 1024], wst[k % 2][:], r=["mwst%d" % (k % 2)], w=["wx"])
                    k += 1
            C = [s1("C%d" % h, [128, 4, 512], F32) for h in range(4)]
            ncol = s1("ncol", [128, 4, 4], F32)
            for h in range(4):
                S.v("memset", C[h][:], 0.0, w=["C%d" % h])
            S.v("memset", ncol[:], 0.0, w=["ncol"])
            cw = s1("cw", [128, 16, 4], F32)
            cb = s1("cb", [128, 16], F32)
            nwc = s1("nwc", [128, 16], F32)
            skc = s1("skc", [128, 16], F32)
            for j_ in range(4):
                S.d("sync", cw[:, :, j_], w["conv_w"][j_].rearrange("(c p) -> p c", p=128), w=["cw"])
            S.d("sync", cb[:], w["conv_b"].rearrange("(c p) -> p c", p=128), w=["cb"])
            S.d("sync", nwc[:], w["mnorm_w"].rearrange("(c p) -> p c", p=128), w=["nwc"])
            S.d("sync", skc[:], w["skip"].rearrange("(c p) -> p c", p=128), w=["skc"])
            wrow = s1("wrow", [128, 3, 16, 4], F32)
            S.d("sync", wrow[:], w["w_qkv"].rearrange("x (c n) j i -> (n j) x c i", c=16), w=["wrow"])
            Wblk = s1("Wblk", [128, 3, 16, 128], BF16)
            for x_ in range(3):
                for cc in range(16):
                    S.v("tensor_tensor", Wblk[:, x_, cc, :].rearrange("p (n i) -> p n i", i=4),
                        bcast(wrow[:, x_, cc, :], 1, 32), self.maskbd[:].rearrange("p (n i) -> p n i", i=4), ALU.mult,
                        r=["wrow", "maskbd"], w=["Wblk"])
            wgs = s1("wgs", [128, 48, 8], F32)
            Wg = s1("Wg", [128, 48, 8], BF16)
            S.d("sync", wgs[:], w["w_gate"].rearrange("(k p) g -> p k g", p=128), w=["wgs"])
            S.v("tensor_copy", Wg[:], wgs[:], r=["wgs"], w=["Wg"])
            bg = s1("bg", [4, 2], F32)
            S.d("sync", bg[:], w["b_gate"].rearrange("(x g) -> g x", x=2), w=["bg"])
            ones4 = s1("ones4", [4, 128], F32)
            S.v("memset", ones4[:], 1.0, w=["ones4"])
            onec = s1("onec", [128, 1], F32)
            S.v("memset", onec[:], 1.0, w=["onec"])
            Bprev = s1("Bprev", [4, 1], F32)
            uprev = s1("uprev", [4, 1], F32)
            upb = s1("upb", [128, 4], F32)
            S.v("memset", Bprev[:], 0.0, w=["Bprev"])
            S.v("memset", uprev[:], -1.0e30, w=["uprev"])
            S.v("memset", upb[:], -1.0e30, w=["upb"])
            P1a = ExitStack()
            s1_keep = s1

            def s1(name, shape, dt):
                return P1a.enter_context(nc.sbuf_tensor("%s_pa%d" % (name, l), shape, dt))
            xmT = s1("xmT", [128, 16, 131], F32)
            S.v("memset", xmT[:, :, 0:3], 0.0, w=["xmT"])
            xt = s1("xt", [128, 1024], F32)
            hT = s1("hT", [128, 8, 128], BF16)
            xmb = s1("xmb", [128, 16, 128], BF16)
            cv = s1("cv", [128, 16, 128], F32)
            cT = s1("cT", [128, 16, 128], BF16)
            csT = s1("csT", [128, 16, 128], BF16)
            qT = s1("qT", [128, 16, 128], BF16)
            kT = s1("kT", [128, 16, 128], BF16)
            vTc = s1("vTc", [128, 4, 128], BF16)
            ktok = s1("ktok", [128, 2048], BF16)
            vtok = s1("vtok", [128, 2048], BF16)
            gsb = s1("gsb", [4, 8, 128], F32)
            ngu = s1("ngu", [4, 128], F32)
            acol = s1("acol", [128, 8], F32)
            Eb = s1("Eb", [128, 4, 128], F32)
            GT = s1("GT", [128, 4, 128], F32)
            dbc = s1("dbc", [128, 4, 128], F32)
            StT = s1("StT", [128, 4, 128], BF16)
            qdT = [s1("qdT%d" % i, [128, 4, 128], F32) for i in range(2)]
            dn = s1("dn", [128, 16], F32)
            hn = s1("hn", [128, 512], BF16)
            poT = s1("poT", [128, 16, 128], BF16)
            kw = s1("kw", [128, 512], BF16)
            for tt in range(NT):
                S.d("sync", xt[:], xsrc[tt * 128:(tt + 1) * 128, :], r=["xres_%d" % tt], w=["Mx"])
                self.front(xt, nw_bc, hT, junk, ss, "M")
                S.d("gpsimd", hTs[tt], hT[:].rearrange("p c n -> p (c n)"), r=["MhT"], w=["mhTs%d" % tt])
                for g4 in range(4):
                    bank = 1 + g4 % 2
                    for j in range(4):
                        cc = g4 * 4 + j
                        for c in range(8):
                            S.p("matmul", ps[bank][:, j * 128:(j + 1) * 128], wx[:, c, cc * 128:(cc + 1) * 128], hT[:, c, :],
                                start=(c == 0), stop=(c == 7), r=["wx", "MhT"], w=["ps%d" % bank])
                    S.v("tensor_copy", xmT[:, g4 * 4:(g4 + 1) * 4, 3:131], ps[bank][:].rearrange("p (j n) -> p j n", j=4),
                        r=["ps%d" % bank], w=["xmT"])
                S.v("tensor_copy", xmb[:], xmT[:, :, 3:131], r=["xmT"], w=["xmb"])
                S.v("tensor_tensor", cv[:], xmT[:, :, 0:128], bcast(cw[:, :, 0], 2, 128), ALU.mult, r=["xmT", "cw"], w=["cv"])
                for j in range(1, 4):
                    S.v("tensor_tensor", cT[:], xmT[:, :, j:j + 128], bcast(cw[:, :, j], 2, 128), ALU.mult, r=["xmT", "cw"], w=["cT"])
                    S.v("tensor_tensor", cv[:], cv[:], cT[:], ALU.add, r=["cv", "cT"], w=["cv"])
                S.v("tensor_tensor", cv[:], cv[:], bcast(cb[:], 2, 128), ALU.add, r=["cv", "cb"], w=["cv"])
                if tt == NT - 1:
                    for j_ in range(3):
                        S.d("gpsimd", w["conv_out"][j_].rearrange("(c p) -> p c", p=128), xmT[:, :, 128 + j_], r=["xmT"], w=["convout%d_%d" % (l, j_)])
                S.v("tensor_copy", xmT[:, :, 0:3], xmT[:, :, 128:131], r=["xmT"], w=["xmT"])
                S.a("activation", cT[:], cv[:], AF.Silu, r=["cv"], w=["cT"])
                S.v("tensor_tensor", csT[:], cT[:], bcast(skc[:], 2, 128), ALU.mult, r=["cT", "skc"], w=["csT"])
                for (x_, dst, nm) in ((0, qT, "qT"), (1, kT, "kT")):
                    for g4 in range(4):
                        bank = 1 + g4 % 2
                        for j in range(4):
                            cc = g4 * 4 + j
                            S.p("matmul", ps[bank][:, j * 128:(j + 1) * 128], Wblk[:, x_, cc, :], cT[:, cc, :], start=True, stop=True,
                                r=["Wblk", "cT"], w=["ps%d" % bank])
                        S.a("copy", dst[:, g4 * 4:(g4 + 1) * 4, :], ps[bank][:].rearrange("p (j n) -> p j n", j=4),
                            r=["ps%d" % bank], w=[nm])
                for (x_, src, dst, nm) in ((1, cT, ktok, "ktok"), (2, xmb, vtok, "vtok")):
                    for g4 in range(4):
                        bank = 1 + g4 % 2
                        for j in range(4):
                            cc = g4 * 4 + j
                            S.p("matmul", ps[bank][:, j * 128:(j + 1) * 128], src[:, cc, :], Wblk[:, x_, cc, :], start=True, stop=True,
                                r=["Wblk", "cT", "xmb"], w=["ps%d" % bank])
                        S.a("copy", dst[:, g4 * 512:(g4 + 1) * 512], ps[bank][:], r=["ps%d" % bank], w=[nm])
                for gi in range(2):
                    first = True
                    for (x_, src, nm) in ((0, qT, "qT"), (1, kT, "kT")):
                        for cc in range(16):
                            S.p("matmul", ps[3][0:4, gi * 128:(gi + 1) * 128], Wg[:, x_ * 16 + cc, gi * 4:(gi + 1) * 4], src[:, cc, :],
                                start=(first and gi == 0), stop=False, skip_group_check=True, r=["Wg", nm], w=["ps3"])
                            first = False
                for g4 in range(4):
                    bank = 1 + g4 % 2
                    for j in range(4):
                        cc = g4 * 4 + j
                        S.p("matmul", ps[bank][:, j * 128:(j + 1) * 128], Wblk[:, 2, cc, :], xmb[:, cc, :], start=True, stop=True,
                            r=["Wblk", "xmb"], w=["ps%d" % bank])
                    S.v("tensor_copy", vTc[:], ps[bank][:].rearrange("p (j n) -> p j n", j=4), r=["ps%d" % bank], w=["vTc"])
                    for gi in range(2):
                        for j in range(4):
                            cc = g4 * 4 + j
                            S.p("matmul", ps[3][0:4, gi * 128:(gi + 1) * 128], Wg[:, 32 + cc, gi * 4:(gi + 1) * 4], vTc[:, j, :],
                                start=False, stop=(g4 == 3 and j == 3), skip_group_check=True, r=["Wg", "vTc"], w=["ps3"])
                S.a("activation", gsb[:, 0, :], ps[3][0:4, 0:128], AF.Identity, bias=bg[:, 0:1], r=["ps3", "bg"], w=["gsb"])
                S.a("activation", gsb[:, 1, :], ps[3][0:4, 128:256], AF.Identity, bias=bg[:, 1:2], r=["ps3", "bg"], w=["gsb"])
                S.v("tensor_scalar", gsb[:, 2, :], gsb[:, 1, :], -1.0, None, ALU.mult, r=["gsb"], w=["gsb"])
                S.v("tensor_tensor", gsb[:, 2, :], gsb[:, 2, :], gsb[:, 1, :], ALU.max, r=["gsb"], w=["gsb"])
                S.a("activation", gsb[:, 2, :], gsb[:, 2, :], AF.Exp, scale=-1.0, r=["gsb"], w=["gsb"])
                S.a("activation", gsb[:, 2, :], gsb[:, 2, :], AF.Ln, bias=onec[0:4, 0:1], r=["gsb", "onec"], w=["gsb"])
                S.v("tensor_scalar", gsb[:, 3, :], gsb[:, 1, :], 0.0, None, ALU.min, r=["gsb"], w=["gsb"])
                S.v("tensor_tensor", gsb[:, 3, :], gsb[:, 3, :], gsb[:, 2, :], ALU.subtract, r=["gsb"], w=["gsb"])
                S.v("tensor_tensor_scan", gsb[:, 4, :], ones4[:], gsb[:, 3, :], Bprev[:, 0:1], ALU.mult, ALU.add,
                    r=["gsb", "ones4", "Bprev"], w=["gsb"])
                S.v("tensor_tensor", gsb[:, 5, :], gsb[:, 0, :], gsb[:, 4, :], ALU.subtract, r=["gsb"], w=["gsb"])
                S.v("tensor_tensor_scan", gsb[:, 6, :], ones4[:], gsb[:, 5, :], uprev[:, 0:1], ALU.mult, ALU.max,
                    r=["gsb", "ones4", "uprev"], w=["gsb"])
                S.v("tensor_tensor", gsb[:, 7, :], gsb[:, 4, :], gsb[:, 6, :], ALU.add, r=["gsb"], w=["gsb"])
                S.v("tensor_scalar", ngu[:], gsb[:, 6, :], -1.0, None, ALU.mult, r=["gsb"], w=["ngu"])
                S.v("tensor_copy", Bprev[:], gsb[:, 4, 127:128], r=["gsb"], w=["Bprev"])
                S.v("tensor_copy", uprev[:], gsb[:, 6, 127:128], r=["gsb"], w=["uprev"])
                if tt == NT - 1:
                    S.d("gpsimd", w["m_out"].rearrange("(h o) -> h o", o=1), gsb[:, 7, 127:128], r=["gsb"], w=["mout%d" % l])
                for h in range(4):
                    S.p("matmul", ps[4][:, h * 128:(h + 1) * 128], self.selh[:, h, :], ngu[:], start=True, stop=True,
                        r=["selh", "ngu"], w=["ps4"])
                S.p("transpose", ps[3][:, 256:260], gsb[:, 5, :], self.ident[0:4, 0:4], r=["gsb", "ident"], w=["ps3"])
                S.p("transpose", ps[3][:, 260:264], gsb[:, 7, :], self.ident[0:4, 0:4], r=["gsb", "ident"], w=["ps3"])
                S.v("tensor_copy", acol[:, 0:4], ps[3][:, 256:260], r=["ps3"], w=["acol"])
                S.a("activation", acol[:, 4:8], ps[3][:, 260:264], AF.Exp, scale=-1.0, r=["ps3"], w=["acol"])
                S.v("tensor_tensor", Eb[:], ps[4][:].rearrange("p (h n) -> p h n", h=4), bcast(self.trib[:], 1, 4), ALU.add,
                    r=["ps4", "trib"], w=["Eb"])
                for h in range(4):
                    S.a("activation", GT[:, h, :], Eb[:, h, :], AF.Exp, bias=acol[:, h:h + 1], r=["Eb", "acol"], w=["GT"])
                    S.a("activation", dbc[:, h, :], ps[4][:, h * 128:(h + 1) * 128], AF.Exp, bias=upb[:, h:h + 1], r=["ps4", "upb"], w=["dbc"])
                S.v("tensor_scalar", upb[:], ps[4][:].rearrange("p (h n) -> p h n", h=4)[:, :, 127], -1.0, None, ALU.mult,
                    r=["ps4", "dbc"], w=["upb"])
                for h in range(4):
                    for dc in range(4):
                        S.p("matmul", ps[5][:, h * 128:(h + 1) * 128], kT[:, 4 * h + dc, :], qT[:, 4 * h + dc, :],
                            start=(dc == 0), stop=(dc == 3), r=["kT", "qT"], w=["ps5"])
                S.v("scalar_tensor_tensor", StT[:].rearrange("p h n -> p (h n)"), ps[5][:], SC, GT[:].rearrange("p h n -> p (h n)"),
                    ALU.mult, ALU.mult, r=["ps5", "GT"], w=["StT"])
                for h in range(4):
                    qd = qdT[h % 2]
                    qn = "qdT%d" % (h % 2)
                    bank = 6 + h % 2
                    S.v("tensor_tensor", qd[:], qT[:, 4 * h:4 * h + 4, :], bcast(dbc[:, h, :], 1, 4), ALU.mult, r=["qT", "dbc"], w=[qn])
                    S.p("matmul", ps[bank][:, :], StT[:, h, :], vtok[:, h * 512:(h + 1) * 512], start=True, stop=False,
                        r=["StT", "vtok"], w=["ps%d" % bank])
                    for dc in range(4):
                        S.p("matmul", ps[bank][:, :], qd[:, dc, :], C[h][:, dc, :], start=False, stop=(dc == 3),
                            r=[qn, "C%d" % h], w=["ps%d" % bank])
                    S.p("matmul", ps[3][:, 264 + h:265 + h], StT[:, h, :], self.onesb[:, 0:1], start=True, stop=False,
                        r=["StT", "onesb"], w=["ps3"])
                    for dc in range(4):
                        S.p("matmul", ps[3][:, 264 + h:265 + h], qd[:, dc, :], ncol[:, h, dc:dc + 1], start=False, stop=(dc == 3),
                            r=[qn, "ncol"], w=["ps3"])
                    S.v("tensor_scalar", dn[:, 7:8], ps[3][:, 264 + h:265 + h], -1.0, None, ALU.mult, r=["ps3"], w=["dn"])
                    S.v("tensor_tensor", dn[:, 0:1], dn[:, 7:8], ps[3][:, 264 + h:265 + h], ALU.max, r=["ps3", "dn"], w=["dn"])
                    S.v("tensor_tensor", dn[:, 0:1], dn[:, 0:1], acol[:, 4 + h:5 + h], ALU.max, r=["dn", "acol"], w=["dn"])
                    S.v("reciprocal", dn[:, 1:2], dn[:, 0:1], r=["dn"], w=["dn"])
                    S.a("activation", junk[:, 0:512], ps[bank][:, :], AF.Square, accum_out=dn[:, 2:3], r=["ps%d" % bank], w=["junk", "dn"])
                    S.v("tensor_tensor", dn[:, 3:4], dn[:, 1:2], dn[:, 1:2], ALU.mult, r=["dn"], w=["dn"])
                    S.v("tensor_tensor", dn[:, 3:4], dn[:, 3:4], dn[:, 2:3], ALU.mult, r=["dn"], w=["dn"])
                    S.a("activation", dn[:, 4:5], dn[:, 3:4], AF.Sqrt, scale=1.0 / MHD, bias=self.epsc[:, 0:1], r=["dn", "epsc"], w=["dn"])
                    S.v("reciprocal", dn[:, 5:6], dn[:, 4:5], r=["dn"], w=["dn"])
                    S.v("tensor_tensor", dn[:, 6:7], dn[:, 5:6], dn[:, 1:2], ALU.mult, r=["dn"], w=["dn"])
                    S.v("tensor_scalar", hn[:], ps[bank][:, :], dn[:, 6:7], None, ALU.mult, r=["ps%d" % bank, "dn"], w=["hn"])
                    pb = self.psb(0)
                    for ec in range(4):
                        S.p("transpose", pb[:, ec * 128:(ec + 1) * 128], hn[:, ec * 128:(ec + 1) * 128], self.identb[:],
                            r=["hn", "identb"], w=["ps0"])
                    S.v("tensor_tensor", poT[:, 4 * h:4 * h + 4, :], pb[:, 0:512].rearrange("p (j n) -> p j n", j=4),
                        bcast(nwc[:, 4 * h:4 * h + 4], 2, 128), ALU.mult, r=["ps0", "nwc"], w=["poT"])
                    S.v("tensor_tensor", poT[:, 4 * h:4 * h + 4, :], poT[:, 4 * h:4 * h + 4, :], csT[:, 4 * h:4 * h + 4, :], ALU.add,
                        r=["poT", "csT"], w=["poT"])
                    S.v("tensor_scalar", kw[:], ktok[:, h * 512:(h + 1) * 512], GT[:, h, 127:128], SC, ALU.mult, ALU.mult,
                        r=["ktok", "GT"], w=["kw"])
                    for dc in range(4):
                        cbk = 1 + dc % 2
                        S.p("matmul", ps[cbk][:, :], kw[:, dc * 128:(dc + 1) * 128], vtok[:, h * 512:(h + 1) * 512], start=True, stop=True,
                            r=["kw", "vtok"], w=["ps%d" % cbk])
                        S.v("scalar_tensor_tensor", C[h][:, dc, :], C[h][:, dc, :], dbc[:, h, 127:128], ps[cbk][:, :], ALU.mult, ALU.add,
                            r=["C%d" % h, "dbc", "ps%d" % cbk], w=["C%d" % h])
                        S.p("matmul", ps[3][:, 272 + dc:273 + dc], kw[:, dc * 128:(dc + 1) * 128], self.onesb[:, 0:1], start=True, stop=True,
                            r=["kw", "onesb"], w=["ps3"])
                    S.v("scalar_tensor_tensor", ncol[:, h, :], ncol[:, h, :], dbc[:, h, 127:128], ps[3][:, 272:276], ALU.mult, ALU.add,
                        r=["ncol", "dbc", "ps3"], w=["ncol"])
                S.d("gpsimd", poTs[tt], poT[:].rearrange("p c n -> p (c n)"), r=["poT"], w=["poTs%d" % tt])
            S.flush()
            P1a.close()
            s1 = s1_keep
            if sm is not None:
                self.mlstm_samp_1(l, sm, w, s1, wx, Wblk, Wg, cw, cb, nwc, skc, nw_bc, junk, ss)
            for h in range(4):
                S.d("gpsimd", w["C_out"][h].rearrange("(c p) e -> p c e", p=128), C[h][:], r=["C%d" % h], w=["Cout%d_%d" % (l, h)])
            S.d("gpsimd", w["n_out"].rearrange("(h c p) -> p h c", h=4, p=128), ncol[:], r=["ncol"], w=["nout%d" % l])
            S.flush()
        with ExitStack() as P2:
            def s2(name, shape, dt):
                return P2.enter_context(nc.sbuf_tensor("%s_q%d" % (name, l), shape, dt))
            wz = s2("wz", [128, 8, 2048], BF16)
            wo = s2("wo", [128, 16, 1024], BF16)
            wst = [s2("wst%d" % i, [128, 1024], F32) for i in range(2)]
            k = 0
            for c in range(8):
                for hf in range(2):
                    S.d("sync", wst[k % 2][:], w["w_in"][c * 128:(c + 1) * 128, 2048 + hf * 1024:2048 + (hf + 1) * 1024], w=["zwst%d" % (k % 2)])
                    if k % 2 == 0:
                        S.v("tensor_copy", wz[:, c, hf * 1024:(hf + 1) * 1024], wst[k % 2][:], r=["zwst%d" % (k % 2)], w=["wz"])
                    else:
                        S.a("copy", wz[:, c, hf * 1024:(hf + 1) * 1024], wst[k % 2][:], r=["zwst%d" % (k % 2)], w=["wz"])
                    k += 1
            for c in range(16):
                S.d("sync", wst[k % 2][:], w["w_out"][c * 128:(c + 1) * 128, :], w=["zwst%d" % (k % 2)])
                if k % 2 == 0:
                    S.v("tensor_copy", wo[:, c, :], wst[k % 2][:], r=["zwst%d" % (k % 2)], w=["wo"])
                else:
                    S.a("copy", wo[:, c, :], wst[k % 2][:], r=["zwst%d" % (k % 2)], w=["wo"])
                k += 1
            hT = [s2("hT%d" % i, [128, 8, 128], BF16) for i in range(2)]
            po = [s2("po%d" % i, [128, 16, 128], BF16) for i in range(2)]
            xt = [s2("xt%d" % i, [128, 1024], F32) for i in range(2)]
            szT = s2("szT", [128, 16, 128], BF16)
            oT = s2("oT", [128, 16, 128], BF16)
            xo = s2("xo", [128, 1024], F32)
            for tt in range(NT):
                b = tt % 2
                S.d("sync", hT[b][:].rearrange("p c n -> p (c n)"), hTs[tt], r=["mhTs%d" % tt], w=["ZhT%d" % b])
                S.d("sync", po[b][:].rearrange("p c n -> p (c n)"), poTs[tt], r=["poTs%d" % tt], w=["Zpo%d" % b])
                S.d("sync", xt[b][:], xsrc[tt * 128:(tt + 1) * 128, :], r=["xres_%d" % tt], w=["Zx%d" % b])
                for g4 in range(4):
                    bank = 1 + g4 % 2
                    for j in range(4):
                        cc = g4 * 4 + j
                        for c in range(8):
                            S.p("matmul", ps[bank][:, j * 128:(j + 1) * 128], wz[:, c, cc * 128:(cc + 1) * 128], hT[b][:, c, :],
                                start=(c == 0), stop=(c == 7), r=["wz", "ZhT%d" % b], w=["ps%d" % bank])
                    S.a("activation", szT[:, g4 * 4:(g4 + 1) * 4, :], ps[bank][:].rearrange("p (j n) -> p j n", j=4), AF.Silu,
                        r=["ps%d" % bank], w=["szT"])
                S.v("tensor_tensor", oT[:], po[b][:], szT[:], ALU.mult, r=["Zpo%d" % b, "szT"], w=["oT"])
                for half in range(2):
                    bank = 6 + half
                    for cc in range(16):
                        S.p("matmul", ps[bank][:, :], oT[:, cc, :], wo[:, cc, half * 512:(half + 1) * 512], start=(cc == 0), stop=(cc == 15),
                            r=["oT", "wo"], w=["ps%d" % bank])
                    S.v("tensor_tensor", xo[:, half * 512:(half + 1) * 512], ps[bank][:, :], xt[b][:, half * 512:(half + 1) * 512], ALU.add,
                        r=["ps%d" % bank, "Zx%d" % b], w=["xo"])
                S.d("gpsimd", xdst[tt * 128:(tt + 1) * 128, :], xo[:], r=["xo"], w=["xres_%d" % tt])
            if sm is not None:
                self.mlstm_samp_2(l, sm, w, s2, wz, wo)
            S.flush()


Builder.mlstm_layer = mlstm_layer


def nsa_samp_alloc(self, l, sb, w):
    S, P = self.S, self.P
    sm = {}
    sm["hT"] = sb("s_hT", [128, 8, 4], BF16)
    sm["KnewT"] = sb("s_KnewT", [102, 2, 4, 4], BF16)
    sm["Vnew"] = sb("s_Vnew", [4, 2, 4, 65], BF16)
    sm["gates"] = sb("s_gates", [4, 48], F32)
    sm["sz"] = sb("s_sz", [4, 1024], BF16)
    sm["QTs"] = sb("s_QTs", [102, 4, 4, 4], BF16)
    sm["xs"] = sb("s_xs", [4, 1024], F32)
    sm["hb"] = sb("s_hb", [128, 2], F32)
    sm["kvts"] = sb("s_kvts", [4, 1024], F32)
    sm["wnts"] = sb("s_wnts", [4, 512], F32)
    S.v("memset", sm["KnewT"][64:96, :, :, :], 0.0, w=["s_KnewT"])
    S.v("memset", sm["KnewT"][64:65, :, :, :], 1.0, w=["s_KnewT"])
    S.v("memset", sm["QTs"][64:96, :, :, :], 0.0, w=["s_QTs"])
    S.v("memset", sm["Vnew"][:, :, :, 64:65], 1.0, w=["s_Vnew"])
    for br in range(2):
        for kk in range(4):
            S.d("sync", sm["KnewT"][96:102, br, kk, :], self.cin["posP"], w=["s_KnewT"])
    for kk in range(4):
        for s in range(4):
            S.d("sync", sm["QTs"][96:102, kk, s, :], self.cin["slope4"][kk], w=["s_QTs"])
    sm["gsc"] = self.dscr("gsc_l%d" % l, [4, 48], F32)
    sm["oscr"] = self.dscr("oscr_l%d" % l, [4, 1024], F32)
    return sm


def nsa_samp_A(self, l, sm, w, wkv, nw_bc, junk, ss, OKV, OWIN, OG, OZ):
    S, ps = self.S, self.ps
    sp = w["samp"]
    S.d("sync", sm["xs"][:], sp["xs_src"], r=["xsres"], w=["Sx"])
    self.front(sm["xs"], nw_bc, sm["hT"], junk, ss, "S", nrows=4)
    hT = sm["hT"]
    for bi, off in ((0, OKV + 512), (1, OWIN)):
        bank = 1 + bi
        for hh in range(4):
            for c in range(8):
                S.p("matmul", ps[bank][0:64, hh * 4:(hh + 1) * 4], wkv[:, c, off + hh * 64:off + (hh + 1) * 64], hT[:, c, :],
                    start=(c == 0), stop=(c == 7), r=["wkv", "ShT"], w=["ps%d" % bank])
        S.v("tensor_copy", sm["KnewT"][0:64, bi, :, :], ps[bank][0:64, 0:16].rearrange("p (h s) -> p h s", h=4),
            r=["ps%d" % bank], w=["s_KnewT"])
    tm = [(OKV, 512, 5), (OKV + 512, 512, 6), (OWIN, 512, 7), (OG, 48, 5), (OZ, 512, 6), (OZ + 512, 512, 7)]
    for ti, (off, ncol, bank) in enumerate(tm):
        pk = "ps%d" % bank
        for c in range(8):
            S.p("matmul", ps[bank][0:4, 0:ncol], hT[:, c, :], wkv[:, c, off:off + ncol], start=(c == 0), stop=(c == 7),
                r=["wkv", "ShT"], w=[pk])
        if ti == 0:
            S.v("tensor_copy", sm["kvts"][:, 0:512], ps[bank][0:4, 0:512], r=[pk], w=["s_kvts"])
        elif ti == 1:
            S.v("tensor_copy", sm["kvts"][:, 512:1024], ps[bank][0:4, 0:512], r=[pk], w=["s_kvts"])
            S.v("tensor_copy", sm["Vnew"][:, 0, :, 0:64], ps[bank][0:4, 256:512].rearrange("p (h d) -> p h d", h=4), r=[pk], w=["s_Vnew"])
            S.d("gpsimd", sp["kv_out"], sm["kvts"][:], r=["s_kvts"], w=["s_kvout%d" % l])
        elif ti == 2:
            S.v("tensor_copy", sm["wnts"][:], ps[bank][0:4, 0:512], r=[pk], w=["s_wnts"])
            S.v("tensor_copy", sm["Vnew"][:, 1, :, 0:64], ps[bank][0:4, 256:512].rearrange("p (h d) -> p h d", h=4), r=[pk], w=["s_Vnew"])
            WR = sp["win_out"].shape[1]
            S.d("gpsimd", sp["win_out"][:, WR - 1, :], sm["wnts"][:], r=["s_wnts"], w=["s_winout%d" % l])
            S.d("gpsimd", sp["win_out"][:, 0:WR - 1, :], sp["cwin"][:, 1:WR, :], w=["s_winsh%d" % l])
        elif ti == 3:
            S.a("activation", sm["gates"][:], ps[bank][0:4, 0:48], AF.Sigmoid, r=[pk], w=["s_gates"])
            S.d("gpsimd", sm["gsc"], sm["gates"][:], r=["s_gates"], w=["s_gsc"])
        elif ti == 4:
            S.a("activation", sm["sz"][:, 0:512], ps[bank][0:4, 0:512], AF.Silu, r=[pk], w=["s_sz"])
        else:
            S.a("activation", sm["sz"][:, 512:1024], ps[bank][0:4, 0:512], AF.Silu, r=[pk], w=["s_sz"])


def nsa_samp_Q(self, sm, wq):
    S, ps = self.S, self.ps
    for kk in range(4):
        bank = 1 + kk % 2
        for g_ in range(4):
            hd = kk * 4 + g_
            for c in range(8):
                S.p("matmul", ps[bank][0:64, g_ * 4:(g_ + 1) * 4], wq[:, c, hd * 64:(hd + 1) * 64], sm["hT"][:, c, :],
                    start=(c == 0), stop=(c == 7), r=["wq", "ShT"], w=["ps%d" % bank])
        S.a("mul", sm["QTs"][0:64, kk, :, :].rearrange("p s g -> p g s"), ps[bank][0:64, 0:16].rearrange("p (g s) -> p g s", g=4),
            HD ** -0.5, r=["ps%d" % bank], w=["s_QTs"])


def nsa_samp_S(self, l, sm, w):
    nc, S, ps, P = self.nc, self.S, self.ps, self.P
    sp = w["samp"]
    NPG = P // 128
    TSP = P + 32
    NCs = P // 16
    NJ = max(1, NCs // 128)
    NCp = NJ * 128 if NCs >= 128 else 128
    NBs = P // 64 + 1
    NBp = ((NBs + 3) // 4) * 4
    NE = P // 64
    with ExitStack() as Sx:
        def sb(name, shape, dt):
            return Sx.enter_context(nc.sbuf_tensor("%s_S%d" % (name, l), shape, dt))
        wo = sb("wo", [128, 8, 1024], BF16)
        wst = sb("wst0", [128, 1024], F32)
        for c in range(8):
            S.d("sync", wst[:], w["w_out"][c * 128:(c + 1) * 128, :], w=["Swst0"])
            if c % 2 == 0:
                S.v("tensor_copy", wo[:, c, :], wst[:], r=["Swst0"], w=["Swo"])
            else:
                S.a("copy", wo[:, c, :], wst[:], r=["Swst0"], w=["Swo"])
        w1s = sb("w1s", [128, 16, 128], F32)
        w1b = [sb("w1b%d" % c, [128, 32, 128], BF16) for c in range(2)]
        w2s = sb("w2s", [128, 2, 64], F32)
        w2b = sb("w2b", [128, 2, 64], BF16)
        for c in range(2):
            for rh in range(2):
                for half in range(2):
                    S.d("sync", w1s[half * 64:(half + 1) * 64, :, :], w["w1"][c][rh * 16:(rh + 1) * 16].rearrange("r d e -> d r e"), w=["Sw1s"])
                S.v("tensor_copy", w1b[c][:, rh * 16:(rh + 1) * 16, :], w1s[:], r=["Sw1s"], w=["Sw1b%d" % c])
        S.d("sync", w2s[:], w["w2"].rearrange("c e d -> e c d"), w=["Sw2s"])
        S.v("tensor_copy", w2b[:], w2s[:], r=["Sw2s"], w=["Sw2b"])
        KVc = sb("KVc", [128, 4, TSP], BF16)
        S.v("memset", KVc[:, :, P:TSP], 0.0, w=["KVc"])
        ShT = sb("ShT", [128, NCp], BF16)
        S.v("memset", ShT[:], 0.0, w=["SShT"])
        KcTs = sb("KcTs", [102, 4, NCp], BF16)
        S.v("memset", KcTs[0:96, :, :], 0.0, w=["KcTs"])
        for kk in range(4):
            S.d("sync", KcTs[96:102, kk, :], self.cin["cend"][:, 0:NCp], w=["KcTs"])
        vcs = sb("vcs", [128, NJ, 4, 64], BF16)
        S.v("memset", vcs[:], 0.0, w=["vcs"])
        EallS = sb("EallS", [NE, P], BF16)
        MovS = sb("MovS", [128, NJ, NBp], BF16)
        S.d("sync", EallS[:], self.cin["EallS"], w=["EallS"])
        S.d("sync", MovS[:], self.cin["MovS"], w=["MovS"])
        posS = sb("posS", [102, P], BF16)
        S.d("sync", posS[96:102, :], self.cin["pos"][:, 0:P], w=["posS"])
        ownm = sb("ownm", [4, 4, 4], BF16)
        S.d("sync", ownm[:], self.cin["ownm"], w=["ownm"])
        maskcs = sb("maskcs", [4, NCp], F32)
        S.d("sync", maskcs[:], self.cin["maskcs"], w=["maskcs"])
        pcol2 = sb("pcol2", [128, 2], F32)
        S.d("sync", pcol2[:], self.cin["pcol2"], w=["pcol2"])
        onec4 = sb("onec4", [4, 1], F32)
        S.v("memset", onec4[:], 1.0, w=["onec4"])
        gT = sb("gT", [4, 4, 4, 3], F32)
        S.d("sync", gT[:], sm["gsc"].rearrange("s (k g t) -> g s k t", k=4, g=4), r=["s_gsc"], w=["gT"])
        pgt = [sb("pgt%d" % i, [128, 512], F32) for i in range(2)]
        Kch = [sb("Kch%d" % i, [102, 4, 128], BF16) for i in range(2)]
        Vch = [sb("Vch%d" % i, [128, 4, 65], BF16) for i in range(2)]
        for i in range(2):
            S.v("memset", Kch[i][64:96, :, :], 0.0, w=["Kch%d" % i])
            S.v("memset", Kch[i][64:65, :, :], 1.0, w=["Kch%d" % i])
            S.v("memset", Vch[i][:, :, 64:65], 1.0, w=["Vch%d" % i])
        PTs = [sb("PTs%d" % i, [128, 16], BF16) for i in range(2)]
        ptb = sb("ptb", [128, NPG], I32)
        idxc = sb("idxc", [128, NPG], I32)
        idxs = sb("idxs", [128, NPG], I32)
        ksum = sb("ksum", [102, 1], F32)
        prod = sb("prod", [102, 4], BF16)
        Sm = sb("Sm", [4, NCp], F32)
        mx = sb("mx", [4, 8], F32)
        Pn = sb("Pn", [4, NCp], BF16)
        PnT = sb("PnT", [128, NJ, 4], BF16)
        impg = sb("impg", [4, NBp], F32)
        scs = sb("scs", [1, NBp], F32)
        sc2 = sb("sc2", [1, NBp], F32)
        m8 = sb("m8", [1, 16], F32)
        bbs = sb("bbs", [1, 128], BF16)
        bbT = sb("bbT", [128, 4, 4], BF16)
        S.v("memset", bbs[:], 0.0, w=["bbs"])
        S.v("memset", bbT[:], 0.0, w=["SbbT"])
        acc = sb("acc", [4, 4, 64], F32)
        gr = sb("gr", [4, 8], F32)
        oall = sb("oall", [4, 1024], F32)
        ozs = sb("ozs", [4, 1024], BF16)
        ozT = sb("ozT", [128, 8, 4], BF16)
        xo = sb("xo", [4, 1024], F32)
        pool2 = sp["pool2"]
        pti = 0
        for s in range(4):
            S.d("sync", ptb[:], sp["pt"][s:s + 1, :].partition_broadcast(128), w=["ptb"])
            S.v("tensor_scalar", idxc[:], ptb[:], 256.0, pcol2[:, 0:1], ALU.mult, ALU.add, r=["ptb", "pcol2"], w=["idxc"])
            S.v("tensor_scalar", idxs[:], ptb[:], 256.0, pcol2[:, 1:2], ALU.mult, ALU.add, r=["ptb", "pcol2"], w=["idxs"])
            for pg in range(NPG):
                t_ = pgt[pg % 2]
                tn = "pgt%d" % (pg % 2)
                bank = 1 + pg % 2
                S.dm("gpsimd", "indirect_dma_start", (t_[:], None, pool2, bass.IndirectOffsetOnAxis(ap=idxc[:, pg:pg + 1], axis=0)), {},
                     r=["idxc"], w=[tn])
                for q4 in range(4):
                    S.p("transpose", ps[bank][:, q4 * 128:(q4 + 1) * 128], t_[:, q4 * 128:(q4 + 1) * 128], self.ident[:],
                        r=[tn, "ident"], w=["ps%d" % bank])
                eng = S.v if pg % 2 == 0 else S.a
                eng("tensor_copy" if pg % 2 == 0 else "copy", KVc[:, :, pg * 128:(pg + 1) * 128],
                    ps[bank][:].rearrange("p (q n) -> p q n", q=4), r=["ps%d" % bank], w=["KVc"])
            for c in range(2):
                for hh in range(4):
                    pr, hl = hh // 2, hh % 2
                    bank = 1 + ((c * 4 + hh) % 2)
                    pk = "ps%d" % bank
                    for r_ in range(32):
                        S.p("matmul", ps[bank][:, 0:NCs], w1b[c][hl * 64:(hl + 1) * 64, r_, :],
                            KVc[hl * 64:(hl + 1) * 64, c * 2 + pr, r_:r_ + 16 * NCs:16],
                            start=(r_ == 0), stop=(r_ == 31), r=["Sw1b%d" % c, "KVc"], w=[pk])
                    S.a("activation", ShT[:, 0:NCs], ps[bank][:, 0:NCs], AF.Silu, bias=sm["hb"][:, c:c + 1], r=[pk, "s_hb"], w=["SShT"])
                    if c == 0:
                        S.p("matmul", ps[4][0:64, 0:NCp], w2b[:, 0, :], ShT[:], start=True, stop=True, r=["Sw2b", "SShT"], w=["ps4"])
                        S.v("tensor_copy", KcTs[0:64, hh, :], ps[4][0:64, 0:NCp], r=["ps4"], w=["KcTs"])
                    else:
                        for j in range(NJ):
                            S.p("matmul", ps[3][:, j * 64:(j + 1) * 64], ShT[:, j * 128:(j + 1) * 128], w2b[:, 1, :],
                                start=True, stop=True, r=["Sw2b", "SShT"], w=["ps3"])
                        S.v("tensor_copy", vcs[:, :, hh, :], ps[3][:, 0:NJ * 64].rearrange("p (j d) -> p j d", j=NJ), r=["ps3"], w=["vcs"])
            for kk in range(4):
                Q = sm["QTs"][:, kk, s, :]
                S.v("tensor_tensor", ksum[:], sm["KnewT"][:, 0, kk, s:s + 1], sm["KnewT"][:, 1, kk, s:s + 1], ALU.add, r=["s_KnewT"], w=["Sksum"])
                S.v("tensor_scalar", prod[:], Q, ksum[:, 0:1], 0.5, ALU.mult, ALU.mult, r=["s_QTs", "Sksum"], w=["Sprod"])
                S.p("matmul", ps[3][:, 256:260], self.selc[:, :], prod[:], start=True, stop=True, r=["selc", "Sprod"], w=["ps3"])
                S.a("activation", sm["QTs"][64:65, kk, s, :], ps[3][64:65, 256:260], AF.Identity, scale=-1.0, bias=self.negm[64:65, 0:1],
                    r=["ps3", "negm"], w=["s_QTs"])
                S.p("matmul", ps[4][0:4, 0:NCp], Q, KcTs[:, kk, :], start=True, stop=True, r=["s_QTs", "KcTs"], w=["ps4"])
                S.v("tensor_tensor", Sm[:], ps[4][0:4, 0:NCp], maskcs[:], ALU.add, r=["ps4", "maskcs"], w=["SSm"])
                S.v("tensor_reduce", mx[:, 0:1], Sm[:], AX.X, ALU.max, r=["SSm"], w=["Smx"])
                S.v("tensor_scalar", mx[:, 1:2], mx[:, 0:1], -10000.0, -1.0, ALU.max, ALU.mult, r=["Smx"], w=["Smx"])
                S.a("activation", Sm[:], Sm[:], AF.Exp, bias=mx[:, 1:2], accum_out=mx[:, 2:3], r=["SSm", "Smx"], w=["SSm", "Smx"])
                S.v("tensor_scalar", mx[:, 3:4], mx[:, 2:3], 1e-30, None, ALU.max, r=["Smx"], w=["Smx"])
                S.v("reciprocal", mx[:, 3:4], mx[:, 3:4], r=["Smx"], w=["Smx"])
                S.v("tensor_scalar", Pn[:], Sm[:], mx[:, 3:4], None, ALU.mult, r=["SSm", "Smx"], w=["SPn"])
                pb = self.psb(0)
                for j in range(NJ):
                    S.p("transpose", pb[:, j * 4:(j + 1) * 4], Pn[:, j * 128:(j + 1) * 128], self.identb[0:4, 0:4],
                        r=["SPn", "identb"], w=["ps0"])
                S.v("tensor_copy", PnT[:].rearrange("p j g -> p (j g)"), pb[:, 0:NJ * 4], r=["ps0"], w=["SPnT"])
                first = True
                for j in range(NJ):
                    S.p("matmul", ps[5][0:4, 0:64], PnT[:, j, :], vcs[:, j, kk, :], start=first, stop=False, skip_group_check=True,
                        r=["SPnT", "vcs"], w=["ps5"])
                    first = False
                    S.p("matmul", ps[5][0:4, 64:64 + NBp], PnT[:, j, :], MovS[:, j, :], start=False, stop=(j == NJ - 1), skip_group_check=True,
                        r=["SPnT", "MovS"], w=["ps5"])
                S.v("tensor_scalar", acc[:, kk, :], ps[5][0:4, 0:64], gT[:, s, kk, 0:1], None, ALU.mult, r=["ps5", "gT"], w=["Sacc"])
                S.v("tensor_copy", impg[:], ps[5][0:4, 64:64 + NBp], r=["ps5"], w=["impg"])
                S.p("matmul", ps[5][0:1, 256:256 + NBp], onec4[:], impg[:], start=True, stop=True, r=["onec4", "impg"], w=["ps5"])
                S.v("tensor_copy", scs[:], ps[5][0:1, 256:256 + NBp], r=["ps5"], w=["scs"])
                if NBp > NBs:
                    S.v("memset", scs[:, NBs:NBp], -1.0, w=["scs"])
                S.v("memset", scs[:, 0:1], 1.0e4, w=["scs"])
                S.v("memset", scs[:, NBs - 2:NBs], 1.0e4, w=["scs"])
                S.v("max", m8[:, 0:8], scs[:], r=["scs"], w=["Sm8"])
                S.v("match_replace", sc2[:], m8[:, 0:8], scs[:], -1.0e30, r=["scs", "Sm8"], w=["Ssc2"])
                S.v("max", m8[:, 8:16], sc2[:], r=["Ssc2"], w=["Sm8"])
                S.v("tensor_scalar", sc2[:], scs[:], m8[:, 15:16], None, ALU.is_ge, r=["scs", "Sm8"], w=["Ssc2"])
                S.v("tensor_scalar", bbs[:, 0:NE], sc2[:, 0:NE], BIG, -BIG, ALU.mult, ALU.add, r=["Ssc2"], w=["bbs"])
                S.p("transpose", pb[0:NE, 64:65], bbs[:, 0:NE], self.identb[0:1, 0:1], r=["bbs", "identb"], w=["ps0"])
                S.v("tensor_copy", bbT[0:NE, kk, :], bcast(pb[0:NE, 64], 1, 4), r=["ps0"], w=["SbbT"])
            for (br, obank, gi) in (("sel", 6, 1), ("win", 7, 2)):
                nch = NPG if br == "sel" else min(512, P) // 128
                ok = "ps%d" % obank
                for pg in range(nch):
                    t_ = pgt[pti % 2]
                    tn = "pgt%d" % (pti % 2)
                    b2 = pti % 2
                    pti += 1
                    bank = 1 + b2
                    if br == "sel":
                        S.dm("gpsimd", "indirect_dma_start", (t_[:], None, pool2, bass.IndirectOffsetOnAxis(ap=idxs[:, pg:pg + 1], axis=0)), {},
                             r=["idxs"], w=[tn])
                        p0 = pg * 128
                    else:
                        S.d("sync", t_[:], sp["cwin"][s, pg * 128:(pg + 1) * 128, :], w=[tn])
                        p0 = P - nch * 128 + pg * 128
                    for hh in range(4):
                        S.p("transpose", ps[bank][0:64, hh * 128:(hh + 1) * 128], t_[:, hh * 64:(hh + 1) * 64], self.ident[:],
                            r=[tn, "ident"], w=["ps%d" % bank])
                    S.v("tensor_copy", Kch[b2][0:64, :, :], ps[bank][0:64, :].rearrange("p (h n) -> p h n", h=4), r=["ps%d" % bank], w=["Kch%d" % b2])
                    S.v("tensor_copy", Kch[b2][96:102, :, :], bcast(posS[96:102, p0:p0 + 128], 1, 4), r=["posS"], w=["Kch%d" % b2])
                    S.a("copy", Vch[b2][:, :, 0:64], t_[:, 256:512].rearrange("p (h d) -> p h d", h=4), r=[tn], w=["Vch%d" % b2])
                    sbank = 3 + b2
                    for kk in range(4):
                        S.p("matmul", ps[sbank][:, kk * 4:(kk + 1) * 4], Kch[b2][:, kk, :], sm["QTs"][:, kk, s, :], start=True, stop=(br != "sel"),
                            r=["Kch%d" % b2, "s_QTs"], w=["ps%d" % sbank])
                        if br == "sel":
                            S.p("matmul", ps[sbank][:, kk * 4:(kk + 1) * 4], EallS[:, pg * 128:(pg + 1) * 128], bbT[0:NE, kk, :], start=False, stop=True,
                                r=["EallS", "SbbT"], w=["ps%d" % sbank])
                    S.a("activation", PTs[b2][:], ps[sbank][:, 0:16], AF.Exp, r=["ps%d" % sbank], w=["PTs%d" % b2])
                    for kk in range(4):
                        S.p("matmul", ps[obank][0:4, kk * 65:(kk + 1) * 65], PTs[b2][:, kk * 4:(kk + 1) * 4], Vch[b2][:, kk, :],
                            start=(pg == 0 and kk == 0), stop=False, skip_group_check=True, r=["PTs%d" % b2, "Vch%d" % b2], w=[ok])
                bi = 0 if br == "sel" else 1
                b2 = pti % 2
                pti += 1
                sbank = 3 + b2
                for kk in range(4):
                    S.p("matmul", ps[sbank][0:4, kk * 4:(kk + 1) * 4], sm["KnewT"][:, bi, kk, :], sm["QTs"][:, kk, s, :], start=True, stop=False,
                        r=["s_KnewT", "s_QTs"], w=["ps%d" % sbank])
                    S.p("matmul", ps[sbank][0:4, kk * 4:(kk + 1) * 4], self.identb[0:4, 0:4], ownm[:, s, :], start=False, stop=True,
                        r=["identb", "ownm"], w=["ps%d" % sbank])
                S.a("activation", PTs[b2][0:4, :], ps[sbank][0:4, 0:16], AF.Exp, r=["ps%d" % sbank], w=["PTs%d" % b2])
                for kk in range(4):
                    S.p("matmul", ps[obank][0:4, kk * 65:(kk + 1) * 65], PTs[b2][0:4, kk * 4:(kk + 1) * 4], sm["Vnew"][:, bi, kk, :],
                        start=False, stop=(kk == 3), skip_group_check=True, r=["PTs%d" % b2, "s_Vnew"], w=[ok])
                ov = ps[obank][0:4, 0:260].rearrange("p (k d) -> p k d", k=4)
                S.v("reciprocal", gr[:, 0:4], ov[:, :, 64], r=[ok], w=["Sgr"])
                S.v("tensor_tensor", gr[:, 4:8], gr[:, 0:4], gT[:, s, :, gi], ALU.mult, r=["Sgr", "gT"], w=["Sgr"])
                for kk in range(4):
                    S.v("scalar_tensor_tensor", acc[:, kk, :], ov[:, kk, 0:64], gr[:, 4 + kk:5 + kk], acc[:, kk, :], ALU.mult, ALU.add,
                        r=[ok, "Sgr", "Sacc"], w=["Sacc"])
            S.d("gpsimd", sm["oscr"][s].rearrange("(k g d) -> g k d", k=4, g=4), acc[:], r=["Sacc"], w=["oscr%d" % s])
        S.d("sync", oall[:], sm["oscr"], r=["oscr%d" % s for s in range(4)], w=["oall"])
        S.v("tensor_tensor", ozs[:], oall[:], sm["sz"][:], ALU.mult, r=["oall", "s_sz"], w=["ozs"])
        pb = self.psb(0)
        for c in range(8):
            S.p("transpose", pb[:, c * 4:(c + 1) * 4], ozs[:, c * 128:(c + 1) * 128], self.identb[0:4, 0:4], r=["ozs", "identb"], w=["ps0"])
        S.v("tensor_copy", ozT[:].rearrange("p c n -> p (c n)"), pb[:, 0:32], r=["ps0"], w=["SozT"])
        for half in range(2):
            bank = 1 + half
            for c in range(8):
                S.p("matmul", ps[bank][0:4, :], ozT[:, c, :], wo[:, c, half * 512:(half + 1) * 512], start=(c == 0), stop=(c == 7),
                    r=["SozT", "Swo"], w=["ps%d" % bank])
            S.v("tensor_tensor", xo[:, half * 512:(half + 1) * 512], ps[bank][0:4, :], sm["xs"][:, half * 512:(half + 1) * 512], ALU.add,
                r=["ps%d" % bank, "Sx"], w=["Sxo"])
        S.d("gpsimd", sp["xs_dst"], xo[:], r=["Sxo"], w=["xsres"])
        S.flush()


Builder.nsa_samp_alloc = nsa_samp_alloc
Builder.nsa_samp_A = nsa_samp_A
Builder.nsa_samp_Q = nsa_samp_Q
Builder.nsa_samp_S = nsa_samp_S


def mlstm_samp_1(self, l, sm, w, s1, wx, Wblk, Wg, cw, cb, nwc, skc, nw_bc, junk, ss):
    nc, S, ps = self.nc, self.S, self.ps
    sp = w["samp"]
    SC = MHD ** -0.5
    xs, hT = sm["xs"], sm["hT"]
    S.d("sync", xs[:], sp["xs_src"], r=["xsres"], w=["Sx"])
    self.front(xs, nw_bc, hT, junk, ss, "S", nrows=4)
    xmT = s1("s_xmT", [128, 16, 4], F32)
    xmb = s1("s_xmb", [128, 16, 4], BF16)
    xmtok = s1("s_xmtok", [4, 2048], F32)
    scv = s1("s_scv", [12, 2048], F32)
    cbuf = s1("s_cbuf", [128, 16, 12], F32)
    cv = s1("s_cv", [128, 16, 4], F32)
    tmp = s1("s_tmp", [128, 16, 4], F32)
    cT = s1("s_cT", [128, 16, 4], BF16)
    csT = s1("s_csT", [128, 16, 4], BF16)
    qT = s1("s_qT", [128, 16, 4], BF16)
    kT = s1("s_kT", [128, 16, 4], BF16)
    vT = s1("s_vT", [128, 16, 4], BF16)
    qf = s1("s_qf", [128, 16, 4], F32)
    qm = s1("s_qm", [128, 16, 4, 4], F32)
    tok = [s1("s_tok%d" % i, [4, 2048], F32) for i in range(3)]
    n0 = s1("s_n0", [4, 2048], F32)
    t2k = s1("s_t2k", [4, 2048], F32)
    g8 = s1("s_g8", [4, 8], F32)
    bgr = s1("s_bgr", [4, 8], F32)
    sc = s1("s_sc", [4, 64], F32)
    eye4 = s1("s_eye4", [4, 4, 4], F32)
    Dm = s1("s_Dm", [4, 4, 4], F32)
    ones4p = s1("s_ones4p", [4, 128], F32)
    decb = s1("s_decb", [128, 16], F32)
    coef = s1("s_coef", [4, 4, 4], F32)
    kwm = s1("s_kwm", [4, 512], F32)
    Cs = [s1("s_C%d" % i, [128, 4, 512], F32) for i in range(2)]
    hn = s1("s_hn", [4, 2048], BF16)
    hcf = s1("s_hcf", [4, 512], F32)
    S.v("memset", ones4p[:], 1.0, w=["s_ones4p"])
    S.d("sync", eye4[:], self.cin["eye4"], w=["s_eye4"])
    S.d("sync", bgr[:], w["b_gate"].rearrange("(o g) -> o g", o=1).partition_broadcast(4), w=["s_bgr"])
    S.d("sync", n0[:], sp["n_in"], w=["s_n0"])
    S.d("sync", sc[:, 0:4], sp["m_in"], w=["s_sc"])
    for cc in range(16):
        for c in range(8):
            S.p("matmul", ps[1][:, cc * 4:(cc + 1) * 4], wx[:, c, cc * 128:(cc + 1) * 128], hT[:, c, :], start=(c == 0), stop=(c == 7),
                r=["wx", "ShT"], w=["ps1"])
    S.v("tensor_copy", xmT[:].rearrange("p c n -> p (c n)"), ps[1][:, 0:64], r=["ps1"], w=["s_xmT"])
    S.v("tensor_copy", xmb[:], xmT[:], r=["s_xmT"], w=["s_xmb"])
    for q4 in range(4):
        bank = 4 + q4
        for c in range(8):
            S.p("matmul", ps[bank][0:4, :], hT[:, c, :], wx[:, c, q4 * 512:(q4 + 1) * 512], start=(c == 0), stop=(c == 7),
                r=["wx", "ShT"], w=["ps%d" % bank])
        S.v("tensor_copy", xmtok[:, q4 * 512:(q4 + 1) * 512], ps[bank][0:4, :], r=["ps%d" % bank], w=["s_xmtok"])
    S.d("gpsimd", sp["conv_out"][:, 2, :], xmtok[:], r=["s_xmtok"], w=["s_convout%d" % l])
    S.d("gpsimd", sp["conv_out"][:, 0:2, :], sp["conv_in"][:, 1:3, :], w=["s_convsh%d" % l])
    S.d("sync", scv[:], sp["conv_in"].rearrange("s j c -> (s j) c"), w=["s_scv"])
    for cc in range(16):
        S.p("transpose", ps[2][:, cc * 12:(cc + 1) * 12], scv[:, cc * 128:(cc + 1) * 128], self.ident[0:12, 0:12], r=["s_scv", "ident"], w=["ps2"])
    S.v("tensor_copy", cbuf[:].rearrange("p c n -> p (c n)"), ps[2][:, 0:192], r=["ps2"], w=["s_cbuf"])
    cb4 = cbuf[:].rearrange("p c (s j) -> p c s j", j=3)
    S.v("tensor_tensor", cv[:], xmT[:], bcast(cw[:, :, 3], 2, 4), ALU.mult, r=["s_xmT", "cw"], w=["s_cv"])
    for j in range(3):
        S.v("tensor_tensor", tmp[:], cb4[:, :, :, j], bcast(cw[:, :, j], 2, 4), ALU.mult, r=["s_cbuf", "cw"], w=["s_tmp"])
        S.v("tensor_tensor", cv[:], cv[:], tmp[:], ALU.add, r=["s_cv", "s_tmp"], w=["s_cv"])
    S.v("tensor_tensor", cv[:], cv[:], bcast(cb[:], 2, 4), ALU.add, r=["s_cv", "cb"], w=["s_cv"])
    S.a("activation", cT[:], cv[:], AF.Silu, r=["s_cv"], w=["s_cT"])
    S.v("tensor_tensor", csT[:], cT[:], bcast(skc[:], 2, 4), ALU.mult, r=["s_cT", "skc"], w=["s_csT"])
    for (x_, src, dst, nm) in ((0, cT, qT, "s_qT"), (1, cT, kT, "s_kT"), (2, xmb, vT, "s_vT")):
        for cc in range(16):
            S.p("matmul", ps[1][:, cc * 4:(cc + 1) * 4], Wblk[:, x_, cc, :], src[:, cc, :], start=True, stop=True,
                r=["Wblk", "s_cT", "s_xmb"], w=["ps1"])
        S.v("tensor_copy", dst[:].rearrange("p c n -> p (c n)"), ps[1][:, 0:64], r=["ps1"], w=[nm])
        if x_ == 0:
            S.v("tensor_copy", qf[:].rearrange("p c n -> p (c n)"), ps[1][:, 0:64], r=["ps1"], w=["s_qf"])
        for q4 in range(4):
            bank = 4 + q4
            for j in range(4):
                cc = q4 * 4 + j
                S.p("matmul", ps[bank][0:4, j * 128:(j + 1) * 128], src[:, cc, :], Wblk[:, x_, cc, :], start=True, stop=True,
                    r=["Wblk", "s_cT", "s_xmb"], w=["ps%d" % bank])
            S.v("tensor_copy", tok[x_][:, q4 * 512:(q4 + 1) * 512], ps[bank][0:4, :], r=["ps%d" % bank], w=["s_tok%d" % x_])
    first = True
    for (x_, src, nm) in ((0, qT, "s_qT"), (1, kT, "s_kT"), (2, vT, "s_vT")):
        for cc in range(16):
            S.p("matmul", ps[3][0:4, 0:8], src[:, cc, :], Wg[:, x_ * 16 + cc, :], start=first, stop=(x_ == 2 and cc == 15),
                r=["Wg", nm], w=["ps3"])
            first = False
    S.v("tensor_tensor", g8[:], ps[3][0:4, 0:8], bgr[:], ALU.add, r=["ps3", "s_bgr"], w=["s_g8"])
    S.v("tensor_scalar", sc[:, 4:8], g8[:, 4:8], -1.0, None, ALU.mult, r=["s_g8"], w=["s_sc"])
    S.v("tensor_tensor", sc[:, 4:8], sc[:, 4:8], g8[:, 4:8], ALU.max, r=["s_sc", "s_g8"], w=["s_sc"])
    S.a("activation", sc[:, 4:8], sc[:, 4:8], AF.Exp, scale=-1.0, r=["s_sc"], w=["s_sc"])
    S.a("activation", sc[:, 4:8], sc[:, 4:8], AF.Ln, bias=sm["onec"][0:4, 0:1], r=["s_sc", "s_onec"], w=["s_sc"])
    S.v("tensor_scalar", sc[:, 8:12], g8[:, 4:8], 0.0, None, ALU.min, r=["s_g8"], w=["s_sc"])
    S.v("tensor_tensor", sc[:, 8:12], sc[:, 8:12], sc[:, 4:8], ALU.subtract, r=["s_sc"], w=["s_sc"])
    S.v("tensor_tensor", sc[:, 4:8], sc[:, 8:12], sc[:, 0:4], ALU.add, r=["s_sc"], w=["s_sc"])
    S.v("tensor_tensor", sc[:, 12:16], sc[:, 4:8], g8[:, 0:4], ALU.max, r=["s_sc", "s_g8"], w=["s_sc"])
    S.v("tensor_tensor", sc[:, 16:20], sc[:, 4:8], sc[:, 12:16], ALU.subtract, r=["s_sc"], w=["s_sc"])
    S.a("activation", sc[:, 16:20], sc[:, 16:20], AF.Exp, r=["s_sc"], w=["s_sc"])
    S.v("tensor_tensor", sc[:, 20:24], g8[:, 0:4], sc[:, 12:16], ALU.subtract, r=["s_sc", "s_g8"], w=["s_sc"])
    S.a("activation", sc[:, 20:24], sc[:, 20:24], AF.Exp, r=["s_sc"], w=["s_sc"])
    S.a("activation", sc[:, 40:44], sc[:, 12:16], AF.Exp, scale=-1.0, r=["s_sc"], w=["s_sc"])
    S.d("gpsimd", sp["m_out"], sc[:, 12:16], r=["s_sc"], w=["s_mout%d" % l])
    S.v("tensor_tensor", t2k[:], tok[0][:], tok[1][:], ALU.mult, r=["s_tok0", "s_tok1"], w=["s_t2k"])
    S.v("tensor_reduce", sc[:, 24:28], t2k[:].rearrange("p (h d) -> p h d", h=4), AX.X, ALU.add, r=["s_t2k"], w=["s_sc"])
    S.v("tensor_scalar", sc[:, 24:28], sc[:, 24:28], SC, None, ALU.mult, r=["s_sc"], w=["s_sc"])
    S.v("tensor_tensor", t2k[:], tok[0][:], n0[:], ALU.mult, r=["s_tok0", "s_n0"], w=["s_t2k"])
    S.v("tensor_reduce", sc[:, 28:32], t2k[:].rearrange("p (h d) -> p h d", h=4), AX.X, ALU.add, r=["s_t2k"], w=["s_sc"])
    S.v("tensor_tensor", sc[:, 44:48], sc[:, 20:24], sc[:, 24:28], ALU.mult, r=["s_sc"], w=["s_sc"])
    S.v("tensor_tensor", sc[:, 32:36], sc[:, 16:20], sc[:, 28:32], ALU.mult, r=["s_sc"], w=["s_sc"])
    S.v("tensor_tensor", sc[:, 32:36], sc[:, 32:36], sc[:, 44:48], ALU.add, r=["s_sc"], w=["s_sc"])
    S.v("tensor_scalar", sc[:, 36:40], sc[:, 32:36], -1.0, None, ALU.mult, r=["s_sc"], w=["s_sc"])
    S.v("tensor_tensor", sc[:, 36:40], sc[:, 36:40], sc[:, 32:36], ALU.max, r=["s_sc"], w=["s_sc"])
    S.v("tensor_tensor", sc[:, 36:40], sc[:, 36:40], sc[:, 40:44], ALU.max, r=["s_sc"], w=["s_sc"])
    S.v("reciprocal", sc[:, 36:40], sc[:, 36:40], r=["s_sc"], w=["s_sc"])
    for h in range(4):
        S.v("tensor_scalar", t2k[:, h * 512:(h + 1) * 512], tok[1][:, h * 512:(h + 1) * 512], sc[:, 20 + h:21 + h], SC, ALU.mult, ALU.mult,
            r=["s_tok1", "s_sc"], w=["s_t2k"])
        S.v("scalar_tensor_tensor", n0[:, h * 512:(h + 1) * 512], n0[:, h * 512:(h + 1) * 512], sc[:, 16 + h:17 + h],
            t2k[:, h * 512:(h + 1) * 512], ALU.mult, ALU.add, r=["s_n0", "s_sc", "s_t2k"], w=["s_n0"])
    S.d("gpsimd", sp["n_out"], n0[:], r=["s_n0"], w=["s_nout%d" % l])
    S.v("tensor_tensor", Dm[:], bcast(sc[:, 16:20], 1, 4), eye4[:], ALU.mult, r=["s_sc", "s_eye4"], w=["s_Dm"])
    S.p("matmul", ps[3][:, 16:32], ones4p[:], Dm[:].rearrange("p s h -> p (s h)"), start=True, stop=True, r=["s_ones4p", "s_Dm"], w=["ps3"])
    S.v("tensor_copy", decb[:], ps[3][:, 16:32], r=["ps3"], w=["s_decb"])
    S.v("tensor_tensor", coef[:], bcast(sc[:, 20:24], 1, 4), eye4[:], ALU.mult, r=["s_sc", "s_eye4"], w=["s_coef"])
    S.v("tensor_scalar", coef[:], coef[:], SC, None, ALU.mult, r=["s_coef"], w=["s_coef"])
    S.v("memset", qm[:], 0.0, w=["s_qm"])
    for s in range(4):
        S.v("tensor_copy", qm[:, :, s, s], qf[:, :, s], r=["s_qf"], w=["s_qm"])
    k_ = 0
    for h in range(4):
        bank = 6 + h % 2
        for s in range(4):
            Cb = Cs[k_ % 2]
            cn = "s_C%d" % (k_ % 2)
            k_ += 1
            S.d("sync", Cb[:], sp["C_in"][s, h].rearrange("(c p) e -> p c e", p=128), w=[cn])
            for dc in range(4):
                S.p("matmul", ps[bank][0:4, :], qm[:, 4 * h + dc, s, :], Cb[:, dc, :], start=(s == 0 and dc == 0), stop=(s == 3 and dc == 3),
                    r=["s_qm", cn], w=["ps%d" % bank])
            S.v("tensor_scalar", kwm[:], tok[1][:, h * 512:(h + 1) * 512], coef[:, s, h:h + 1], None, ALU.mult, r=["s_tok1", "s_coef"], w=["s_kwm"])
            for dc in range(4):
                cbk = 1 + dc % 2
                S.p("matmul", ps[cbk][:, :], kwm[:, dc * 128:(dc + 1) * 128], tok[2][:, h * 512:(h + 1) * 512], start=True, stop=True,
                    r=["s_kwm", "s_tok2"], w=["ps%d" % cbk])
                S.v("scalar_tensor_tensor", Cb[:, dc, :], Cb[:, dc, :], decb[:, s * 4 + h:s * 4 + h + 1], ps[cbk][:, :], ALU.mult, ALU.add,
                    r=[cn, "s_decb", "ps%d" % cbk], w=[cn])
            S.d("gpsimd", sp["C_out"][s, h].rearrange("(c p) e -> p c e", p=128), Cb[:], r=[cn], w=["s_Cout%d_%d_%d" % (l, s, h)])
        S.v("tensor_scalar", hcf[:], tok[2][:, h * 512:(h + 1) * 512], sc[:, 44 + h:45 + h], None, ALU.mult, r=["s_tok2", "s_sc"], w=["s_hcf"])
        S.v("scalar_tensor_tensor", hcf[:], ps[bank][0:4, :], sc[:, 16 + h:17 + h], hcf[:], ALU.mult, ALU.add,
            r=["ps%d" % bank, "s_sc", "s_hcf"], w=["s_hcf"])
        S.v("tensor_scalar", hcf[:], hcf[:], sc[:, 36 + h:37 + h], None, ALU.mult, r=["s_hcf", "s_sc"], w=["s_hcf"])
        S.a("activation", junk[0:4, 0:512], hcf[:], AF.Square, accum_out=sc[:, 48 + h:49 + h], r=["s_hcf"], w=["junk", "s_sc"])
        S.a("activation", sc[:, 52 + h:53 + h], sc[:, 48 + h:49 + h], AF.Sqrt, scale=1.0 / MHD, bias=self.epsc[0:4, 0:1], r=["s_sc", "epsc"], w=["s_sc"])
        S.v("reciprocal", sc[:, 56 + h:57 + h], sc[:, 52 + h:53 + h], r=["s_sc"], w=["s_sc"])
        S.v("tensor_scalar", hn[:, h * 512:(h + 1) * 512], hcf[:], sc[:, 56 + h:57 + h], None, ALU.mult, r=["s_hcf", "s_sc"], w=["s_hn"])
    pb = self.psb(0)
    for cc in range(16):
        S.p("transpose", pb[:, cc * 4:(cc + 1) * 4], hn[:, cc * 128:(cc + 1) * 128], self.identb[0:4, 0:4], r=["s_hn", "identb"], w=["ps0"])
    po = sm["po"]
    S.v("tensor_tensor", po[:], pb[:, 0:64].rearrange("p (c n) -> p c n", c=16), bcast(nwc[:], 2, 4), ALU.mult, r=["ps0", "nwc"], w=["s_po"])
    S.v("tensor_tensor", po[:], po[:], csT[:], ALU.add, r=["s_po", "s_csT"], w=["s_po"])


def mlstm_samp_2(self, l, sm, w, s2, wz, wo):
    S, ps = self.S, self.ps
    sp = w["samp"]
    hT = sm["hT"]
    szT = s2("s_szT", [128, 16, 4], BF16)
    oT = s2("s_oT", [128, 16, 4], BF16)
    xo = s2("s_xo", [4, 1024], F32)
    for cc in range(16):
        for c in range(8):
            S.p("matmul", ps[1][:, cc * 4:(cc + 1) * 4], wz[:, c, cc * 128:(cc + 1) * 128], hT[:, c, :], start=(c == 0), stop=(c == 7),
                r=["wz", "ShT"], w=["ps1"])
    S.a("activation", szT[:].rearrange("p c n -> p (c n)"), ps[1][:, 0:64], AF.Silu, r=["ps1"], w=["s_szT"])
    S.v("tensor_tensor", oT[:], sm["po"][:], szT[:], ALU.mult, r=["s_po", "s_szT"], w=["s_oT"])
    for half in range(2):
        bank = 6 + half
        for cc in range(16):
            S.p("matmul", ps[bank][0:4, :], oT[:, cc, :], wo[:, cc, half * 512:(half + 1) * 512], start=(cc == 0), stop=(cc == 15),
                r=["s_oT", "wo"], w=["ps%d" % bank])
        S.v("tensor_tensor", xo[:, half * 512:(half + 1) * 512], ps[bank][0:4, :], sm["xs"][:, half * 512:(half + 1) * 512], ALU.add,
            r=["ps%d" % bank, "Sx"], w=["s_xo"])
    S.d("gpsimd", sp["xs_dst"], xo[:], r=["s_xo"], w=["xsres"])


Builder.mlstm_samp_1 = mlstm_samp_1
Builder.mlstm_samp_2 = mlstm_samp_2
```
